# Optimizing a Trainium2 kernel written in Bass

```python
import math
import jax, jax.numpy as jnp
from jax import lax
import numpy as np

D_MODEL = 2048
BATCH = 16
SEQ = 256
DEPTH = 4
DEC_BATCH = 4
DEC_SEQ = 4096
PAST_LEN = 512

GRID_W = 64
CHUNK = 128
H_RET = 8
DK_RET = 128
DV_RET = 128
D_RET = H_RET * DV_RET
H_SSD = 16
P_SSD = 64
D_SSD = H_SSD * P_SSD
G_SSD = 2
N_SSD = 128
D_CONV = 3
D_XBC = D_SSD + 2 * G_SSD * N_SSD
D_FF = 4 * D_MODEL
D_IN = 3 * H_RET * DK_RET // 1 - 2 * H_RET * DK_RET + 2 * H_RET * DK_RET + D_RET + D_SSD + D_XBC + 2 * H_SSD
N_MOD = 6
EPS = 1e-6
ROPE_BASE = 10000.0
SPLITS = [H_RET * DK_RET, 2 * H_RET * DK_RET, 2 * H_RET * DK_RET + D_RET,
          2 * H_RET * DK_RET + 2 * D_RET, 2 * H_RET * DK_RET + 2 * D_RET + D_SSD,
          2 * H_RET * DK_RET + 2 * D_RET + D_SSD + D_XBC]

kernel_name = "hybrid_retention_ssd_diffusion_step"


def rms_norm(x, w):
    xf = x.astype(jnp.float32)
    y = xf * lax.rsqrt(jnp.mean(xf * xf, -1, keepdims=True) + EPS)
    return (y * w.astype(jnp.float32)).astype(x.dtype)


def head_rms(y):
    yf = y.astype(jnp.float32)
    return (yf * lax.rsqrt(jnp.mean(yf * yf, -1, keepdims=True) + EPS)).astype(y.dtype)


def grid_rope(length):
    rows = length // GRID_W
    pos = jnp.arange(rows * GRID_W)
    row = (pos // GRID_W).astype(jnp.float32)
    col = (pos % GRID_W).astype(jnp.float32)
    half = DK_RET // 2
    inv = 1.0 / (ROPE_BASE ** (jnp.arange(0, half, 2, dtype=jnp.float32) / half))
    ang = jnp.concatenate([row[:, None] * inv, col[:, None] * inv], -1)
    return jnp.cos(ang), jnp.sin(ang)


def apply_rope(x, cos, sin):
    c = cos[None, :, None, :].astype(x.dtype)
    s = sin[None, :, None, :].astype(x.dtype)
    x1, x2 = x[..., :DK_RET // 2], x[..., DK_RET // 2:]
    return jnp.concatenate([x1 * c - x2 * s, x1 * s + x2 * c], -1)


def chunked_scan(cq, kb, xv, log_a, init):
    b, l, h, n = kb.shape
    p = xv.shape[-1]
    nc = l // CHUNK
    cq = cq.reshape(b, nc, CHUNK, h, n)
    kb = kb.reshape(b, nc, CHUNK, h, n)
    xv = xv.reshape(b, nc, CHUNK, h, p)
    a = jnp.moveaxis(log_a.astype(jnp.float32).reshape(b, nc, CHUNK, h), -1, 1)
    a_cum = jnp.cumsum(a, -1)
    lower = jnp.tril(jnp.ones((CHUNK, CHUNK), bool))
    seg = a_cum[..., :, None] - a_cum[..., None, :]
    decay_in = jnp.where(lower, jnp.exp(jnp.where(lower, seg, 0.0)), 0.0)
    y_in = jnp.einsum("bclhn,bcshn,bhcls,bcshp->bclhp", cq, kb, decay_in.astype(xv.dtype), xv)
    decay_st = jnp.exp(a_cum[..., -1:] - a_cum)
    chunk_st = jnp.einsum("bclhn,bhcl,bclhp->bchpn", kb, decay_st.astype(xv.dtype), xv)
    chunk_decay = jnp.exp(a_cum[..., -1])

    def step(s, inp):
        st_c, dec_c = inp
        return dec_c[..., None, None] * s + st_c, s

    final, starts = lax.scan(step, init.astype(jnp.float32),
                             (jnp.moveaxis(chunk_st, 1, 0).astype(jnp.float32),
                              jnp.moveaxis(chunk_decay, 2, 0)))
    starts = jnp.moveaxis(starts, 0, 1)
    y_off = jnp.einsum("bclhn,bchpn,bhcl->bclhp", cq, starts.astype(xv.dtype),
                       jnp.exp(a_cum).astype(xv.dtype))
    y = (y_in + y_off).reshape(b, l, h, p)
    return y, final.astype(xv.dtype)


def bidir_scan(cq, kb, xv_f, xv_b, log_a_f, log_a_b, init_f, init_b):
    y_f, s_f = chunked_scan(cq, kb, xv_f, log_a_f, init_f)
    flip = lambda t: jnp.flip(t, 1)
    y_b, s_b = chunked_scan(flip(cq), flip(kb), flip(xv_b), flip(log_a_b), init_b)
    return y_f + flip(y_b), s_f, s_b


def dw_conv(x, w, bias):
    y = lax.conv_general_dilated(x, w[:, None, :].astype(x.dtype), window_strides=(1,),
                                 padding=[(D_CONV // 2, D_CONV // 2)],
                                 dimension_numbers=("NWC", "WIO", "NWC"),
                                 feature_group_count=x.shape[-1])
    return y + bias


def mixer(h, rope, init_ret, init_ssd, w_in, ret_log_decay, conv_w, conv_b, dt_bias, a_log,
          d_skip, ssd_norm_w, w_out):
    b, l, _ = h.shape
    proj = h @ w_in
    q, k, v, g, z, xbc, dt = jnp.split(proj, SPLITS, axis=-1)
    q = q.reshape(b, l, H_RET, DK_RET)
    k = k.reshape(b, l, H_RET, DK_RET)
    v = v.reshape(b, l, H_RET, DV_RET)
    if rope is not None:
        q = apply_rope(q, *rope)
        k = apply_rope(k, *rope)
    la = lambda d: jnp.broadcast_to(ret_log_decay[d].astype(jnp.float32), (b, l, H_RET))
    y_r, r_f, r_b = bidir_scan(q * (DK_RET ** -0.5), k, v, v, la(0), la(1), init_ret[0], init_ret[1])
    y_r = jax.nn.silu(g) * head_rms(y_r).reshape(b, l, D_RET)
    xbc = jax.nn.silu(dw_conv(xbc, conv_w, conv_b))
    xs, bm, cm = jnp.split(xbc, [D_SSD, D_SSD + G_SSD * N_SSD], axis=-1)
    xs = xs.reshape(b, l, H_SSD, P_SSD)
    rep = H_SSD // G_SSD
    bm = jnp.repeat(bm.reshape(b, l, G_SSD, N_SSD), rep, axis=2)
    cm = jnp.repeat(cm.reshape(b, l, G_SSD, N_SSD), rep, axis=2)
    dt = jax.nn.softplus(dt.astype(jnp.float32).reshape(b, l, 2, H_SSD) + dt_bias.astype(jnp.float32))
    a = -jnp.exp(a_log.astype(jnp.float32))
    xdt = lambda d: xs * dt[:, :, d, :, None].astype(xs.dtype)
    y_s, s_f, s_b = bidir_scan(cm, bm, xdt(0), xdt(1), dt[:, :, 0] * a[0], dt[:, :, 1] * a[1],
                               init_ssd[0], init_ssd[1])
    y_s = y_s + d_skip[:, None] * xs
    y_s = rms_norm(y_s.reshape(b, l, D_SSD) * jax.nn.silu(z), ssd_norm_w)
    out = jnp.concatenate([y_r, y_s], -1) @ w_out
    return out, (r_f, r_b, s_f, s_b)


def setup_inputs(seed: int = 0) -> dict:
    key = jax.random.key(seed)
    ks = jax.random.split(key, 24)
    f32 = jnp.float32
    nrm = lambda k, s, sc: jax.random.normal(k, s, f32) * sc
    base_decay = jnp.log(1.0 - 2.0 ** (-5.0 - jnp.arange(H_RET, dtype=f32)))
    ret_log_decay = base_decay[None, None, :] * jnp.exp(nrm(ks[10], (DEPTH, 2, H_RET), 0.05))
    dt0 = jnp.exp(jax.random.uniform(ks[11], (DEPTH, 2, H_SSD), f32, math.log(1e-3), math.log(1e-1)))
    dt_bias = dt0 + jnp.log(-jnp.expm1(-dt0))
    a_log = jnp.log(jax.random.uniform(ks[12], (DEPTH, 2, H_SSD), f32, 1.0, 16.0))
    return {
        "x_prompt": nrm(ks[0], (BATCH, SEQ, D_MODEL), 1.0),
        "x_sample": nrm(ks[1], (DEC_BATCH, DEC_SEQ, D_MODEL), 1.0),
        "state_ret": nrm(ks[2], (DEC_BATCH, DEPTH, 2, H_RET, DV_RET, DK_RET), 1.0),
        "state_ssd": nrm(ks[3], (DEC_BATCH, DEPTH, 2, H_SSD, P_SSD, N_SSD), 1.0),
        "c": nrm(ks[4], (DEC_BATCH, D_MODEL), 1.0),
        "c_ctx": nrm(ks[5], (D_MODEL,), 1.0),
        "w_ada": nrm(ks[6], (DEPTH, D_MODEL, N_MOD * D_MODEL), 0.5 * D_MODEL ** -0.5),
        "b_ada": nrm(ks[7], (DEPTH, N_MOD * D_MODEL), 0.02),
        "norm1_w": 1.0 + nrm(ks[8], (DEPTH, D_MODEL), 0.02),
        "w_in": nrm(ks[9], (DEPTH, D_MODEL, D_IN), D_MODEL ** -0.5),
        "ret_log_decay": ret_log_decay,
        "conv_w": nrm(ks[13], (DEPTH, D_CONV, D_XBC), D_CONV ** -0.5),
        "conv_b": nrm(ks[14], (DEPTH, D_XBC), 0.02),
        "dt_bias": dt_bias,
        "a_log": a_log,
        "d_skip": 1.0 + nrm(ks[15], (DEPTH, H_SSD), 0.02),
        "ssd_norm_w": 1.0 + nrm(ks[16], (DEPTH, D_SSD), 0.02),
        "w_out": nrm(ks[17], (DEPTH, D_RET + D_SSD, D_MODEL), (D_RET + D_SSD) ** -0.5),
        "norm2_w": 1.0 + nrm(ks[18], (DEPTH, D_MODEL), 0.02),
        "w_ff1": nrm(ks[19], (DEPTH, D_MODEL, D_FF), D_MODEL ** -0.5),
        "w_ff2": nrm(ks[20], (DEPTH, D_FF, D_MODEL), D_FF ** -0.5),
        "final_norm_w": 1.0 + nrm(ks[21], (D_MODEL,), 0.02),
    }


def reference(x_prompt, x_sample, state_ret, state_ssd, c, c_ctx, w_ada, b_ada, norm1_w, w_in,
              ret_log_decay, conv_w, conv_b, dt_bias, a_log, d_skip, ssd_norm_w, w_out, norm2_w,
              w_ff1, w_ff2, final_norm_w):
    def run_layer(x, cond, l, rope, init_ret, init_ssd):
        mod = (jax.nn.silu(cond) @ w_ada[l] + b_ada[l])[:, None, :]
        sh1, sc1, g1, sh2, sc2, g2 = jnp.split(mod, N_MOD, axis=-1)
        h = rms_norm(x, norm1_w[l]) * (1 + sc1) + sh1
        m, st = mixer(h, rope, init_ret, init_ssd, w_in[l], ret_log_decay[l], conv_w[l], conv_b[l],
                      dt_bias[l], a_log[l], d_skip[l], ssd_norm_w[l], w_out[l])
        x = x + g1 * m
        h = rms_norm(x, norm2_w[l]) * (1 + sc2) + sh2
        x = x + g2 * (jnp.square(jax.nn.relu(h @ w_ff1[l])) @ w_ff2[l])
        return x, st

    bp = x_prompt.shape[0]
    zr = jnp.zeros((bp, H_RET, DV_RET, DK_RET), x_prompt.dtype)
    zs = jnp.zeros((bp, H_SSD, P_SSD, N_SSD), x_prompt.dtype)
    xp = x_prompt
    ret_states = []
    ssd_states = []
    for l in range(DEPTH):
        xp, (rf, rb, sf, sb) = run_layer(xp, c_ctx[None, :], l, None, (zr, zr), (zs, zs))
        ret_states.append(jnp.stack([rf, rb], 1))
        ssd_states.append(jnp.stack([sf, sb], 1))
    new_state_ret = jnp.stack(ret_states, 1)
    new_state_ssd = jnp.stack(ssd_states, 1)
    y_prompt = rms_norm(xp, final_norm_w)

    rope = grid_rope(x_sample.shape[1])
    xs = x_sample
    for l in range(DEPTH):
        xs, _ = run_layer(xs, c, l, rope,
                          (state_ret[:, l, 0], state_ret[:, l, 1]),
                          (state_ssd[:, l, 0], state_ssd[:, l, 1]))
    y_sample = rms_norm(xs, final_norm_w)
    return (y_prompt, y_sample, new_state_ret, new_state_ssd)
```

```python
import types
import numpy as np
import ml_dtypes
from contextlib import ExitStack
import concourse.bass as bass
import concourse.mybir as mybir
from concourse.bass_utils import run_bass_kernel_spmd

F32 = mybir.dt.float32
BF16 = mybir.dt.bfloat16
ALU = mybir.AluOpType
AF = mybir.ActivationFunctionType
AX = mybir.AxisListType

EPOCH = 30000


class Tok:
    __slots__ = ("key", "val", "eng")

    def __init__(self, eng):
        self.key = None
        self.val = None
        self.eng = eng


class Buf:
    __slots__ = ("name", "w", "r", "dsem")

    def __init__(self, name):
        self.name = name
        self.w = None
        self.r = []
        self.dsem = None


class Prog:
    CE = ("pe", "act", "dve", "pool")
    ALLE = ("pe", "act", "dve", "pool", "sp")

    def __init__(self, nc, stack, arena_words=53200):
        self.nc = nc
        self.stack = stack
        self.ops = {e: [] for e in self.ALLE}
        self.sems = {}
        self.ecount = {e: 0 for e in self.CE}
        self.waited = {e: {} for e in self.ALLE}
        self.pe_pending = []
        self.pe_last_rec = None
        self.dma_sems = {}
        self.nsem = 0
        self.AW = arena_words
        self.arena = stack.enter_context(nc.sbuf_tensor("arena", [128, arena_words], F32))
        self.top = 0
        self.psum = [stack.enter_context(nc.psum_tensor(f"psb{i}", [128, 512], F32)) for i in range(8)]
        self.pbuf = [Buf(f"psum{i}") for i in range(8)]
        self.n_inst = {e: 0 for e in self.ALLE}
        self.dsem_ctr = 0
        self.dsem_base = 0

    def pin(self, buf):
        buf.dsem = ("d", self.dsem_ctr)
        self.dsem_ctr += 1
        self.dsem_base = self.dsem_ctr

    def new_phase(self):
        self.dsem_ctr = self.dsem_base

    def alloc(self, name, free_shape, dtype=F32):
        n = int(np.prod(free_shape))
        nw = n if dtype == F32 else (n + 1) // 2
        nw = (nw + 7) // 8 * 8
        off = self.top
        self.top += nw
        assert self.top <= self.AW, f"arena overflow at {name}: {self.top}"
        v = self.arena[:, off:off + nw]
        if dtype != F32:
            v = v.bitcast(dtype)
        v = v[:, 0:n]
        if len(free_shape) == 2:
            v = v.rearrange("p (a b) -> p a b", a=free_shape[0])
        elif len(free_shape) == 3:
            v = v.rearrange("p (a b c) -> p a b c", a=free_shape[0], b=free_shape[1])
        return v, Buf(name)

    def ps(self, i, dtype=F32):
        v = self.psum[i][:, :]
        if dtype != F32:
            v = v.bitcast(dtype)
        return v

    def _sem(self, key):
        if key not in self.sems:
            self.nsem += 1
            self.sems[key] = self.stack.enter_context(self.nc.semaphore(f"s{self.nsem}"))
        return self.sems[key]

    def _new_signal(self, eng):
        c = self.ecount[eng]
        self.ecount[eng] = c + 1
        key = ("e", eng, c // EPOCH)
        self._sem(key)
        return key, (c % EPOCH) + 1

    def _need(self, eng, tok, raw):
        if tok is None:
            return
        if tok.eng == eng and eng in self.CE and not raw:
            return
        if tok.key is None:
            self._force_pe_signal()
        w = self.waited[eng]
        if w.get(tok.key, 0) >= tok.val:
            return
        w[tok.key] = tok.val
        sem = self.sems[tok.key]
        val = tok.val
        self.ops[eng].append(lambda e, sem=sem, val=val: e.wait_ge(sem, val))
        self.n_inst[eng] += 1

    def _force_pe_signal(self):
        rec = self.pe_last_rec
        assert rec["sig"] is None
        key, val = self._new_signal("pe")
        rec["sig"] = (self.sems[key], 1)
        for t in self.pe_pending:
            t.key, t.val = key, val
        self.pe_pending = []

    def _deps(self, eng, reads, writes):
        for b in reads:
            self._need(eng, b.w, True)
        for b in writes:
            self._need(eng, b.w, False)
            for t in b.r:
                self._need(eng, t, False)

    def _commit(self, tok, reads, writes):
        for b in reads:
            if len(b.r) > 24:
                d = {}
                rest = []
                for t in b.r:
                    if t.key is None:
                        rest.append(t)
                    elif t.key not in d or d[t.key].val < t.val:
                        d[t.key] = t
                b.r = rest + list(d.values())
            b.r.append(tok)
        for b in writes:
            b.w = tok
            b.r = []

    @staticmethod
    def _freeze(fn):
        if fn.__closure__ is None:
            return fn
        cells = []
        for c in fn.__closure__:
            try:
                cells.append(types.CellType(c.cell_contents))
            except ValueError:
                cells.append(c)
        g = types.FunctionType(fn.__code__, fn.__globals__, fn.__name__, fn.__defaults__, tuple(cells))
        g.__kwdefaults__ = fn.__kwdefaults__
        return g

    def op(self, eng, fn, reads=(), writes=(), signal=True):
        fn = self._freeze(fn)
        self._deps(eng, reads, writes)
        tok = Tok(eng)
        rec = {"sig": None}
        if signal:
            key, val = self._new_signal(eng)
            tok.key, tok.val = key, val
            rec["sig"] = (self.sems[key], 1)
            if eng == "pe":
                for t in self.pe_pending:
                    t.key, t.val = key, val
                self.pe_pending = []
        else:
            assert eng == "pe"
            self.pe_pending.append(tok)
        if eng == "pe":
            self.pe_last_rec = rec

        def run(e, fn=fn, rec=rec):
            ins = fn(e)
            if rec["sig"] is not None:
                ins.then_inc(rec["sig"][0], rec["sig"][1])
        self.ops[eng].append(run)
        self.n_inst[eng] += 1
        self._commit(tok, reads, writes)
        return tok

    def dma(self, q, out, in_, reads=(), writes=(), sbuf=None, sem_key=None, **kw):
        self._deps(q, reads, writes)
        if sem_key is None:
            if sbuf.dsem is None:
                sbuf.dsem = ("d", self.dsem_ctr)
                self.dsem_ctr += 1
            sem_key = sbuf.dsem
        self._sem(sem_key)
        cnt = self.dma_sems.get(sem_key, 0) + 16
        self.dma_sems[sem_key] = cnt
        tok = Tok(None)
        tok.key, tok.val = sem_key, cnt
        sem = self.sems[sem_key]

        def run(e, out=out, in_=in_, sem=sem, kw=kw):
            e.dma_start(out=out, in_=in_, **kw).then_inc(sem, 16)
        self.ops[q].append(run)
        self.n_inst[q] += 1
        self._commit(tok, reads, writes)
        return tok

    def collective(self, kind, groups, in_ap, out_ap, reads=(), writes=(), inc=16):
        q = "pool"
        self._deps(q, reads, writes)
        sem_key = ("cc",)
        self._sem(sem_key)
        cnt = self.dma_sems.get(sem_key, 0) + inc
        self.dma_sems[sem_key] = cnt
        tok = Tok(None)
        tok.key, tok.val = sem_key, cnt
        sem = self.sems[sem_key]

        def run(e, kind=kind, groups=groups, in_ap=in_ap, out_ap=out_ap, sem=sem, inc=inc):
            e.collective_compute(kind, ALU.bypass, replica_groups=groups, ins=[in_ap], outs=[out_ap]).then_inc(sem, inc)
        self.ops[q].append(run)
        self.n_inst[q] += 1
        self._commit(tok, reads, writes)
        return tok

    def barrier(self):
        toks = []
        for e in self.CE:
            if e == "pe" and self.pe_pending:
                self._force_pe_signal()
            c = self.ecount[e]
            if c > 0:
                t = Tok(None)
                t.key = ("e", e, (c - 1) // EPOCH)
                t.val = ((c - 1) % EPOCH) + 1
                toks.append(t)
        for k, cnt in self.dma_sems.items():
            t = Tok(None)
            t.key, t.val = k, cnt
            toks.append(t)
        for e in self.ALLE:
            for t in toks:
                self._need(e, t, True)

    def emit(self):
        ops = self.ops
        with self.nc.Block() as block:
            @block.tensor
            def _(e):
                for f in ops["pe"]:
                    f(e)

            @block.scalar
            def _(e):
                for f in ops["act"]:
                    f(e)

            @block.vector
            def _(e):
                for f in ops["dve"]:
                    f(e)

            @block.gpsimd
            def _(e):
                for f in ops["pool"]:
                    f(e)

            @block.sync
            def _(e):
                for f in ops["sp"]:
                    f(e)


D = 2048
DIN = 6688
HR = 8
HS = 16
PSD = 64
T = 128
DFF = 8192
EPS = 1e-6
DKS = 128 ** -0.5


def bc_last(ap, n):
    return ap.unsqueeze(2).broadcast_to([ap.shape[0], ap.shape[1], n])


def bc_mid(ap, n):
    return ap.unsqueeze(1).broadcast_to([ap.shape[0], n, ap.shape[1]])


def build_program(DEPTH, NP, PL, SL, EXCH=False):
    PAIRS = [[0, 1], [2, 3], [4, 5], [6, 7]]
    NTOK = NP * PL + SL
    NT = NTOK // T
    CP = PL // T
    CS = SL // T
    seqs = [(i * CP, CP, False, i) for i in range(NP)] + [(NP * CP, CS, True, NP)]
    NSEQ = len(seqs)
    groups = []
    pt = list(range(NP * CP))
    for i in range(0, len(pt), 4):
        groups.append((pt[i:i + 4], 0))
    stl = list(range(NP * CP, NT))
    for i in range(0, len(stl), 4):
        groups.append((stl[i:i + 4], 1))

    nc = bass.Bass("TRN2", target_bir_lowering=False)

    def din(name, shape, dt=F32):
        return nc.dram_tensor(name, list(shape), dt, kind="ExternalInput").ap()

    def dout(name, shape, dt=F32):
        return nc.dram_tensor(name, list(shape), dt, kind="ExternalOutput").ap()

    def dscr(name, shape, dt=F32):
        return nc.dram_tensor(name, list(shape), dt, kind=("ExternalOutput" if (DEBUG_SCRATCH and not name.startswith("wbf")) else "Internal")).ap()

    x_in = din("x_in", [NTOK, D])
    cond = din("cond", [2, D])
    st_ret = din("st_ret", [DEPTH, 2, 128, 1024])
    st_ssd = din("st_ssd", [DEPTH, 2, 128, 1024])
    rope_cs = din("rope_cs", [SL, 128])
    cmask_d = din("cmask", [128, 5, 128])
    diff_d = din("diffm", [128, 128])
    pidx_d = din("pidx", [128, 4])
    zrow_d = din("zrow", [1, 1536], BF16)
    psel_d = din("psel", [128, 2])
    if not EXCH:
        w_ada = din("w_ada", [DEPTH, D, 6 * D])
        b_ada = din("b_ada", [DEPTH, 6 * D])
        norm1_w = din("norm1_w", [DEPTH, D])
        norm2_w = din("norm2_w", [DEPTH, D])
    w_in = din("w_in", [DEPTH, D, DIN])
    rld = din("ret_log_decay", [DEPTH, 16])
    conv_w = din("conv_w", [DEPTH, 3 * 1536])
    conv_b = din("conv_b", [DEPTH, 1536])
    dt_bias = din("dt_bias", [DEPTH, 32])
    a_log = din("a_log", [DEPTH, 32])
    d_skip = din("d_skip", [DEPTH, 16])
    ssd_nw = din("ssd_norm_w", [DEPTH, 1024])
    w_out = din("w_out", [DEPTH, D, D])
    w_ff1 = din("w_ff1", [DEPTH, D, DFF])
    w_ff2 = din("w_ff2", [DEPTH, DFF, D])
    fnw = din("final_norm_w", [1, D])

    y_out = dout("y_out", [NTOK, D])
    ns_ret = dout("ns_ret", [NP, DEPTH, 2, 128, 1024])
    ns_ssd = dout("ns_ssd", [NP, DEPTH, 2, 128, 1024])

    xs_d = dscr("xs_d", [NTOK, D])
    qk_d = dscr("qk_d", [NTOK, 2048], BF16)
    v_d = dscr("v_d", [NTOK, 1024], BF16)
    sg_d = dscr("sg_d", [NTOK, 1024], BF16)
    sz_d = dscr("sz_d", [NTOK, 1024], BF16)
    xpre_d = dscr("xpre_d", [NTOK + 2 * NSEQ, 1536], BF16)
    xpost_d = dscr("xpost_d", [NTOK, 1536], BF16)
    dt_d = dscr("dt_d", [NTOK, 32])
    y_d = dscr("y_d", [NTOK, 2048], BF16)
    stf_d = dscr("stf_d", [NT, 128, 2048], BF16)
    cstb_d = dscr("cstb_d", [NT, 128, 2048])
    cdb_d = dscr("cdb_d", [NT, 128, 16])
    mod_d = dscr("mod_d", [DEPTH, 2, 6 * D])
    NWT = 14 + 4 + 16 + 16
    if DEBUG_SCRATCH:
        dbg_h = dout("dbg_h", [NTOK, 2048], BF16)
        dbg_x = dout("dbg_x", [NTOK, 2048])
        dbg_a = dout("dbg_a", [NTOK, 2048])
        dbg_h32 = dout("dbg_h32", [NTOK, 2048])
    wbf = [dict(w0=dscr(f"wbf{i}_in", [D, DIN], BF16), w1=dscr(f"wbf{i}_out", [D, D], BF16),
                w2=dscr(f"wbf{i}_ff1", [D, DFF], BF16), w3=dscr(f"wbf{i}_ff2", [DFF, D], BF16)) for i in range(2)]
    exst_in = dscr("exst_in", [128, 2048])
    exst_out = dscr("exst_out", [256, 2048])
    exrow_in = dscr("exrow_in", [1, 1536], BF16)
    exrow_out = dscr("exrow_out", [2, 1536], BF16)
    if EXCH:
        w_ada_h = din("w_ada_h", [DEPTH, D, 3 * D])
        b_ada_h = din("b_ada_h", [DEPTH, 3 * D])
        nw_h = din("nw_h", [DEPTH, D])
        exmod_in = dscr("exmod_in", [DEPTH * 2, 3 * D])
        exmod_out = dscr("exmod_out", [2 * DEPTH * 2, 3 * D])

    def modrow(l, ci, slot):
        if EXCH:
            r = (slot // 3) * DEPTH * 2 + l * 2 + ci
            return exmod_out[r, (slot % 3) * D:(slot % 3 + 1) * D]
        return mod_d[l, ci, slot * D:(slot + 1) * D]

    def xrow(si, t):
        t0, ncks, _, _ = seqs[si]
        return t0 * T + 2 * si + 1 + t

    with ExitStack() as st:
        P = Prog(nc, st)
        ident, b_ident = P.alloc("ident", [128], BF16)
        cmask, b_cmask = P.alloc("cmask", [5, 128])
        diffm, b_diff = P.alloc("diffm", [128])
        pidx, b_pidx = P.alloc("pidx", [4])
        psel, b_psel = P.alloc("psel", [2])
        NRING = 3
        wring = [P.alloc(f"wring{i}", [16, 512], BF16) for i in range(NRING)]
        base_top = P.top
        for b_ in (b_cmask, b_diff, b_pidx, b_psel, wring[0][1], wring[1][1], wring[2][1]):
            P.pin(b_)
        M_gt, M_le, M_lt, M_ge, M_one = [cmask[:, i, :] for i in range(5)]

        d_const = Buf("d_const")
        B_xs = [Buf(f"xs{t}") for t in range(NT)]
        B_qk = [Buf(f"qk{t}") for t in range(NT)]
        B_v = [Buf(f"v{t}") for t in range(NT)]
        B_sg = [Buf(f"sg{t}") for t in range(NT)]
        B_sz = [Buf(f"sz{t}") for t in range(NT)]
        B_xpre = Buf("xpre")
        B_xpost = [Buf(f"xpost{t}") for t in range(NT)]
        B_dt = [Buf(f"dt{t}") for t in range(NT)]
        B_y = [Buf(f"y{t}") for t in range(NT)]
        B_stf = [Buf(f"stf{t}") for t in range(NT)]
        B_cstb = [Buf(f"cstb{t}") for t in range(NT)]
        B_cdb = [Buf(f"cdb{t}") for t in range(NT)]
        B_mod = Buf("mod")
        B_wbf = [[Buf(f"wbf{i}_{k}") for k in range(4)] for i in range(2)]
        B_out = Buf("outs")
        B_exin = Buf("exin"); B_exout = Buf("exout"); B_exrin = Buf("exrin"); B_exrout = Buf("exrout")

        P.dma("sp", cmask, cmask_d, reads=[d_const], writes=[b_cmask], sbuf=b_cmask)
        P.dma("sp", diffm, diff_d, reads=[d_const], writes=[b_diff], sbuf=b_diff)
        P.dma("sp", pidx, pidx_d, reads=[d_const], writes=[b_pidx], sbuf=b_pidx)
        P.dma("sp", psel, psel_d, reads=[d_const], writes=[b_psel], sbuf=b_psel)
        P.op("pool", lambda e: e.memset(ident, 0.0), writes=[b_ident])
        P.op("pool", lambda e: e.affine_select(out=ident, in_=ident, pattern=[[-1, 128]],
                                               compare_op=ALU.not_equal, fill=1.0, base=0,
                                               channel_multiplier=1), reads=[b_ident], writes=[b_ident])
        for si in range(NSEQ):
            for t in (-1, seqs[si][1] * T):
                r = xrow(si, t)
                P.dma("sp", xpre_d[r:r + 1, :], zrow_d, reads=[d_const], writes=[B_xpre], sem_key=("x", "zrow"))

        def wtile_src(par, i):
            if i < 14:
                nco = 512 if i < 13 else 32
                return 0, wbf[par]["w0"][:, i * 512:i * 512 + nco].rearrange("(kc p) c -> p kc c", p=128), nco
            i -= 14
            if i < 4:
                return 1, wbf[par]["w1"][:, i * 512:(i + 1) * 512].rearrange("(kc p) c -> p kc c", p=128), 512
            i -= 4
            if i < 16:
                return 2, wbf[par]["w2"][:, i * 512:(i + 1) * 512].rearrange("(kc p) c -> p kc c", p=128), 512
            i -= 16
            cb, kp = i // 4, i % 4
            return 3, wbf[par]["w3"][kp * 2048:(kp + 1) * 2048, cb * 512:(cb + 1) * 512].rearrange("(kc p) c -> p kc c", p=128), 512

        conv_jobs = []

        def convert_layer(l):
            par = l % 2
            for kind, (src, R) in enumerate(((w_in[l], D), (w_out[l], D), (w_ff1[l], D), (w_ff2[l], DFF))):
                dst = wbf[par]["w%d" % kind]
                for r0 in range(0, R, 128):
                    conv_jobs.append((par, kind, dst[r0:r0 + 128, :], src[r0:r0 + 128, :]))

        def pump_convert(n):
            for _ in range(min(n, len(conv_jobs))):
                par, kind, dst, src = conv_jobs.pop(0)
                P.dma("pool", dst, src, reads=[d_const], writes=[B_wbf[par][kind]], sem_key=("wc", par, kind))

        class WStream:
            def __init__(self):
                self.seq = []
                self.issued = 0
                self.slot_n = 0

            def extend(self, l, idxs):
                for i in idxs:
                    self.seq.append((l, i))

            def prefetch(self, upto):
                while self.issued < min(upto, len(self.seq)):
                    l, i = self.seq[self.issued]
                    kind, src, nco = wtile_src(l % 2, i)
                    slot = self.issued % NRING
                    wt, bw = wring[slot]
                    P.dma("sp", wt[:, :, 0:nco], src, reads=[B_wbf[l % 2][kind]], writes=[bw], sbuf=bw)
                    self.issued += 1

            def get(self, n):
                self.prefetch(n + NRING)
                return wring[n % NRING]

        WS = WStream()
        wcount = [0]

        def next_w():
            r = WS.get(wcount[0])
            wcount[0] += 1
            return r

        for l in range(DEPTH):
            for g in groups:
                WS.extend(l, range(14))
            for g in groups:
                WS.extend(l, range(14, NWT))

        convert_layer(0)
        pump_convert(10 ** 6)
        m0 = P.top
        cT, b_cT = P.alloc("cT", [16, 2])
        modsb, b_modsb = P.alloc("modsb", [6 * D])
        badab, b_bada = P.alloc("badab", [6 * D])
        nwb, b_nwb = P.alloc("nwb", [2 * D])
        wada = [P.alloc(f"wada{i}", [4096]) for i in range(2)]
        for ci_ in range(2):
            P.dma("sp", cT[:, :, ci_], cond[ci_, :].rearrange("(kc p) -> p kc", p=128), reads=[d_const], writes=[b_cT], sbuf=b_cT,
                  allow_slow_non_contiguous=True)
        P.op("act", lambda e: e.activation(out=cT, in_=cT, func=AF.Silu), reads=[b_cT], writes=[b_cT])
        wn = 0
        for l in (range(DEPTH) if EXCH else []):
            P.dma("sp", badab[0:2, 0:3 * D], b_ada_h[l, :].partition_broadcast(2), reads=[d_const], writes=[b_bada], sbuf=b_bada)
            P.dma("sp", nwb[0:2, 0:D], nw_h[l, :].partition_broadcast(2), reads=[d_const], writes=[b_nwb], sbuf=b_nwb)
            for cg in range(2):
                ncol = 4096 if cg == 0 else 2048
                nbk = ncol // 512
                for kc in range(16):
                    wt, bw = wada[wn % 2]
                    wn += 1
                    P.dma("sp", wt[:, 0:ncol], w_ada_h[l][kc * 128:(kc + 1) * 128, cg * 4096:cg * 4096 + ncol],
                          reads=[d_const], writes=[bw], sbuf=bw)
                    for b in range(nbk):
                        P.op("pe", lambda e, b=b, kc=kc, wt=wt: e.matmul(P.ps(b)[0:2, :], lhsT=cT[:, kc, :], rhs=wt[:, b * 512:(b + 1) * 512],
                                                                         start=(kc == 0), stop=(kc == 15)),
                             reads=[b_cT, bw], writes=[P.pbuf[b]], signal=(kc == 15 or b == nbk - 1))
                for b in range(nbk):
                    c0 = cg * 4096 + b * 512
                    P.op("dve", lambda e, b=b, c0=c0: e.tensor_tensor(out=modsb[0:2, c0:c0 + 512], in0=P.ps(b)[0:2, :],
                                                                      in1=badab[0:2, c0:c0 + 512], op=ALU.add),
                         reads=[P.pbuf[b], b_bada], writes=[b_modsb])
            P.op("dve", lambda e: e.scalar_tensor_tensor(out=modsb[0:2, D:2 * D], in0=modsb[0:2, D:2 * D], scalar=1.0,
                                                         in1=nwb[0:2, 0:D], op0=ALU.add, op1=ALU.mult), reads=[b_modsb, b_nwb], writes=[b_modsb])
            P.dma("sp", exmod_in[l * 2:l * 2 + 2, :], modsb[0:2, 0:3 * D], reads=[b_modsb], writes=[B_mod], sbuf=b_modsb)
        if EXCH:
            B_modin = B_mod
            B_mod = Buf("modout")
            P.collective("AllGather", PAIRS, exmod_in, exmod_out, reads=[B_modin], writes=[B_mod], inc=1)
        for l in ([] if EXCH else range(DEPTH)):
            P.dma("sp", badab[0:2, :], b_ada[l, :].partition_broadcast(2),
                  reads=[d_const], writes=[b_bada], sbuf=b_bada)
            P.dma("sp", nwb[0:2, 0:D], norm1_w[l, :].partition_broadcast(2), reads=[d_const], writes=[b_nwb], sbuf=b_nwb)
            P.dma("sp", nwb[0:2, D:2 * D], norm2_w[l, :].partition_broadcast(2), reads=[d_const], writes=[b_nwb], sbuf=b_nwb)
            for cg in range(3):
                for kc in range(16):
                    wt, bw = wada[wn % 2]
                    wn += 1
                    P.dma("sp", wt, w_ada[l][kc * 128:(kc + 1) * 128, cg * 4096:(cg + 1) * 4096],
                          reads=[d_const], writes=[bw], sbuf=bw)
                    for b in range(8):
                        P.op("pe", lambda e, b=b, kc=kc, wt=wt: e.matmul(P.ps(b)[0:2, :], lhsT=cT[:, kc, :], rhs=wt[:, b * 512:(b + 1) * 512],
                                                                         start=(kc == 0), stop=(kc == 15)),
                             reads=[b_cT, bw], writes=[P.pbuf[b]], signal=(kc == 15 or b == 7))
                for b in range(8):
                    c0 = cg * 4096 + b * 512
                    P.op("dve", lambda e, b=b, c0=c0: e.tensor_tensor(out=modsb[0:2, c0:c0 + 512], in0=P.ps(b)[0:2, :],
                                                                      in1=badab[0:2, c0:c0 + 512], op=ALU.add),
                         reads=[P.pbuf[b], b_bada], writes=[b_modsb])
            for slot, off in ((1, 0), (4, D)):
                P.op("dve", lambda e, slot=slot, off=off: e.scalar_tensor_tensor(
                    out=modsb[0:2, slot * D:(slot + 1) * D], in0=modsb[0:2, slot * D:(slot + 1) * D], scalar=1.0,
                    in1=nwb[0:2, off:off + D], op0=ALU.add, op1=ALU.mult), reads=[b_modsb, b_nwb], writes=[b_modsb])
            P.dma("sp", mod_d[l], modsb[0:2, :], reads=[b_modsb], writes=[B_mod], sbuf=b_modsb)
        P.barrier()
        P.new_phase()
        P.top = m0

        def rstd_of(ss, n, ncols, tmp, out, b_ss, b_tmp, b_out):
            P.op("act", lambda e: e.activation(out=tmp, in_=ss, func=AF.Ln, scale=1.0 / n, bias=EPS), reads=[b_ss], writes=[b_tmp])
            P.op("act", lambda e: e.activation(out=out, in_=tmp, func=AF.Exp, scale=-0.5), reads=[b_tmp], writes=[b_out])

        ps_rot = [0]

        def transposes_to(src, b_src, nblk, dst_fn, b_dst, evac_engs=("act", "dve")):
            i = 0
            k = 0
            while i < nblk:
                n = min(8, nblk - i)
                bank = ps_rot[0] % 2
                ps_rot[0] += 1
                pst = P.ps(bank, BF16)
                for j in range(n):
                    P.op("pe", lambda e, j=j, i=i, pst=pst: e.transpose(out=pst[:, j * 128:(j + 1) * 128], in_=src[:, (i + j) * 128:(i + j + 1) * 128],
                                                                       identity=ident),
                         reads=[b_src, b_ident], writes=[P.pbuf[bank]], signal=(j == n - 1))
                eng = evac_engs[k % len(evac_engs)]
                k += 1
                dst = dst_fn(i, n)
                srcv = pst[:, 0:n * 128].rearrange("p (a b) -> p a b", a=n)
                if eng == "act":
                    P.op("act", lambda e, dst=dst, srcv=srcv: e.copy(out=dst, in_=srcv), reads=[P.pbuf[bank]], writes=[b_dst])
                else:
                    P.op("dve", lambda e, dst=dst, srcv=srcv: e.tensor_copy(out=dst, in_=srcv), reads=[P.pbuf[bank]], writes=[b_dst])
                i += n

        mm_rot = [0]

        def mm_bank():
            b = 2 + (mm_rot[0] % 6)
            mm_rot[0] += 1
            return b

        for l in range(DEPTH):
            last = (l == DEPTH - 1)
            if l + 1 < DEPTH:
                convert_layer(l + 1)
            x_src = x_in if l == 0 else xs_d
            B_xsrc = (lambda t: d_const) if l == 0 else (lambda t: B_xs[t])

            mA = P.top
            hT, b_hT = P.alloc("hT", [16, 512], BF16)
            xb = [P.alloc(f"xb{i}", [D]) for i in range(2)]
            h32, b_h32 = P.alloc("h32", [D])
            hbf = [P.alloc(f"hbf{i}", [D], BF16) for i in range(2)]
            junk, b_junk = P.alloc("junk", [D], BF16)
            a1bc, b_a1 = P.alloc("a1bc", [D])
            sh1bc, b_sh1 = P.alloc("sh1bc", [D])
            small, b_small = P.alloc("smallA", [8])
            ropet, b_rope = P.alloc("ropet", [4, 128])
            dtbb, b_dtbb = P.alloc("dtbb", [32])
            stg = [P.alloc(f"stg{i}", [512], BF16) for i in range(6)]
            rt = [P.alloc(f"rt{i}", [4, 64]) for i in range(4)]
            dtst = [P.alloc(f"dtst{i}", [32]) for i in range(2)]
            P.dma("sp", dtbb, dt_bias[l, :].partition_broadcast(128), reads=[d_const], writes=[b_dtbb], sbuf=b_dtbb)
            cur_ci = -1
            stg_n = 0
            for (tiles, ci) in groups:
                G = len(tiles) * T
                if ci != cur_ci:
                    cur_ci = ci
                    P.dma("sp", a1bc, modrow(l, ci, 1).partition_broadcast(128), reads=[B_mod], writes=[b_a1], sbuf=b_a1)
                    P.dma("sp", sh1bc, modrow(l, ci, 0).partition_broadcast(128), reads=[B_mod], writes=[b_sh1], sbuf=b_sh1)
                if ci == 1:
                    r0 = (tiles[0] - NP * CP) * T
                    P.dma("sp", ropet[:, 0:len(tiles), :], rope_cs[r0:r0 + G, :].rearrange("(j p) c -> p j c", p=128),
                          reads=[d_const], writes=[b_rope], sbuf=b_rope)
                for j, tt in enumerate(tiles):
                    pump_convert(3)
                    xt, bx = xb[j % 2]
                    hb, bh = hbf[j % 2]
                    P.dma("sp", xt, x_src[tt * T:(tt + 1) * T, :], reads=[B_xsrc(tt)], writes=[bx], sbuf=bx)
                    P.op("dve", lambda e: e.memset(small[:, 0:1], 0.0), writes=[b_small])
                    P.op("act", lambda e, xt=xt: e.activation(out=junk, in_=xt, func=AF.Square, accum_out=small[:, 0:1]),
                         reads=[bx], writes=[b_junk, b_small])
                    rstd_of(small[:, 0:1], D, 1, small[:, 1:2], small[:, 2:3], b_small, b_small, b_small)
                    P.op("dve", lambda e, xt=xt: e.scalar_tensor_tensor(out=h32, in0=xt, scalar=small[:, 2:3], in1=a1bc,
                                                                        op0=ALU.mult, op1=ALU.mult),
                         reads=[bx, b_small, b_a1], writes=[b_h32])
                    P.op("pool", lambda e, hb=hb: e.tensor_tensor(out=hb, in0=h32, in1=sh1bc, op=ALU.add),
                         reads=[b_h32, b_sh1], writes=[bh])
                    transposes_to(hb, bh, 16, lambda i, n, j=j: hT[:, i:i + n, j * T:(j + 1) * T], b_hT)
                    if DEBUG_SCRATCH and l == 0:
                        P.dma("sp", dbg_h[tt * T:(tt + 1) * T, :], hb, reads=[bh], writes=[B_out], sbuf=bh)
                        P.dma("sp", dbg_x[tt * T:(tt + 1) * T, :], xt, reads=[bx], writes=[B_out], sbuf=bx)
                        P.dma("sp", dbg_a[tt * T:(tt + 1) * T, :], a1bc, reads=[b_a1], writes=[B_out], sbuf=b_a1)
                        P.dma("sp", dbg_h32[tt * T:(tt + 1) * T, :], h32, reads=[b_h32], writes=[B_out], sbuf=b_h32)
                for cb in range(14):
                    wt, bw = next_w()
                    nco = 512 if cb < 13 else 32
                    for j, tt in enumerate(tiles):
                        bank = mm_bank()
                        psv = P.ps(bank)[:, 0:nco]
                        for kc in range(16):
                            P.op("pe", lambda e, kc=kc, j=j, wt=wt, psv=psv, nco=nco: e.matmul(
                                psv, lhsT=hT[:, kc, j * T:(j + 1) * T], rhs=wt[:, kc, 0:nco], start=(kc == 0), stop=(kc == 15)),
                                reads=[b_hT, bw], writes=[P.pbuf[bank]], signal=(kc == 15))
                        rows = slice(tt * T, (tt + 1) * T)
                        if cb < 13:
                            sgt, bs = stg[stg_n % 6]
                            stg_n += 1
                        if cb < 4:
                            if ci == 1:
                                ps3 = psv.rearrange("p (h n) -> p h n", h=4)
                                x1 = ps3[:, :, 0:64]
                                x2 = ps3[:, :, 64:128]
                                cs = bc_mid(ropet[:, j, 0:64], 4)
                                sn = bc_mid(ropet[:, j, 64:128], 4)
                                o3 = sgt.rearrange("p (h n) -> p h n", h=4)
                                (t1, bt1), (t2, bt2), (t3, bt3), (t4, bt4) = rt
                                P.op("dve", lambda e, x1=x1, cs=cs, t1=t1: e.tensor_tensor(out=t1, in0=x1, in1=cs, op=ALU.mult),
                                     reads=[P.pbuf[bank], b_rope], writes=[bt1])
                                P.op("dve", lambda e, x2=x2, sn=sn, t2=t2: e.tensor_tensor(out=t2, in0=x2, in1=sn, op=ALU.mult),
                                     reads=[P.pbuf[bank], b_rope], writes=[bt2])
                                P.op("dve", lambda e, x1=x1, sn=sn, t3=t3: e.tensor_tensor(out=t3, in0=x1, in1=sn, op=ALU.mult),
                                     reads=[P.pbuf[bank], b_rope], writes=[bt3])
                                P.op("dve", lambda e, x2=x2, cs=cs, t4=t4: e.tensor_tensor(out=t4, in0=x2, in1=cs, op=ALU.mult),
                                     reads=[P.pbuf[bank], b_rope], writes=[bt4])
                                P.op("pool", lambda e, o3=o3, t1=t1, t2=t2: e.tensor_tensor(out=o3[:, :, 0:64], in0=t1, in1=t2, op=ALU.subtract),
                                     reads=[bt1, bt2], writes=[bs])
                                P.op("pool", lambda e, o3=o3, t3=t3, t4=t4: e.tensor_tensor(out=o3[:, :, 64:128], in0=t3, in1=t4, op=ALU.add),
                                     reads=[bt3, bt4], writes=[bs])
                            else:
                                P.op("act", lambda e, sgt=sgt, psv=psv: e.copy(out=sgt, in_=psv), reads=[P.pbuf[bank]], writes=[bs])
                            P.dma("pool", qk_d[rows, cb * 512:(cb + 1) * 512], sgt, reads=[bs], writes=[B_qk[tt]], sbuf=bs)
                        elif cb < 6:
                            P.op("act", lambda e, sgt=sgt, psv=psv: e.copy(out=sgt, in_=psv), reads=[P.pbuf[bank]], writes=[bs])
                            P.dma("pool", v_d[rows, (cb - 4) * 512:(cb - 3) * 512], sgt, reads=[bs], writes=[B_v[tt]], sbuf=bs)
                        elif cb < 10:
                            P.op("act", lambda e, sgt=sgt, psv=psv: e.activation(out=sgt, in_=psv, func=AF.Silu),
                                 reads=[P.pbuf[bank]], writes=[bs])
                            if cb < 8:
                                P.dma("pool", sg_d[rows, (cb - 6) * 512:(cb - 5) * 512], sgt, reads=[bs], writes=[B_sg[tt]], sbuf=bs)
                            else:
                                P.dma("pool", sz_d[rows, (cb - 8) * 512:(cb - 7) * 512], sgt, reads=[bs], writes=[B_sz[tt]], sbuf=bs)
                        elif cb < 13:
                            P.op("dve", lambda e, sgt=sgt, psv=psv: e.tensor_copy(out=sgt, in_=psv), reads=[P.pbuf[bank]], writes=[bs])
                            si = [k for k, s in enumerate(seqs) if s[0] <= tt < s[0] + s[1]][0]
                            r = xrow(si, (tt - seqs[si][0]) * T)
                            P.dma("pool", xpre_d[r:r + T, (cb - 10) * 512:(cb - 9) * 512], sgt, reads=[bs], writes=[B_xpre], sbuf=bs)
                        else:
                            dtt, bdt = dtst[j % 2]
                            P.op("dve", lambda e, dtt=dtt, psv=psv: e.tensor_tensor(out=dtt, in0=psv, in1=dtbb, op=ALU.add),
                                 reads=[P.pbuf[bank], b_dtbb], writes=[bdt])
                            P.op("dve", lambda e, dtt=dtt: e.tensor_scalar_min(out=dtt, in0=dtt, scalar1=60.0), reads=[bdt], writes=[bdt])
                            P.op("act", lambda e, dtt=dtt: e.activation(out=dtt, in_=dtt, func=AF.Exp), reads=[bdt], writes=[bdt])
                            P.op("act", lambda e, dtt=dtt: e.activation(out=dtt, in_=dtt, func=AF.Ln, bias=1.0, scale=1.0), reads=[bdt], writes=[bdt])
                            P.dma("pool", dt_d[rows, :], dtt, reads=[bdt], writes=[B_dt[tt]], sbuf=bdt)
            P.barrier()
            P.new_phase()
            P.top = mA

            mB = P.top
            ldbc, b_ld = P.alloc("ldbc", [16])
            Mret, b_Mret = P.alloc("Mret", [8, 128])
            rtab, b_rtab = P.alloc("rtab", [5, 16])
            Abc, b_Abc = P.alloc("Abc", [32])
            dskb, b_dsk = P.alloc("dskb", [16])
            cwbc, b_cw = P.alloc("cwbc", [3, 1536])
            cbbc, b_cb = P.alloc("cbbc", [1536])
            snwb, b_snw = P.alloc("snwb", [1024])
            mt = [P.alloc(f"mt{i}", [128]) for i in range(4)]
            P.dma("sp", ldbc, rld[l, :].partition_broadcast(128), reads=[d_const], writes=[b_ld], sbuf=b_ld)
            P.dma("sp", Abc, a_log[l, :].partition_broadcast(128), reads=[d_const], writes=[b_Abc], sbuf=b_Abc)
            P.dma("sp", dskb, d_skip[l, :].partition_broadcast(128), reads=[d_const], writes=[b_dsk], sbuf=b_dsk)
            P.dma("sp", cwbc, conv_w[l, :].partition_broadcast(128).rearrange("p (k c) -> p k c", k=3), reads=[d_const], writes=[b_cw], sbuf=b_cw)
            P.dma("sp", cbbc, conv_b[l, :].partition_broadcast(128), reads=[d_const], writes=[b_cb], sbuf=b_cb)
            P.dma("sp", snwb, ssd_nw[l, :].partition_broadcast(128), reads=[d_const], writes=[b_snw], sbuf=b_snw)
            P.op("act", lambda e: e.activation(out=Abc, in_=Abc, func=AF.Exp), reads=[b_Abc], writes=[b_Abc])
            P.op("dve", lambda e: e.tensor_scalar_mul(out=Abc, in0=Abc, scalar1=-1.0), reads=[b_Abc], writes=[b_Abc])
            P.op("act", lambda e: e.activation(out=rtab[:, 0, 0:8], in_=ldbc[:, 0:8], func=AF.Exp, scale=pidx[:, 0:1]), reads=[b_ld, b_pidx], writes=[b_rtab])
            P.op("act", lambda e: e.activation(out=rtab[:, 0, 8:16], in_=ldbc[:, 8:16], func=AF.Exp, scale=pidx[:, 2:3]), reads=[b_ld, b_pidx], writes=[b_rtab])
            P.op("dve", lambda e: e.tensor_scalar_mul(out=rtab[:, 0, :], in0=rtab[:, 0, :], scalar1=DKS), reads=[b_rtab], writes=[b_rtab])
            P.op("act", lambda e: e.activation(out=rtab[:, 1, 0:8], in_=ldbc[:, 0:8], func=AF.Exp, scale=pidx[:, 1:2]), reads=[b_ld, b_pidx], writes=[b_rtab])
            P.op("act", lambda e: e.activation(out=rtab[:, 1, 8:16], in_=ldbc[:, 8:16], func=AF.Exp, scale=pidx[:, 3:4]), reads=[b_ld, b_pidx], writes=[b_rtab])
            P.op("act", lambda e: e.activation(out=rtab[:, 2, :], in_=ldbc, func=AF.Exp, scale=float(T)), reads=[b_ld], writes=[b_rtab])
            (dpos, b_dpos), (dneg, b_dneg), (mtmp, b_mtmp), (mtmp2, b_mtmp2) = mt
            P.op("dve", lambda e: e.tensor_scalar_max(out=dpos, in0=diffm, scalar1=0.0), reads=[b_diff], writes=[b_dpos])
            P.op("dve", lambda e: e.tensor_scalar(out=dneg, in0=diffm, scalar1=-1.0, scalar2=0.0, op0=ALU.mult, op1=ALU.max), reads=[b_diff], writes=[b_dneg])
            for h in range(8):
                P.op("act", lambda e, h=h: e.activation(out=mtmp, in_=dpos, func=AF.Exp, scale=ldbc[:, h:h + 1]), reads=[b_dpos, b_ld], writes=[b_mtmp])
                P.op("act", lambda e, h=h: e.activation(out=mtmp2, in_=dneg, func=AF.Exp, scale=ldbc[:, 8 + h:9 + h]), reads=[b_dneg, b_ld], writes=[b_mtmp2])
                P.op("dve", lambda e: e.tensor_tensor(out=mtmp, in0=mtmp, in1=M_le, op=ALU.mult), reads=[b_mtmp, b_cmask], writes=[b_mtmp])
                P.op("dve", lambda e: e.tensor_tensor(out=mtmp2, in0=mtmp2, in1=M_ge, op=ALU.mult), reads=[b_mtmp2, b_cmask], writes=[b_mtmp2])
                P.op("dve", lambda e, h=h: e.scalar_tensor_tensor(out=Mret[:, h, :], in0=mtmp, scalar=DKS, in1=mtmp2, op0=ALU.mult, op1=ALU.add),
                     reads=[b_mtmp, b_mtmp2], writes=[b_Mret])
                P.op("dve", lambda e, h=h: e.scalar_tensor_tensor(out=Mret[:, h, :], in0=mtmp2, scalar=DKS - 1.0, in1=Mret[:, h, :], op0=ALU.mult, op1=ALU.add),
                     reads=[b_mtmp2, b_Mret], writes=[b_Mret])

            Sf = [P.alloc(f"Sf{i}", [1024]) for i in range(2)]
            Sb = [P.alloc(f"Sb{i}", [1024]) for i in range(2)]
            Sbb = [P.alloc(f"Sbb{i}", [1024], BF16) for i in range(2)]
            NB = 1
            qkt = [P.alloc(f"qkt{i}", [2048], BF16) for i in range(NB)]
            vt = [P.alloc(f"vt{i}", [1024], BF16) for i in range(NB)]
            xsh = [[P.alloc(f"xsh{i}_{k}", [1536], BF16) for k in range(3)] for i in range(1)]
            xpo = [P.alloc(f"xpo{i}", [1536], BF16) for i in range(NB)]
            dtb = [P.alloc(f"dtb{i}", [32]) for i in range(2)]
            sgz = [P.alloc(f"sgz{i}", [2048], BF16) for i in range(NB)]
            stfb = [P.alloc(f"stfb{i}", [2048], BF16) for i in range(NB)]
            cstb = [P.alloc(f"cstb{i}", [2048]) for i in range(NB)]
            cdbb = [P.alloc(f"cdbb{i}", [16]) for i in range(NB)]
            cacc, b_cacc = P.alloc("cacc", [1536])
            ct1, b_ct1 = P.alloc("ct1", [1536])
            ct2, b_ct2 = P.alloc("ct2", [1536])
            yw, b_yw = cacc[:, 0:1024], b_cacc
            yt, b_yt = ct1[:, 0:1024], b_ct1
            ysq, b_ysq = ct2[:, 0:1024], b_ct2
            av, b_av = P.alloc("av", [32])
            dec, b_dec = P.alloc("dec", [64])
            wgt, b_wgt = P.alloc("wgt", [32])
            xw = [P.alloc(f"xw{i}", [1024], BF16) for i in range(2)]
            kw = [P.alloc(f"kw{i}", [1024], BF16) for i in range(2)]
            stst = [P.alloc(f"stst{i}", [2048], BF16) for i in range(1)]
            cst_st = [P.alloc(f"cst_st{i}", [2048]) for i in range(1)]
            cdst = [P.alloc(f"cdst{i}", [16]) for i in range(2)]
            qT, b_qT = P.alloc("qT", [8, 128], BF16)
            kT, b_kT = P.alloc("kT", [8, 128], BF16)
            bcT, b_bcT = P.alloc("bcT", [4, 128], BF16)
            AT, b_AT = P.alloc("AT", [8, 128], BF16)
            Xd = [P.alloc(f"Xd{i}", [16, 128], BF16) for i in range(2)]
            ahl, b_ahl = P.alloc("ahl", [2, 32], BF16)
            cmb, b_cmb = P.alloc("cmb", [2, 128], BF16)
            P.op("dve", lambda e: e.tensor_copy(out=cmb[:, 0, :], in_=M_le), reads=[b_cmask], writes=[b_cmb])
            P.op("dve", lambda e: e.tensor_copy(out=cmb[:, 1, :], in_=M_ge), reads=[b_cmask], writes=[b_cmb])
            Ed = [P.alloc(f"Ed{i}", [16, 128], BF16) for i in range(2)]
            Md = Ed
            Gm, b_Gm = P.alloc("Gm", [4, 128], BF16)
            ysm, b_ysm = P.alloc("ysm", [32])
            yst = [P.alloc(f"yst{i}", [2048], BF16) for i in range(1)]

            cnt1 = [0]
            cnt2 = [0]
            def run_pass1(seq):
                (t0, nck, is_s, sidx) = seq
                for ty in range(2):
                    sv, bsv = Sf[ty]
                    if is_s:
                        src = (st_ret if ty == 0 else st_ssd)[l, 0]
                        P.dma("sp", sv, src, reads=[d_const], writes=[bsv], sbuf=bsv)
                    else:
                        P.op("pool", lambda e, sv=sv: e.memset(sv, 0.0), writes=[bsv])
                for c in range(nck):
                    tt = t0 + c
                    i = cnt1[0] % NB
                    cnt1[0] += 1
                    rows = slice(tt * T, (tt + 1) * T)
                    kt_, bkt = qkt[i]
                    vt_, bvt = vt[i]
                    dtt, bdtt = dtb[c % 2]
                    P.dma("sp", kt_[:, 1024:2048], qk_d[rows, 1024:2048], reads=[B_qk[tt]], writes=[bkt], sbuf=bkt)
                    P.dma("sp", vt_, v_d[rows, :], reads=[B_v[tt]], writes=[bvt], sbuf=bvt)
                    P.dma("sp", dtt, dt_d[rows, :], reads=[B_dt[tt]], writes=[bdtt], sbuf=bdtt)
                    r = xrow(sidx, c * T)
                    for k3 in range(3):
                        xv, bxv = xsh[0][k3]
                        P.dma("sp", xv, xpre_d[r - 1 + k3:r - 1 + k3 + T, :], reads=[B_xpre], writes=[bxv], sbuf=bxv)
                    P.op("dve", lambda e, xv=xsh[0][1][0]: e.tensor_tensor(out=cacc, in0=xv, in1=cwbc[:, 1, :], op=ALU.mult),
                         reads=[xsh[0][1][1], b_cw], writes=[b_cacc])
                    P.op("pool", lambda e, xv=xsh[0][0][0]: e.tensor_tensor(out=ct1, in0=xv, in1=cwbc[:, 0, :], op=ALU.mult),
                         reads=[xsh[0][0][1], b_cw], writes=[b_ct1])
                    P.op("pool", lambda e, xv=xsh[0][2][0]: e.tensor_tensor(out=ct2, in0=xv, in1=cwbc[:, 2, :], op=ALU.mult),
                         reads=[xsh[0][2][1], b_cw], writes=[b_ct2])
                    P.op("pool", lambda e: e.tensor_tensor(out=ct1, in0=ct1, in1=cbbc, op=ALU.add), reads=[b_ct1, b_cb], writes=[b_ct1])
                    P.op("dve", lambda e: e.tensor_tensor(out=cacc, in0=cacc, in1=ct2, op=ALU.add), reads=[b_cacc, b_ct2], writes=[b_cacc])
                    P.op("dve", lambda e: e.tensor_tensor(out=cacc, in0=cacc, in1=ct1, op=ALU.add), reads=[b_cacc, b_ct1], writes=[b_cacc])
                    xp_, bxp = xpo[i]
                    P.op("act", lambda e, xp_=xp_: e.activation(out=xp_, in_=cacc, func=AF.Silu), reads=[b_cacc], writes=[bxp])
                    P.dma("pool", xpost_d[rows, :], xp_, reads=[bxp], writes=[B_xpost[tt]], sbuf=bxp)
                    xs3 = xp_[:, 0:1024].rearrange("p (h c) -> p h c", h=16)
                    Btok = xp_[:, 1024:1280].rearrange("p (g n) -> p g n", g=2)
                    P.op("dve", lambda e, dtt=dtt: e.tensor_tensor(out=av, in0=dtt, in1=Abc, op=ALU.mult), reads=[bdtt, b_Abc], writes=[b_av])
                    bk = mm_bank()
                    pc = P.ps(bk)
                    P.op("pe", lambda e, pc=pc: e.matmul(pc[:, 0:16], lhsT=M_gt, rhs=av[:, 0:16], start=True, stop=True), reads=[b_cmask, b_av], writes=[P.pbuf[bk]], signal=False)
                    P.op("pe", lambda e, pc=pc: e.matmul(pc[:, 16:32], lhsT=M_lt, rhs=av[:, 16:32], start=True, stop=True), reads=[b_cmask, b_av], writes=[P.pbuf[bk]], signal=False)
                    P.op("pe", lambda e, pc=pc: e.matmul(pc[:, 32:64], lhsT=M_one, rhs=av[:, 0:32], start=True, stop=True), reads=[b_cmask, b_av], writes=[P.pbuf[bk]])
                    P.op("act", lambda e, pc=pc: e.activation(out=dec, in_=pc[:, 0:64], func=AF.Exp), reads=[P.pbuf[bk]], writes=[b_dec])
                    P.op("dve", lambda e, dtt=dtt: e.tensor_tensor(out=wgt, in0=dtt, in1=dec[:, 0:32], op=ALU.mult), reads=[bdtt, b_dec], writes=[b_wgt])
                    k3v = kt_[:, 1024:2048].rearrange("p (h n) -> p h n", h=8)
                    v3v = vt_.rearrange("p (h c) -> p h c", h=8)
                    for d in range(2):
                        xw_, bxw = xw[d]
                        kw_, bkw = kw[d]
                        P.op("dve", lambda e, d=d, xw_=xw_: e.tensor_tensor(out=xw_.rearrange("p (h c) -> p h c", h=16), in0=xs3,
                                                                          in1=bc_last(wgt[:, d * 16:(d + 1) * 16], 64), op=ALU.mult),
                             reads=[bxp, b_wgt], writes=[bxw])
                        P.op("pool", lambda e, d=d, kw_=kw_: e.tensor_tensor(out=kw_.rearrange("p (h n) -> p h n", h=8), in0=k3v,
                                                                           in1=bc_last(rtab[:, 1, d * 8:(d + 1) * 8], 128), op=ALU.mult),
                             reads=[bkt, b_rtab], writes=[bkw])
                    def chunk_state_mms(d):
                        res = {}
                        for ty in range(2):
                            b0, b1 = mm_bank(), mm_bank()
                            res[ty] = (b0, b1)
                            if ty == 0:
                                kw3 = kw[d][0].rearrange("p (h n) -> p h n", h=8)
                                for h in range(8):
                                    bnk = (b0, b1)[h // 4]
                                    P.op("pe", lambda e, h=h, bnk=bnk, kw3=kw3: e.matmul(P.ps(bnk)[:, (h % 4) * 128:(h % 4 + 1) * 128], lhsT=kw3[:, h, :],
                                                                                       rhs=v3v[:, h, :], start=True, stop=True),
                                         reads=[kw[d][1], bvt], writes=[P.pbuf[bnk]], signal=(h % 4 == 3))
                            else:
                                for g in range(2):
                                    bnk = (b0, b1)[g]
                                    P.op("pe", lambda e, g=g, bnk=bnk, d=d: e.matmul(P.ps(bnk), lhsT=Btok[:, g, :], rhs=xw[d][0][:, g * 512:(g + 1) * 512],
                                                                                    start=True, stop=True),
                                         reads=[bxp, xw[d][1]], writes=[P.pbuf[bnk]])
                        return res
                    sst, bsst = stst[0]
                    for ty in range(2):
                        sv, bsv = Sf[ty]
                        P.op("act", lambda e, ty=ty, sv=sv, sst=sst: e.copy(out=sst[:, ty * 1024:(ty + 1) * 1024], in_=sv), reads=[bsv], writes=[bsst])
                    P.dma("pool", stf_d[tt], sst, reads=[bsst], writes=[B_stf[tt]], sbuf=bsst)
                    psf = chunk_state_mms(0)
                    for ty in range(2):
                        sv, bsv = Sf[ty]
                        b0, b1 = psf[ty]
                        if ty == 0:
                            P.op("dve", lambda e, sv=sv: e.tensor_tensor(out=sv.rearrange("p (h c) -> p h c", h=8), in0=sv.rearrange("p (h c) -> p h c", h=8),
                                                                        in1=bc_last(rtab[:, 2, 0:8], 128), op=ALU.mult), reads=[bsv, b_rtab], writes=[bsv])
                        else:
                            P.op("dve", lambda e, sv=sv: e.tensor_tensor(out=sv.rearrange("p (h c) -> p h c", h=16), in0=sv.rearrange("p (h c) -> p h c", h=16),
                                                                        in1=bc_last(dec[:, 32:48], 64), op=ALU.mult), reads=[bsv, b_dec], writes=[bsv])
                        for hb_, bnk in enumerate((b0, b1)):
                            P.op("dve", lambda e, sv=sv, hb_=hb_, bnk=bnk: e.tensor_tensor(out=sv[:, hb_ * 512:(hb_ + 1) * 512], in0=sv[:, hb_ * 512:(hb_ + 1) * 512],
                                                                                       in1=P.ps(bnk), op=ALU.add), reads=[bsv, P.pbuf[bnk]], writes=[bsv])
                    psbk = chunk_state_mms(1)
                    cs_, bcs = cst_st[0]
                    for ty in range(2):
                        b0, b1 = psbk[ty]
                        for hb_, bnk in enumerate((b0, b1)):
                            P.op("act", lambda e, ty=ty, hb_=hb_, bnk=bnk, cs_=cs_: e.copy(out=cs_[:, ty * 1024 + hb_ * 512: ty * 1024 + (hb_ + 1) * 512], in_=P.ps(bnk)),
                                 reads=[P.pbuf[bnk]], writes=[bcs])
                    P.dma("pool", cstb_d[tt], cs_, reads=[bcs], writes=[B_cstb[tt]], sbuf=bcs)
                    cd_, bcd = cdst[c % 2]
                    P.op("act", lambda e, cd_=cd_: e.copy(out=cd_, in_=dec[:, 48:64]), reads=[b_dec], writes=[bcd])
                    P.dma("pool", cdb_d[tt], cd_, reads=[bcd], writes=[B_cdb[tt]], sbuf=bcd)
                if not is_s:
                    for ty in range(2):
                        sv, bsv = Sf[ty]
                        dst = (ns_ret if ty == 0 else ns_ssd)[sidx, l, 0]
                        P.dma("pool", dst, sv, reads=[bsv], writes=[B_out], sbuf=bsv)

                if is_s and EXCH:
                    for ty in range(2):
                        sv, bsv = Sf[ty]
                        P.dma("sp", exst_in[:, ty * 1024:(ty + 1) * 1024], sv, reads=[bsv], writes=[B_exin], sbuf=bsv)
                    P.collective("AllGather", PAIRS, exst_in, exst_out, reads=[B_exin], writes=[B_exout], inc=1)

            def run_pass2(seq):
                (t0, nck, is_s, sidx) = seq
                for ty in range(2):
                    sv, bsv = Sb[ty]
                    if is_s and EXCH:
                        ev, bev = cstb[0]
                        od, bod = cst_st[0]
                        P.dma("sp", ev[:, 0:1024], exst_out[0:128, ty * 1024:(ty + 1) * 1024], reads=[B_exout], writes=[bev], sbuf=bev)
                        P.dma("sp", od[:, 0:1024], exst_out[128:256, ty * 1024:(ty + 1) * 1024], reads=[B_exout], writes=[bod], sbuf=bod)
                        P.op("dve", lambda e, sv=sv, ev=ev: e.tensor_scalar(out=sv, in0=ev[:, 0:1024], scalar1=psel[:, 0:1], scalar2=None, op0=ALU.mult),
                             reads=[bev, b_psel], writes=[bsv])
                        P.op("dve", lambda e, sv=sv, od=od: e.scalar_tensor_tensor(out=sv, in0=od[:, 0:1024], scalar=psel[:, 1:2], in1=sv, op0=ALU.mult, op1=ALU.add),
                             reads=[bod, b_psel, bsv], writes=[bsv])
                    elif is_s:
                        src = (st_ret if ty == 0 else st_ssd)[l, 1]
                        P.dma("sp", sv, src, reads=[d_const], writes=[bsv], sbuf=bsv)
                    else:
                        P.op("pool", lambda e, sv=sv: e.memset(sv, 0.0), writes=[bsv])
                for ty in range(2):
                    P.op("act", lambda e, ty=ty: e.copy(out=Sbb[ty][0], in_=Sb[ty][0]), reads=[Sb[ty][1]], writes=[Sbb[ty][1]])
                for c in range(nck - 1, -1, -1):
                    tt = t0 + c
                    i = cnt2[0] % NB
                    cnt2[0] += 1
                    rows = slice(tt * T, (tt + 1) * T)
                    qk_, bqk = qkt[i]
                    vt_, bvt = vt[i]
                    dtt, bdtt = dtb[c % 2]
                    xp_, bxp = xpo[i]
                    sgz_, bsgz = sgz[i]
                    stf_, bstf = stfb[i]
                    csb_, bcsb = cstb[i]
                    cdb_, bcdb = cdbb[i]
                    P.dma("sp", qk_, qk_d[rows, :], reads=[B_qk[tt]], writes=[bqk], sbuf=bqk)
                    P.dma("sp", vt_, v_d[rows, :], reads=[B_v[tt]], writes=[bvt], sbuf=bvt)
                    P.dma("sp", dtt, dt_d[rows, :], reads=[B_dt[tt]], writes=[bdtt], sbuf=bdtt)
                    P.dma("sp", xp_, xpost_d[rows, :], reads=[B_xpost[tt]], writes=[bxp], sbuf=bxp)
                    P.dma("sp", sgz_[:, 0:1024], sg_d[rows, :], reads=[B_sg[tt]], writes=[bsgz], sbuf=bsgz)
                    P.dma("sp", sgz_[:, 1024:2048], sz_d[rows, :], reads=[B_sz[tt]], writes=[bsgz], sbuf=bsgz)
                    P.dma("sp", stf_, stf_d[tt], reads=[B_stf[tt]], writes=[bstf], sbuf=bstf)
                    P.dma("sp", csb_, cstb_d[tt], reads=[B_cstb[tt]], writes=[bcsb], sbuf=bcsb)
                    P.dma("sp", cdb_, cdb_d[tt], reads=[B_cdb[tt]], writes=[bcdb], sbuf=bcdb)
                    v3v = vt_.rearrange("p (h c) -> p h c", h=8)
                    xs3 = xp_[:, 0:1024].rearrange("p (h c) -> p h c", h=16)
                    transposes_to(qk_[:, 0:1024], bqk, 8, lambda i0, n: qT[:, i0:i0 + n, :], b_qT)
                    transposes_to(qk_[:, 1024:2048], bqk, 8, lambda i0, n: kT[:, i0:i0 + n, :], b_kT)
                    transposes_to(xp_[:, 1024:1536], bxp, 4, lambda i0, n: bcT[:, i0:i0 + n, :], b_bcT)
                    pb = (mm_bank(), mm_bank())
                    for h in range(8):
                        bnk = pb[h // 4]
                        P.op("pe", lambda e, h=h, bnk=bnk: e.matmul(P.ps(bnk)[:, (h % 4) * 128:(h % 4 + 1) * 128], lhsT=kT[:, h, :], rhs=qT[:, h, :],
                                                                  start=True, stop=True), reads=[b_kT, b_qT], writes=[P.pbuf[bnk]], signal=(h % 4 == 3))
                    for hb_ in range(2):
                        P.op("dve", lambda e, hb_=hb_: e.tensor_tensor(out=AT[:, hb_ * 4:(hb_ + 1) * 4, :], in0=P.ps(pb[hb_]).rearrange("p (h n) -> p h n", h=4),
                                                                     in1=Mret[:, hb_ * 4:(hb_ + 1) * 4, :], op=ALU.mult),
                             reads=[P.pbuf[pb[hb_]], b_Mret], writes=[b_AT])
                    pin = (mm_bank(), mm_bank())
                    pof = (mm_bank(), mm_bank())
                    for h in range(8):
                        bnk = pin[h // 4]
                        P.op("pe", lambda e, h=h, bnk=bnk: e.matmul(P.ps(bnk)[:, (h % 4) * 128:(h % 4 + 1) * 128], lhsT=AT[:, h, :], rhs=v3v[:, h, :],
                                                                  start=True, stop=True), reads=[b_AT, bvt], writes=[P.pbuf[bnk]], signal=(h % 4 == 3))
                    for h in range(8):
                        bnk = pof[h // 4]
                        P.op("pe", lambda e, h=h, bnk=bnk, stf_=stf_: e.matmul(P.ps(bnk)[:, (h % 4) * 128:(h % 4 + 1) * 128], lhsT=qT[:, h, :],
                                                                             rhs=stf_[:, h * 128:(h + 1) * 128], start=True, stop=True),
                             reads=[b_qT, bstf], writes=[P.pbuf[bnk]], signal=(h % 4 == 3))
                    yw3 = yw.rearrange("p (h c) -> p h c", h=8)
                    yt3 = yt.rearrange("p (h c) -> p h c", h=8)
                    for hb_ in range(2):
                        sl = slice(hb_ * 4, (hb_ + 1) * 4)
                        P.op("dve", lambda e, hb_=hb_, sl=sl: e.tensor_tensor(out=yt3[:, sl, :], in0=P.ps(pof[hb_]).rearrange("p (h n) -> p h n", h=4),
                                                                            in1=bc_last(rtab[:, 0, sl], 128), op=ALU.mult),
                             reads=[P.pbuf[pof[hb_]], b_rtab], writes=[b_yt])
                        P.op("dve", lambda e, hb_=hb_, sl=sl: e.tensor_tensor(out=yw3[:, sl, :], in0=P.ps(pin[hb_]).rearrange("p (h n) -> p h n", h=4),
                                                                            in1=yt3[:, sl, :], op=ALU.add),
                             reads=[P.pbuf[pin[hb_]], b_yt], writes=[b_yw])
                    pob = (mm_bank(), mm_bank())
                    for h in range(8):
                        bnk = pob[h // 4]
                        P.op("pe", lambda e, h=h, bnk=bnk: e.matmul(P.ps(bnk)[:, (h % 4) * 128:(h % 4 + 1) * 128], lhsT=qT[:, h, :],
                                                                  rhs=Sbb[0][0][:, h * 128:(h + 1) * 128], start=True, stop=True),
                             reads=[b_qT, Sbb[0][1]], writes=[P.pbuf[bnk]], signal=(h % 4 == 3))
                    for hb_ in range(2):
                        sl = slice(hb_ * 4, (hb_ + 1) * 4)
                        P.op("dve", lambda e, hb_=hb_, sl=sl: e.tensor_tensor(out=yt3[:, sl, :], in0=P.ps(pob[hb_]).rearrange("p (h n) -> p h n", h=4),
                                                                            in1=bc_last(rtab[:, 0, 8 + hb_ * 4:8 + (hb_ + 1) * 4], 128), op=ALU.mult),
                             reads=[P.pbuf[pob[hb_]], b_rtab], writes=[b_yt])
                    P.op("pool", lambda e: e.tensor_tensor(out=yw, in0=yw, in1=yt, op=ALU.add), reads=[b_yw, b_yt], writes=[b_yw])
                    P.op("act", lambda e: e.activation(out=ysq, in_=yw, func=AF.Square), reads=[b_yw], writes=[b_ysq])
                    P.op("dve", lambda e: e.tensor_reduce(out=ysm[:, 0:8], in_=ysq.rearrange("p (h c) -> p h c", h=8), axis=AX.X, op=ALU.add),
                         reads=[b_ysq], writes=[b_ysm])
                    rstd_of(ysm[:, 0:8], 128, 8, ysm[:, 8:16], ysm[:, 16:24], b_ysm, b_ysm, b_ysm)
                    yo_, byo = yst[0]
                    P.op("dve", lambda e: e.tensor_tensor(out=yw3, in0=yw3, in1=bc_last(ysm[:, 16:24], 128), op=ALU.mult), reads=[b_yw, b_ysm], writes=[b_yw])
                    P.op("pool", lambda e, yo_=yo_, sgz_=sgz_: e.tensor_tensor(out=yo_[:, 0:1024], in0=yw, in1=sgz_[:, 0:1024], op=ALU.mult),
                         reads=[b_yw, bsgz], writes=[byo])
                    P.op("dve", lambda e, dtt=dtt: e.tensor_tensor(out=av, in0=dtt, in1=Abc, op=ALU.mult), reads=[bdtt, b_Abc], writes=[b_av])
                    bk = mm_bank()
                    pc = P.ps(bk)
                    P.op("pe", lambda e, pc=pc: e.matmul(pc[:, 0:16], lhsT=M_le, rhs=av[:, 0:16], start=True, stop=True), reads=[b_cmask, b_av], writes=[P.pbuf[bk]], signal=False)
                    P.op("pe", lambda e, pc=pc: e.matmul(pc[:, 16:32], lhsT=M_ge, rhs=av[:, 16:32], start=True, stop=True), reads=[b_cmask, b_av], writes=[P.pbuf[bk]])
                    P.op("act", lambda e, pc=pc: e.activation(out=dec[:, 0:32], in_=pc[:, 0:32], func=AF.Exp), reads=[P.pbuf[bk]], writes=[b_dec])
                    for d in range(2):
                        xw_, bxw = xw[d]
                        P.op("dve", lambda e, d=d, xw_=xw_, dtt=dtt: e.tensor_tensor(out=xw_.rearrange("p (h c) -> p h c", h=16), in0=xs3,
                                                                                   in1=bc_last(dtt[:, d * 16:(d + 1) * 16], 64), op=ALU.mult),
                             reads=[bxp, bdtt], writes=[bxw])
                    bg = mm_bank()
                    for g in range(2):
                        P.op("pe", lambda e, g=g: e.matmul(P.ps(bg)[:, g * 128:(g + 1) * 128], lhsT=bcT[:, g, :], rhs=bcT[:, 2 + g, :], start=True, stop=True),
                             reads=[b_bcT], writes=[P.pbuf[bg]], signal=(g == 1))
                    P.op("dve", lambda e: e.tensor_tensor(out=Gm[:, 0:2, :], in0=P.ps(bg)[:, 0:256].rearrange("p (g n) -> p g n", g=2),
                                                          in1=bc_mid(M_le, 2), op=ALU.mult), reads=[P.pbuf[bg], b_cmask], writes=[b_Gm])
                    P.op("dve", lambda e: e.tensor_tensor(out=Gm[:, 2:4, :], in0=P.ps(bg)[:, 0:256].rearrange("p (g n) -> p g n", g=2),
                                                          in1=bc_mid(M_ge, 2), op=ALU.mult), reads=[P.pbuf[bg], b_cmask], writes=[b_Gm])
                    P.op("dve", lambda e: e.tensor_copy(out=ahl[:, 0, :], in_=av), reads=[b_av], writes=[b_ahl])
                    P.op("dve", lambda e: e.tensor_tensor(out=ahl[:, 1, :], in0=av, in1=ahl[:, 0, :], op=ALU.subtract), reads=[b_av, b_ahl], writes=[b_ahl])
                    for d in range(2):
                        Ed_, bEd = Ed[d]
                        Md_, bMd = Md[d]
                        um = cmb[:, d, :]
                        msk = M_gt if d == 0 else M_lt
                        for hl in range(2):
                            Xq, bXq = Xd[hl]
                            P.op("pool" if hl == 0 else "dve", lambda e, d=d, hl=hl, Xq=Xq, msk=msk: e.tensor_tensor(
                                out=Xq, in0=bc_last(ahl[:, hl, d * 16:(d + 1) * 16], 128), in1=bc_mid(msk, 16), op=ALU.mult),
                                reads=[b_ahl, b_cmask], writes=[bXq])
                        for q4 in range(4):
                            bnk = mm_bank()
                            for jj in range(4):
                                j = q4 * 4 + jj
                                for hl in range(2):
                                    P.op("pe", lambda e, j=j, jj=jj, bnk=bnk, hl=hl, um=um: e.matmul(P.ps(bnk)[:, jj * 128:(jj + 1) * 128], lhsT=Xd[hl][0][:, j, :], rhs=um,
                                                                                                   start=(hl == 0), stop=(hl == 1)),
                                         reads=[Xd[hl][1], b_cmb], writes=[P.pbuf[bnk]], signal=(jj == 3 and hl == 1))
                            P.op("act", lambda e, q4=q4, bnk=bnk, Ed_=Ed_: e.activation(out=Ed_[:, q4 * 4:(q4 + 1) * 4, :],
                                                                                       in_=P.ps(bnk).rearrange("p (h n) -> p h n", h=4), func=AF.Exp),
                                 reads=[P.pbuf[bnk]], writes=[bEd])
                        for g in range(2):
                            P.op("dve", lambda e, g=g, d=d, Ed_=Ed_, Md_=Md_: e.tensor_tensor(out=Md_[:, g * 8:(g + 1) * 8, :], in0=Ed_[:, g * 8:(g + 1) * 8, :],
                                                                                           in1=bc_mid(Gm[:, d * 2 + g, :], 8), op=ALU.mult),
                                 reads=[bEd, b_Gm], writes=[bMd])
                    pin = (mm_bank(), mm_bank())
                    for j in range(16):
                        bnk = pin[j // 8]
                        o = P.ps(bnk)[:, (j % 8) * 64:(j % 8 + 1) * 64]
                        P.op("pe", lambda e, j=j, o=o: e.matmul(o, lhsT=Md[0][0][:, j, :], rhs=xw[0][0][:, j * 64:(j + 1) * 64], start=True, stop=False),
                             reads=[Md[0][1], xw[0][1]], writes=[P.pbuf[bnk]], signal=False)
                        P.op("pe", lambda e, j=j, o=o: e.matmul(o, lhsT=Md[1][0][:, j, :], rhs=xw[1][0][:, j * 64:(j + 1) * 64], start=False, stop=True),
                             reads=[Md[1][1], xw[1][1]], writes=[P.pbuf[bnk]], signal=(j % 8 == 7))
                    yw16 = yw.rearrange("p (h c) -> p h c", h=16)
                    yt16 = yt.rearrange("p (h c) -> p h c", h=16)
                    for d in range(2):
                        po = (mm_bank(), mm_bank())
                        srcS = stf_[:, 1024:2048] if d == 0 else Sbb[1][0]
                        bsrc = bstf if d == 0 else Sbb[1][1]
                        for g in range(2):
                            P.op("pe", lambda e, g=g, po=po, srcS=srcS: e.matmul(P.ps(po[g]), lhsT=bcT[:, 2 + g, :], rhs=srcS[:, g * 512:(g + 1) * 512], start=True, stop=True),
                                 reads=[b_bcT, bsrc], writes=[P.pbuf[po[g]]])
                        for g in range(2):
                            sl = slice(g * 8, (g + 1) * 8)
                            P.op("dve", lambda e, g=g, d=d, sl=sl, po=po: e.tensor_tensor(out=yt16[:, sl, :], in0=P.ps(po[g]).rearrange("p (h c) -> p h c", h=8),
                                                                                        in1=bc_last(dec[:, d * 16 + g * 8:d * 16 + (g + 1) * 8], 64), op=ALU.mult),
                                 reads=[P.pbuf[po[g]], b_dec], writes=[b_yt])
                            if d == 0:
                                P.op("dve", lambda e, g=g, sl=sl: e.tensor_tensor(out=yw16[:, sl, :], in0=P.ps(pin[g]).rearrange("p (h c) -> p h c", h=8),
                                                                                in1=yt16[:, sl, :], op=ALU.add),
                                     reads=[P.pbuf[pin[g]], b_yt], writes=[b_yw])
                        if d == 1:
                            P.op("pool", lambda e: e.tensor_tensor(out=yw, in0=yw, in1=yt, op=ALU.add), reads=[b_yw, b_yt], writes=[b_yw])
                    P.op("dve", lambda e: e.tensor_tensor(out=yt16, in0=xs3, in1=bc_last(dskb, 64), op=ALU.mult), reads=[bxp, b_dsk], writes=[b_yt])
                    P.op("pool", lambda e: e.tensor_tensor(out=yw, in0=yw, in1=yt, op=ALU.add), reads=[b_yw, b_yt], writes=[b_yw])
                    P.op("dve", lambda e, sgz_=sgz_: e.tensor_tensor(out=yw, in0=yw, in1=sgz_[:, 1024:2048], op=ALU.mult), reads=[b_yw, bsgz], writes=[b_yw])
                    P.op("dve", lambda e: e.memset(ysm[:, 24:25], 0.0), writes=[b_ysm])
                    P.op("act", lambda e: e.activation(out=ysq, in_=yw, func=AF.Square, accum_out=ysm[:, 24:25]), reads=[b_yw], writes=[b_ysq, b_ysm])
                    rstd_of(ysm[:, 24:25], 1024, 1, ysm[:, 25:26], ysm[:, 26:27], b_ysm, b_ysm, b_ysm)
                    P.op("dve", lambda e, yo_=yo_: e.scalar_tensor_tensor(out=yo_[:, 1024:2048], in0=yw, scalar=ysm[:, 26:27], in1=snwb, op0=ALU.mult, op1=ALU.mult),
                         reads=[b_yw, b_ysm, b_snw], writes=[byo])
                    P.dma("pool", y_d[rows, :], yo_, reads=[byo], writes=[B_y[tt]], sbuf=byo)
                    for ty in range(2):
                        sv, bsv = Sb[ty]
                        if ty == 0:
                            P.op("dve", lambda e, sv=sv: e.tensor_tensor(out=sv.rearrange("p (h c) -> p h c", h=8), in0=sv.rearrange("p (h c) -> p h c", h=8),
                                                                        in1=bc_last(rtab[:, 2, 8:16], 128), op=ALU.mult), reads=[bsv, b_rtab], writes=[bsv])
                        else:
                            P.op("dve", lambda e, sv=sv, cdb_=cdb_: e.tensor_tensor(out=sv.rearrange("p (h c) -> p h c", h=16), in0=sv.rearrange("p (h c) -> p h c", h=16),
                                                                                   in1=bc_last(cdb_, 64), op=ALU.mult), reads=[bsv, bcdb], writes=[bsv])
                        P.op("pool", lambda e, sv=sv, ty=ty, csb_=csb_: e.tensor_tensor(out=sv, in0=sv, in1=csb_[:, ty * 1024:(ty + 1) * 1024], op=ALU.add),
                             reads=[bsv, bcsb], writes=[bsv])
                        P.op("act", lambda e, ty=ty, sv=sv: e.copy(out=Sbb[ty][0], in_=sv), reads=[bsv], writes=[Sbb[ty][1]])
                if not is_s:
                    for ty in range(2):
                        sv, bsv = Sb[ty]
                        dst = (ns_ret if ty == 0 else ns_ssd)[sidx, l, 1]
                        P.dma("pool", dst, sv, reads=[bsv], writes=[B_out], sbuf=bsv)

            if EXCH:
                rl = xrow(NSEQ - 1, CS * T - 1)
                P.dma("sp", exrow_in, xpre_d[rl:rl + 1, :], reads=[B_xpre], writes=[B_exrin], sem_key=("x", "exrow"))
                P.collective("AllGather", PAIRS, exrow_in, exrow_out, reads=[B_exrin], writes=[B_exrout], inc=1)
                r0_, b0_ = xsh[0][0]
                r1_, b1_ = xsh[0][1]
                r2_, b2_ = xsh[0][2]
                P.dma("sp", r0_[0:1, :], exrow_out[0:1, :], reads=[B_exrout], writes=[b0_], sbuf=b0_)
                P.dma("sp", r1_[0:1, :], exrow_out[1:2, :], reads=[B_exrout], writes=[b1_], sbuf=b1_)
                P.op("dve", lambda e: e.tensor_scalar(out=r2_[0:1, :], in0=r0_[0:1, :], scalar1=psel[0:1, 0:1], scalar2=None, op0=ALU.mult),
                     reads=[b0_, b_psel], writes=[b2_])
                P.op("dve", lambda e: e.scalar_tensor_tensor(out=r2_[0:1, :], in0=r1_[0:1, :], scalar=psel[0:1, 1:2], in1=r2_[0:1, :], op0=ALU.mult, op1=ALU.add),
                     reads=[b1_, b_psel, b2_], writes=[b2_])
                P.dma("sp", xpre_d[rl + 1:rl + 2, :], r2_[0:1, :], reads=[b2_], writes=[B_xpre], sbuf=b2_)
                run_pass1(seqs[-1])
                for sq in seqs[:-1]:
                    run_pass1(sq)
                    run_pass2(sq)
                run_pass2(seqs[-1])
            else:
                for sq in seqs:
                    run_pass1(sq)
                    run_pass2(sq)
            P.barrier()
            P.new_phase()
            P.top = mB

            mC = P.top
            hT, b_hT = P.alloc("hTc", [16, 512], BF16)
            uT, b_uT = P.alloc("uT", [64, 512], BF16)
            xg, b_xg0 = P.alloc("xg", [4, D])
            b_xg = [Buf(f"xg{j}") for j in range(4)]
            modc = [P.alloc(f"modc{i}", [D], BF16) for i in range(4)]
            modl, b_modl = P.alloc("modl", [D])
            yb = [P.alloc(f"yb{i}", [D], BF16) for i in range(1)]
            h2b, b_h2b = P.alloc("h2b", [D], BF16)
            tmpc = [P.alloc(f"tmpc{i}", [512]) for i in range(2)]
            junk, b_junk = h2b, b_h2b
            small, b_small = P.alloc("smallC", [8])
            if last:
                fnb, b_fnb = P.alloc("fnb", [D], BF16)
                P.dma("sp", modl, fnw[0, :].partition_broadcast(128), reads=[d_const], writes=[b_modl], sbuf=b_modl)
                P.op("dve", lambda e: e.tensor_copy(out=fnb, in_=modl), reads=[b_modl], writes=[b_fnb])
            cur_ci = -1
            tn = 0
            for (tiles, ci) in groups:
                G = len(tiles) * T
                if ci != cur_ci:
                    cur_ci = ci
                    for k, slot in enumerate((2, 4, 3, 5)):
                        P.dma("sp", modl, modrow(l, ci, slot).partition_broadcast(128), reads=[B_mod], writes=[b_modl], sbuf=b_modl)
                        P.op("dve", lambda e, k=k: e.tensor_copy(out=modc[k][0], in_=modl), reads=[b_modl], writes=[modc[k][1]])
                (g1, bg1), (a2, ba2), (sh2, bsh2), (g2, bg2) = modc
                for j, tt in enumerate(tiles):
                    yv, byv = yb[0]
                    P.dma("sp", yv, y_d[tt * T:(tt + 1) * T, :], reads=[B_y[tt]], writes=[byv], sbuf=byv)
                    P.dma("sp", xg[:, j, :], x_src[tt * T:(tt + 1) * T, :], reads=[B_xsrc(tt)], writes=[b_xg[j]], sbuf=b_xg[j])
                    transposes_to(yv, byv, 16, lambda i, n, j=j: hT[:, i:i + n, j * T:(j + 1) * T], b_hT)
                for cb in range(4):
                    wt, bw = next_w()
                    for j, tt in enumerate(tiles):
                        bank = mm_bank()
                        for kc in range(16):
                            P.op("pe", lambda e, kc=kc, j=j, wt=wt, bank=bank: e.matmul(P.ps(bank), lhsT=hT[:, kc, j * T:(j + 1) * T], rhs=wt[:, kc, :],
                                                                                       start=(kc == 0), stop=(kc == 15)),
                                 reads=[b_hT, bw], writes=[P.pbuf[bank]], signal=(kc == 15))
                        tm, btm = tmpc[tn % 2]
                        tn += 1
                        P.op("dve", lambda e, tm=tm, bank=bank, cb=cb: e.tensor_tensor(out=tm, in0=P.ps(bank), in1=g1[:, cb * 512:(cb + 1) * 512], op=ALU.mult),
                             reads=[P.pbuf[bank], bg1], writes=[btm])
                        P.op("pool", lambda e, tm=tm, j=j, cb=cb: e.tensor_tensor(out=xg[:, j, cb * 512:(cb + 1) * 512], in0=xg[:, j, cb * 512:(cb + 1) * 512], in1=tm, op=ALU.add),
                             reads=[btm, b_xg[j]], writes=[b_xg[j]])
                for j, tt in enumerate(tiles):
                    xj = xg[:, j, :]
                    P.op("dve", lambda e: e.memset(small[:, 0:1], 0.0), writes=[b_small])
                    P.op("act", lambda e, xj=xj: e.activation(out=junk, in_=xj, func=AF.Square, accum_out=small[:, 0:1]), reads=[b_xg[j]], writes=[b_junk, b_small])
                    rstd_of(small[:, 0:1], D, 1, small[:, 1:2], small[:, 2:3], b_small, b_small, b_small)
                    P.op("dve", lambda e, xj=xj: e.scalar_tensor_tensor(out=modl, in0=xj, scalar=small[:, 2:3], in1=a2, op0=ALU.mult, op1=ALU.mult),
                         reads=[b_xg[j], b_small, ba2], writes=[b_modl])
                    P.op("pool", lambda e: e.tensor_tensor(out=h2b, in0=modl, in1=sh2, op=ALU.add), reads=[b_modl, bsh2], writes=[b_h2b])
                    transposes_to(h2b, b_h2b, 16, lambda i, n, j=j: hT[:, i:i + n, j * T:(j + 1) * T], b_hT)
                for fb in range(16):
                    if fb % 2 == 0:
                        pump_convert(2)
                    wt, bw = next_w()
                    for fc in range(4):
                        bank = mm_bank()
                        for kc in range(16):
                            P.op("pe", lambda e, kc=kc, fc=fc, wt=wt, bank=bank, G=G: e.matmul(P.ps(bank)[:, 0:G], lhsT=wt[:, kc, fc * 128:(fc + 1) * 128], rhs=hT[:, kc, 0:G],
                                                                                              start=(kc == 0), stop=(kc == 15)),
                                 reads=[b_hT, bw], writes=[P.pbuf[bank]], signal=(kc == 15))
                        P.op("act" if fc % 2 == 0 else "dve",
                             (lambda e, bank=bank, fb=fb, fc=fc, G=G: e.activation(out=uT[:, fb * 4 + fc, 0:G], in_=P.ps(bank)[:, 0:G], func=AF.Relu)) if fc % 2 == 0 else
                             (lambda e, bank=bank, fb=fb, fc=fc, G=G: e.tensor_scalar_max(out=uT[:, fb * 4 + fc, 0:G], in0=P.ps(bank)[:, 0:G], scalar1=0.0)),
                             reads=[P.pbuf[bank]], writes=[b_uT])
                        P.op("pool", lambda e, fb=fb, fc=fc, G=G: e.tensor_tensor(out=uT[:, fb * 4 + fc, 0:G], in0=uT[:, fb * 4 + fc, 0:G], in1=uT[:, fb * 4 + fc, 0:G], op=ALU.mult),
                             reads=[b_uT], writes=[b_uT])
                for cb in range(4):
                    banks = [2 + ((cb * 4 + j) % 6) for j in range(len(tiles))]
                    for kp in range(4):
                        wt, bw = next_w()
                        for j, tt in enumerate(tiles):
                            bank = banks[j]
                            for kc in range(16):
                                P.op("pe", lambda e, kc=kc, j=j, wt=wt, bank=bank, kp=kp: e.matmul(P.ps(bank), lhsT=uT[:, kp * 16 + kc, j * T:(j + 1) * T], rhs=wt[:, kc, :],
                                                                                                 start=(kp == 0 and kc == 0), stop=(kp == 3 and kc == 15)),
                                     reads=[b_uT, bw], writes=[P.pbuf[bank]], signal=(kc == 15))
                    for j, tt in enumerate(tiles):
                        bank = banks[j]
                        tm, btm = tmpc[tn % 2]
                        tn += 1
                        P.op("dve", lambda e, tm=tm, bank=bank, cb=cb: e.tensor_tensor(out=tm, in0=P.ps(bank), in1=g2[:, cb * 512:(cb + 1) * 512], op=ALU.mult),
                             reads=[P.pbuf[bank], bg2], writes=[btm])
                        P.op("pool", lambda e, tm=tm, j=j, cb=cb: e.tensor_tensor(out=xg[:, j, cb * 512:(cb + 1) * 512], in0=xg[:, j, cb * 512:(cb + 1) * 512], in1=tm, op=ALU.add),
                             reads=[btm, b_xg[j]], writes=[b_xg[j]])
                for j, tt in enumerate(tiles):
                    xj = xg[:, j, :]
                    if not last:
                        P.dma("pool", xs_d[tt * T:(tt + 1) * T, :], xj, reads=[b_xg[j]], writes=[B_xs[tt]], sbuf=b_xg[j])
                    else:
                        P.op("dve", lambda e: e.memset(small[:, 0:1], 0.0), writes=[b_small])
                        P.op("act", lambda e, xj=xj: e.activation(out=junk, in_=xj, func=AF.Square, accum_out=small[:, 0:1]), reads=[b_xg[j]], writes=[b_junk, b_small])
                        rstd_of(small[:, 0:1], D, 1, small[:, 1:2], small[:, 2:3], b_small, b_small, b_small)
                        P.op("dve", lambda e, xj=xj: e.scalar_tensor_tensor(out=xj, in0=xj, scalar=small[:, 2:3], in1=fnb, op0=ALU.mult, op1=ALU.mult),
                             reads=[b_xg[j], b_small, b_fnb], writes=[b_xg[j]])
                        P.dma("pool", y_out[tt * T:(tt + 1) * T, :], xj, reads=[b_xg[j]], writes=[B_out], sbuf=b_xg[j])
            pump_convert(10 ** 6)
            P.barrier()
            P.new_phase()
            P.top = mC

        P.barrier()
        print("instruction counts", P.n_inst, "sems", P.nsem)
        P.emit()
    return nc


def _consts():
    s = np.arange(128)
    tt, ss = s[:, None], s[None, :]
    cm = np.stack([(tt > ss), (tt <= ss), (tt < ss), (tt >= ss), np.ones((128, 128), bool)], 1).astype(np.float32)
    diff = (ss - tt).astype(np.float32)
    pidx = np.stack([s + 1, 128 - 1 - s, 128 - s, s], 1).astype(np.float32)
    return cm, diff, pidx


def _rope(length):
    GRID_W = 64
    pos = np.arange(length)
    row = (pos // GRID_W).astype(np.float32)
    col = (pos % GRID_W).astype(np.float32)
    half = 64
    inv = (1.0 / (np.float32(10000.0) ** (np.arange(0, half, 2, dtype=np.float32) / np.float32(half)))).astype(np.float32)
    ang = np.concatenate([row[:, None] * inv, col[:, None] * inv], -1).astype(np.float32)
    return np.concatenate([np.cos(ang), np.sin(ang)], -1).astype(np.float32)


_NC_CACHE = {}
DEBUG_SCRATCH = False
_LAST_RES = [None]


def kernel(x_prompt, x_sample, state_ret, state_ssd, c, c_ctx, w_ada, b_ada, norm1_w, w_in,
           ret_log_decay, conv_w, conv_b, dt_bias, a_log, d_skip, ssd_norm_w, w_out, norm2_w,
           w_ff1, w_ff2, final_norm_w, _n_cores=8):
    f = lambda a: np.ascontiguousarray(np.asarray(a, dtype=np.float32))
    x_prompt, x_sample, state_ret, state_ssd, c, c_ctx = map(f, (x_prompt, x_sample, state_ret, state_ssd, c, c_ctx))
    BP, PL, _ = x_prompt.shape
    BS, SLEN, _ = x_sample.shape
    DEPTH = w_ada.shape[0]
    n_cores = _n_cores
    EXCH = (n_cores == 2 * BS) and (BP % n_cores == 0) and (SLEN % 256 == 0)
    if EXCH:
        n_work = n_cores
        NP = BP // n_cores
        SL = SLEN // 2
    else:
        n_work = BS
        NP = BP // n_work
        SL = SLEN
    key = (DEPTH, NP, PL, SL, EXCH)
    if key not in _NC_CACHE:
        _NC_CACHE[key] = build_program(DEPTH, NP, PL, SL, EXCH)
    nc = _NC_CACHE[key]
    cm, diff, pidx = _consts()
    rope = _rope(SLEN)
    w_in = f(w_in)
    rld = f(ret_log_decay)
    cw = f(conv_w)
    dtb = f(dt_bias)
    alg = f(a_log)
    base = dict(
        cmask=cm, diffm=diff, pidx=pidx, zrow=np.zeros((1, 1536), ml_dtypes.bfloat16),
        conv_b=f(conv_b), d_skip=f(d_skip),
        ssd_norm_w=f(ssd_norm_w), w_out=f(w_out), w_ff1=f(w_ff1), w_ff2=f(w_ff2),
        final_norm_w=f(final_norm_w).reshape(1, D),
    )
    w_ada = f(w_ada)
    b_ada = f(b_ada)
    if EXCH:
        ada_half = [dict(w_ada_h=np.ascontiguousarray(w_ada[:, :, h * 3 * D:(h + 1) * 3 * D]),
                         b_ada_h=np.ascontiguousarray(b_ada[:, h * 3 * D:(h + 1) * 3 * D]),
                         nw_h=f(norm1_w) if h == 0 else f(norm2_w)) for h in range(2)]
    else:
        base.update(w_ada=w_ada, b_ada=b_ada, norm1_w=f(norm1_w), norm2_w=f(norm2_w))
    variants = {}
    for flip in ((False, True) if EXCH else (False,)):
        if not flip:
            v = dict(w_in=w_in, ret_log_decay=rld.reshape(DEPTH, 16), conv_w=cw.reshape(DEPTH, 3 * 1536),
                     dt_bias=dtb.reshape(DEPTH, 32), a_log=alg.reshape(DEPTH, 32))
        else:
            w2 = w_in.copy()
            w2[:, :, 6656:6672] = w_in[:, :, 6672:6688]
            w2[:, :, 6672:6688] = w_in[:, :, 6656:6672]
            v = dict(w_in=w2, ret_log_decay=np.ascontiguousarray(rld[:, ::-1]).reshape(DEPTH, 16),
                     conv_w=np.ascontiguousarray(cw[:, ::-1]).reshape(DEPTH, 3 * 1536),
                     dt_bias=np.ascontiguousarray(dtb[:, ::-1]).reshape(DEPTH, 32),
                     a_log=np.ascontiguousarray(alg[:, ::-1]).reshape(DEPTH, 32))
        variants[flip] = v
    tr = lambda s_: np.ascontiguousarray(s_.transpose(0, 1, 4, 2, 3)).reshape(DEPTH, 2, 128, 1024)
    in_maps = []
    meta = []
    for core in range(n_cores):
        if EXCH:
            b, half = core // 2, core % 2
            flip = (half == 1)
            plist = list(range(core * NP, (core + 1) * NP))
            xs_ = x_sample[b, half * SL:(half + 1) * SL]
            pos = np.arange(half * SL, (half + 1) * SL)
            sr, ss = tr(state_ret[b]), tr(state_ssd[b])
            if flip:
                xs_ = xs_[::-1]
                pos = pos[::-1]
                sr, ss = np.ascontiguousarray(sr[:, ::-1]), np.ascontiguousarray(ss[:, ::-1])
            xps = [x_prompt[p][::-1] if flip else x_prompt[p] for p in plist]
            psel = np.zeros((128, 2), np.float32)
            psel[:, 1 - half] = 1.0
        else:
            w = core % n_work
            b, half, flip = w, 0, False
            plist = list(range(w * NP, (w + 1) * NP))
            xs_ = x_sample[b]
            pos = np.arange(SLEN)
            sr, ss = tr(state_ret[b]), tr(state_ssd[b])
            xps = [x_prompt[p] for p in plist]
            psel = np.zeros((128, 2), np.float32)
        m = dict(base)
        m.update(variants[flip])
        if EXCH:
            m.update(ada_half[half])
        m.update(x_in=np.ascontiguousarray(np.concatenate(xps + [xs_], 0)), cond=np.ascontiguousarray(np.stack([c_ctx, c[b]], 0)),
                 st_ret=sr, st_ssd=ss, rope_cs=np.ascontiguousarray(rope[pos]), psel=psel)
        in_maps.append(m)
        meta.append((b, half, flip, plist))
    res = run_bass_kernel_spmd(nc, in_maps, core_ids=list(range(n_cores)))
    _LAST_RES[0] = res
    y_prompt = np.zeros((BP, PL, D), np.float32)
    y_sample = np.zeros((BS, SLEN, D), np.float32)
    nsr = np.zeros((BP, DEPTH, 2, 8, 128, 128), np.float32)
    nss = np.zeros((BP, DEPTH, 2, 16, 64, 128), np.float32)
    for core in range(n_work):
        b, half, flip, plist = meta[core]
        r = res.results[core]
        yo = r["y_out"]
        a = r["ns_ret"].reshape(NP, DEPTH, 2, 128, 8, 128).transpose(0, 1, 2, 4, 5, 3)
        bb = r["ns_ssd"].reshape(NP, DEPTH, 2, 128, 16, 64).transpose(0, 1, 2, 4, 5, 3)
        for i, p in enumerate(plist):
            yp = yo[i * PL:(i + 1) * PL]
            y_prompt[p] = yp[::-1] if flip else yp
            nsr[p] = a[i][:, ::-1] if flip else a[i]
            nss[p] = bb[i][:, ::-1] if flip else bb[i]
        ys = yo[NP * PL:]
        y_sample[b, half * SL:(half + 1) * SL] = ys[::-1] if flip else ys
    return (y_prompt, y_sample, nsr, nss)
```

```python
import types
import numpy as np
import ml_dtypes
from contextlib import ExitStack
import concourse.bass as bass
import concourse.mybir as mybir
from concourse.bass_utils import run_bass_kernel_spmd

F32 = mybir.dt.float32
BF16 = mybir.dt.bfloat16
ALU = mybir.AluOpType
AF = mybir.ActivationFunctionType
AX = mybir.AxisListType

EPOCH = 30000


class Tok:
    __slots__ = ("key", "val", "eng")

    def __init__(self, eng):
        self.key = None
        self.val = None
        self.eng = eng


class Buf:
    __slots__ = ("name", "w", "r", "dsem")

    def __init__(self, name):
        self.name = name
        self.w = None
        self.r = []
        self.dsem = None


class Prog:
    CE = ("pe", "act", "dve", "pool")
    ALLE = ("pe", "act", "dve", "pool", "sp")

    def __init__(self, nc, stack, arena_words=53200):
        self.nc = nc
        self.stack = stack
        self.ops = {e: [] for e in self.ALLE}
        self.sems = {}
        self.ecount = {e: 0 for e in self.CE}
        self.waited = {e: {} for e in self.ALLE}
        self.pe_pending = []
        self.pe_last_rec = None
        self.dma_sems = {}
        self.nsem = 0
        self.AW = arena_words
        self.arena = stack.enter_context(nc.sbuf_tensor("arena", [128, arena_words], F32))
        self.top = 0
        self.psum = [stack.enter_context(nc.psum_tensor(f"psb{i}", [128, 512], F32)) for i in range(8)]
        self.pbuf = [Buf(f"psum{i}") for i in range(8)]
        self.n_inst = {e: 0 for e in self.ALLE}
        self.dsem_ctr = 0
        self.dsem_base = 0

    def pin(self, buf):
        buf.dsem = ("d", self.dsem_ctr)
        self.dsem_ctr += 1
        self.dsem_base = self.dsem_ctr

    def new_phase(self):
        self.dsem_ctr = self.dsem_base

    def alloc(self, name, free_shape, dtype=F32):
        n = int(np.prod(free_shape))
        nw = n if dtype == F32 else (n + 1) // 2
        nw = (nw + 7) // 8 * 8
        off = self.top
        self.top += nw
        assert self.top <= self.AW, f"arena overflow at {name}: {self.top}"
        v = self.arena[:, off:off + nw]
        if dtype != F32:
            v = v.bitcast(dtype)
        v = v[:, 0:n]
        if len(free_shape) == 2:
            v = v.rearrange("p (a b) -> p a b", a=free_shape[0])
        elif len(free_shape) == 3:
            v = v.rearrange("p (a b c) -> p a b c", a=free_shape[0], b=free_shape[1])
        return v, Buf(name)

    def ps(self, i, dtype=F32):
        v = self.psum[i][:, :]
        if dtype != F32:
            v = v.bitcast(dtype)
        return v

    def _sem(self, key):
        if key not in self.sems:
            self.nsem += 1
            self.sems[key] = self.stack.enter_context(self.nc.semaphore(f"s{self.nsem}"))
        return self.sems[key]

    def _new_signal(self, eng):
        c = self.ecount[eng]
        self.ecount[eng] = c + 1
        key = ("e", eng, c // EPOCH)
        self._sem(key)
        return key, (c % EPOCH) + 1

    def _need(self, eng, tok, raw):
        if tok is None:
            return
        if tok.eng == eng and eng in self.CE and not raw:
            return
        if tok.key is None:
            self._force_pe_signal()
        w = self.waited[eng]
        if w.get(tok.key, 0) >= tok.val:
            return
        w[tok.key] = tok.val
        sem = self.sems[tok.key]
        val = tok.val
        self.ops[eng].append(lambda e, sem=sem, val=val: e.wait_ge(sem, val))
        self.n_inst[eng] += 1

    def _force_pe_signal(self):
        rec = self.pe_last_rec
        assert rec["sig"] is None
        key, val = self._new_signal("pe")
        rec["sig"] = (self.sems[key], 1)
        for t in self.pe_pending:
            t.key, t.val = key, val
        self.pe_pending = []

    def _deps(self, eng, reads, writes):
        for b in reads:
            self._need(eng, b.w, True)
        for b in writes:
            self._need(eng, b.w, False)
            for t in b.r:
                self._need(eng, t, False)

    def _commit(self, tok, reads, writes):
        for b in reads:
            if len(b.r) > 24:
                d = {}
                rest = []
                for t in b.r:
                    if t.key is None:
                        rest.append(t)
                    elif t.key not in d or d[t.key].val < t.val:
                        d[t.key] = t
                b.r = rest + list(d.values())
            b.r.append(tok)
        for b in writes:
            b.w = tok
            b.r = []

    @staticmethod
    def _freeze(fn):
        if fn.__closure__ is None:
            return fn
        cells = []
        for c in fn.__closure__:
            try:
                cells.append(types.CellType(c.cell_contents))
            except ValueError:
                cells.append(c)
        g = types.FunctionType(fn.__code__, fn.__globals__, fn.__name__, fn.__defaults__, tuple(cells))
        g.__kwdefaults__ = fn.__kwdefaults__
        return g

    def op(self, eng, fn, reads=(), writes=(), signal=True):
        fn = self._freeze(fn)
        self._deps(eng, reads, writes)
        tok = Tok(eng)
        rec = {"sig": None}
        if signal:
            key, val = self._new_signal(eng)
            tok.key, tok.val = key, val
            rec["sig"] = (self.sems[key], 1)
            if eng == "pe":
                for t in self.pe_pending:
                    t.key, t.val = key, val
                self.pe_pending = []
        else:
            assert eng == "pe"
            self.pe_pending.append(tok)
        if eng == "pe":
            self.pe_last_rec = rec

        def run(e, fn=fn, rec=rec):
            ins = fn(e)
            if rec["sig"] is not None:
                ins.then_inc(rec["sig"][0], rec["sig"][1])
        self.ops[eng].append(run)
        self.n_inst[eng] += 1
        self._commit(tok, reads, writes)
        return tok

    def dma(self, q, out, in_, reads=(), writes=(), sbuf=None, sem_key=None, **kw):
        self._deps(q, reads, writes)
        if sem_key is None:
            if sbuf.dsem is None:
                sbuf.dsem = ("d", self.dsem_ctr)
                self.dsem_ctr += 1
            sem_key = sbuf.dsem
        self._sem(sem_key)
        cnt = self.dma_sems.get(sem_key, 0) + 16
        self.dma_sems[sem_key] = cnt
        tok = Tok(None)
        tok.key, tok.val = sem_key, cnt
        sem = self.sems[sem_key]

        def run(e, out=out, in_=in_, sem=sem, kw=kw):
            e.dma_start(out=out, in_=in_, **kw).then_inc(sem, 16)
        self.ops[q].append(run)
        self.n_inst[q] += 1
        self._commit(tok, reads, writes)
        return tok

    def collective(self, kind, groups, in_ap, out_ap, reads=(), writes=(), inc=16):
        q = "pool"
        self._deps(q, reads, writes)
        sem_key = ("cc",)
        self._sem(sem_key)
        cnt = self.dma_sems.get(sem_key, 0) + inc
        self.dma_sems[sem_key] = cnt
        tok = Tok(None)
        tok.key, tok.val = sem_key, cnt
        sem = self.sems[sem_key]

        def run(e, kind=kind, groups=groups, in_ap=in_ap, out_ap=out_ap, sem=sem, inc=inc):
            e.collective_compute(kind, ALU.bypass, replica_groups=groups, ins=[in_ap], outs=[out_ap]).then_inc(sem, inc)
        self.ops[q].append(run)
        self.n_inst[q] += 1
        self._commit(tok, reads, writes)
        return tok

    def barrier(self):
        toks = []
        for e in self.CE:
            if e == "pe" and self.pe_pending:
                self._force_pe_signal()
            c = self.ecount[e]
            if c > 0:
                t = Tok(None)
                t.key = ("e", e, (c - 1) // EPOCH)
                t.val = ((c - 1) % EPOCH) + 1
                toks.append(t)
        for k, cnt in self.dma_sems.items():
            t = Tok(None)
            t.key, t.val = k, cnt
            toks.append(t)
        for e in self.ALLE:
            for t in toks:
                self._need(e, t, True)

    def emit(self):
        ops = self.ops
        with self.nc.Block() as block:
            @block.tensor
            def _(e):
                for f in ops["pe"]:
                    f(e)

            @block.scalar
            def _(e):
                for f in ops["act"]:
                    f(e)

            @block.vector
            def _(e):
                for f in ops["dve"]:
                    f(e)

            @block.gpsimd
            def _(e):
                for f in ops["pool"]:
                    f(e)

            @block.sync
            def _(e):
                for f in ops["sp"]:
                    f(e)


D = 2048
DIN = 6688
HR = 8
HS = 16
PSD = 64
T = 128
DFF = 8192
EPS = 1e-6
DKS = 128 ** -0.5


def bc_last(ap, n):
    return ap.unsqueeze(2).broadcast_to([ap.shape[0], ap.shape[1], n])


def bc_mid(ap, n):
    return ap.unsqueeze(1).broadcast_to([ap.shape[0], n, ap.shape[1]])


def build_program(DEPTH, NP, PL, SL, EXCH=False):
    PAIRS = [[0, 1], [2, 3], [4, 5], [6, 7]]
    NTOK = NP * PL + SL
    NT = NTOK // T
    CP = PL // T
    CS = SL // T
    seqs = [(i * CP, CP, False, i) for i in range(NP)] + [(NP * CP, CS, True, NP)]
    NSEQ = len(seqs)
    groups = []
    pt = list(range(NP * CP))
    for i in range(0, len(pt), 4):
        groups.append((pt[i:i + 4], 0))
    stl = list(range(NP * CP, NT))
    for i in range(0, len(stl), 4):
        groups.append((stl[i:i + 4], 1))

    nc = bass.Bass("TRN2", target_bir_lowering=False)

    def din(name, shape, dt=F32):
        return nc.dram_tensor(name, list(shape), dt, kind="ExternalInput").ap()

    def dout(name, shape, dt=F32):
        return nc.dram_tensor(name, list(shape), dt, kind="ExternalOutput").ap()

    def dscr(name, shape, dt=F32):
        return nc.dram_tensor(name, list(shape), dt, kind=("ExternalOutput" if (DEBUG_SCRATCH and not name.startswith("wbf")) else "Internal")).ap()

    x_in = din("x_in", [NTOK, D])
    cond = din("cond", [2, D])
    st_ret = din("st_ret", [DEPTH, 2, 128, 1024])
    st_ssd = din("st_ssd", [DEPTH, 2, 128, 1024])
    rope_cs = din("rope_cs", [SL, 128])
    cmask_d = din("cmask", [128, 5, 128])
    diff_d = din("diffm", [128, 128])
    pidx_d = din("pidx", [128, 4])
    zrow_d = din("zrow", [1, 1536], BF16)
    psel_d = din("psel", [128, 2])
    if not EXCH:
        w_ada = din("w_ada", [DEPTH, D, 6 * D])
        b_ada = din("b_ada", [DEPTH, 6 * D])
        norm1_w = din("norm1_w", [DEPTH, D])
        norm2_w = din("norm2_w", [DEPTH, D])
    w_in = din("w_in", [DEPTH, D, DIN])
    rld = din("ret_log_decay", [DEPTH, 16])
    conv_w = din("conv_w", [DEPTH, 3 * 1536])
    conv_b = din("conv_b", [DEPTH, 1536])
    dt_bias = din("dt_bias", [DEPTH, 32])
    a_log = din("a_log", [DEPTH, 32])
    d_skip = din("d_skip", [DEPTH, 16])
    ssd_nw = din("ssd_norm_w", [DEPTH, 1024])
    w_out = din("w_out", [DEPTH, D, D])
    w_ff1 = din("w_ff1", [DEPTH, D, DFF])
    w_ff2 = din("w_ff2", [DEPTH, DFF, D])
    fnw = din("final_norm_w", [1, D])

    y_out = dout("y_out", [NTOK, D])
    ns_ret = dout("ns_ret", [NP, DEPTH, 2, 128, 1024])
    ns_ssd = dout("ns_ssd", [NP, DEPTH, 2, 128, 1024])

    xs_d = dscr("xs_d", [NTOK, D])
    qk_d = dscr("qk_d", [NTOK, 2048], BF16)
    v_d = dscr("v_d", [NTOK, 1024], BF16)
    sg_d = dscr("sg_d", [NTOK, 1024], BF16)
    sz_d = dscr("sz_d", [NTOK, 1024], BF16)
    xpre_d = dscr("xpre_d", [NTOK + 2 * NSEQ, 1536], BF16)
    xpost_d = dscr("xpost_d", [NTOK, 1536], BF16)
    dt_d = dscr("dt_d", [NTOK, 32])
    y_d = dscr("y_d", [NTOK, 2048], BF16)
    stf_d = dscr("stf_d", [NT, 128, 2048], BF16)
    cstb_d = dscr("cstb_d", [NT, 128, 2048])
    cdb_d = dscr("cdb_d", [NT, 128, 16])
    mod_d = dscr("mod_d", [DEPTH, 2, 6 * D])
    NWT = 14 + 4 + 16 + 16
    if DEBUG_SCRATCH:
        dbg_h = dout("dbg_h", [NTOK, 2048], BF16)
        dbg_x = dout("dbg_x", [NTOK, 2048])
        dbg_a = dout("dbg_a", [NTOK, 2048])
        dbg_h32 = dout("dbg_h32", [NTOK, 2048])
    wbf = [dict(w0=dscr(f"wbf{i}_in", [D, DIN], BF16), w1=dscr(f"wbf{i}_out", [D, D], BF16),
                w2=dscr(f"wbf{i}_ff1", [D, DFF], BF16), w3=dscr(f"wbf{i}_ff2", [DFF, D], BF16)) for i in range(2)]
    exst_in = dscr("exst_in", [128, 2048])
    exst_out = dscr("exst_out", [256, 2048])
    exrow_in = dscr("exrow_in", [1, 1536], BF16)
    exrow_out = dscr("exrow_out", [2, 1536], BF16)
    if EXCH:
        w_ada_h = din("w_ada_h", [DEPTH, D, 3 * D])
        b_ada_h = din("b_ada_h", [DEPTH, 3 * D])
        nw_h = din("nw_h", [DEPTH, D])
        exmod_in = dscr("exmod_in", [DEPTH * 2, 3 * D])
        exmod_out = dscr("exmod_out", [2 * DEPTH * 2, 3 * D])

    def modrow(l, ci, slot):
        if EXCH:
            r = (slot // 3) * DEPTH * 2 + l * 2 + ci
            return exmod_out[r, (slot % 3) * D:(slot % 3 + 1) * D]
        return mod_d[l, ci, slot * D:(slot + 1) * D]

    def xrow(si, t):
        t0, ncks, _, _ = seqs[si]
        return t0 * T + 2 * si + 1 + t

    with ExitStack() as st:
        P = Prog(nc, st)
        ident, b_ident = P.alloc("ident", [128], BF16)
        cmask, b_cmask = P.alloc("cmask", [5, 128])
        diffm, b_diff = P.alloc("diffm", [128])
        pidx, b_pidx = P.alloc("pidx", [4])
        psel, b_psel = P.alloc("psel", [2])
        NRING = 3
        wring = [P.alloc(f"wring{i}", [16, 512], BF16) for i in range(NRING)]
        base_top = P.top
        for b_ in (b_cmask, b_diff, b_pidx, b_psel, wring[0][1], wring[1][1], wring[2][1]):
            P.pin(b_)
        M_gt, M_le, M_lt, M_ge, M_one = [cmask[:, i, :] for i in range(5)]

        d_const = Buf("d_const")
        B_xs = [Buf(f"xs{t}") for t in range(NT)]
        B_qk = [Buf(f"qk{t}") for t in range(NT)]
        B_v = [Buf(f"v{t}") for t in range(NT)]
        B_sg = [Buf(f"sg{t}") for t in range(NT)]
        B_sz = [Buf(f"sz{t}") for t in range(NT)]
        B_xpre = Buf("xpre")
        B_xpost = [Buf(f"xpost{t}") for t in range(NT)]
        B_dt = [Buf(f"dt{t}") for t in range(NT)]
        B_y = [Buf(f"y{t}") for t in range(NT)]
        B_stf = [Buf(f"stf{t}") for t in range(NT)]
        B_cstb = [Buf(f"cstb{t}") for t in range(NT)]
        B_cdb = [Buf(f"cdb{t}") for t in range(NT)]
        B_mod = Buf("mod")
        B_wbf = [[Buf(f"wbf{i}_{k}") for k in range(4)] for i in range(2)]
        B_out = Buf("outs")
        B_exin = Buf("exin"); B_exout = Buf("exout"); B_exrin = Buf("exrin"); B_exrout = Buf("exrout")

        P.dma("sp", cmask, cmask_d, reads=[d_const], writes=[b_cmask], sbuf=b_cmask)
        P.dma("sp", diffm, diff_d, reads=[d_const], writes=[b_diff], sbuf=b_diff)
        P.dma("sp", pidx, pidx_d, reads=[d_const], writes=[b_pidx], sbuf=b_pidx)
        P.dma("sp", psel, psel_d, reads=[d_const], writes=[b_psel], sbuf=b_psel)
        P.op("pool", lambda e: e.memset(ident, 0.0), writes=[b_ident])
        P.op("pool", lambda e: e.affine_select(out=ident, in_=ident, pattern=[[-1, 128]],
                                               compare_op=ALU.not_equal, fill=1.0, base=0,
                                               channel_multiplier=1), reads=[b_ident], writes=[b_ident])
        for si in range(NSEQ):
            for t in (-1, seqs[si][1] * T):
                r = xrow(si, t)
                P.dma("sp", xpre_d[r:r + 1, :], zrow_d, reads=[d_const], writes=[B_xpre], sem_key=("x", "zrow"))

        def wtile_src(par, i):
            if i < 14:
                nco = 512 if i < 13 else 32
                return 0, wbf[par]["w0"][:, i * 512:i * 512 + nco].rearrange("(kc p) c -> p kc c", p=128), nco
            i -= 14
            if i < 4:
                return 1, wbf[par]["w1"][:, i * 512:(i + 1) * 512].rearrange("(kc p) c -> p kc c", p=128), 512
            i -= 4
            if i < 16:
                return 2, wbf[par]["w2"][:, i * 512:(i + 1) * 512].rearrange("(kc p) c -> p kc c", p=128), 512
            i -= 16
            cb, kp = i // 4, i % 4
            return 3, wbf[par]["w3"][kp * 2048:(kp + 1) * 2048, cb * 512:(cb + 1) * 512].rearrange("(kc p) c -> p kc c", p=128), 512

        conv_jobs = []

        def convert_layer(l):
            par = l % 2
            for kind, (src, R) in enumerate(((w_in[l], D), (w_out[l], D), (w_ff1[l], D), (w_ff2[l], DFF))):
                dst = wbf[par]["w%d" % kind]
                for r0 in range(0, R, 128):
                    conv_jobs.append((par, kind, dst[r0:r0 + 128, :], src[r0:r0 + 128, :]))

        def pump_convert(n):
            for _ in range(min(n, len(conv_jobs))):
                par, kind, dst, src = conv_jobs.pop(0)
                P.dma("pool", dst, src, reads=[d_const], writes=[B_wbf[par][kind]], sem_key=("wc", par, kind))

        class WStream:
            def __init__(self):
                self.seq = []
                self.issued = 0
                self.slot_n = 0

            def extend(self, l, idxs):
                for i in idxs:
                    self.seq.append((l, i))

            def prefetch(self, upto):
                while self.issued < min(upto, len(self.seq)):
                    l, i = self.seq[self.issued]
                    kind, src, nco = wtile_src(l % 2, i)
                    slot = self.issued % NRING
                    wt, bw = wring[slot]
                    P.dma("sp", wt[:, :, 0:nco], src, reads=[B_wbf[l % 2][kind]], writes=[bw], sbuf=bw)
                    self.issued += 1

            def get(self, n):
                self.prefetch(n + NRING)
                return wring[n % NRING]

        WS = WStream()
        wcount = [0]

        def next_w():
            r = WS.get(wcount[0])
            wcount[0] += 1
            return r

        for l in range(DEPTH):
            for g in groups:
                WS.extend(l, range(14))
            for g in groups:
                WS.extend(l, range(14, NWT))

        convert_layer(0)
        pump_convert(10 ** 6)
        m0 = P.top
        cT, b_cT = P.alloc("cT", [16, 2])
        modsb, b_modsb = P.alloc("modsb", [6 * D])
        badab, b_bada = P.alloc("badab", [6 * D])
        nwb, b_nwb = P.alloc("nwb", [2 * D])
        wada = [P.alloc(f"wada{i}", [4096]) for i in range(2)]
        for ci_ in range(2):
            P.dma("sp", cT[:, :, ci_], cond[ci_, :].rearrange("(kc p) -> p kc", p=128), reads=[d_const], writes=[b_cT], sbuf=b_cT,
                  allow_slow_non_contiguous=True)
        P.op("act", lambda e: e.activation(out=cT, in_=cT, func=AF.Silu), reads=[b_cT], writes=[b_cT])
        wn = 0
        for l in (range(DEPTH) if EXCH else []):
            P.dma("sp", badab[0:2, 0:3 * D], b_ada_h[l, :].partition_broadcast(2), reads=[d_const], writes=[b_bada], sbuf=b_bada)
            P.dma("sp", nwb[0:2, 0:D], nw_h[l, :].partition_broadcast(2), reads=[d_const], writes=[b_nwb], sbuf=b_nwb)
            for cg in range(2):
                ncol = 4096 if cg == 0 else 2048
                nbk = ncol // 512
                for kc in range(16):
                    wt, bw = wada[wn % 2]
                    wn += 1
                    P.dma("sp", wt[:, 0:ncol], w_ada_h[l][kc * 128:(kc + 1) * 128, cg * 4096:cg * 4096 + ncol],
                          reads=[d_const], writes=[bw], sbuf=bw)
                    for b in range(nbk):
                        P.op("pe", lambda e, b=b, kc=kc, wt=wt: e.matmul(P.ps(b)[0:2, :], lhsT=cT[:, kc, :], rhs=wt[:, b * 512:(b + 1) * 512],
                                                                         start=(kc == 0), stop=(kc == 15)),
                             reads=[b_cT, bw], writes=[P.pbuf[b]], signal=(kc == 15 or b == nbk - 1))
                for b in range(nbk):
                    c0 = cg * 4096 + b * 512
                    P.op("dve", lambda e, b=b, c0=c0: e.tensor_tensor(out=modsb[0:2, c0:c0 + 512], in0=P.ps(b)[0:2, :],
                                                                      in1=badab[0:2, c0:c0 + 512], op=ALU.add),
                         reads=[P.pbuf[b], b_bada], writes=[b_modsb])
            P.op("dve", lambda e: e.scalar_tensor_tensor(out=modsb[0:2, D:2 * D], in0=modsb[0:2, D:2 * D], scalar=1.0,
                                                         in1=nwb[0:2, 0:D], op0=ALU.add, op1=ALU.mult), reads=[b_modsb, b_nwb], writes=[b_modsb])
            P.dma("sp", exmod_in[l * 2:l * 2 + 2, :], modsb[0:2, 0:3 * D], reads=[b_modsb], writes=[B_mod], sbuf=b_modsb)
        if EXCH:
            B_modin = B_mod
            B_mod = Buf("modout")
            P.collective("AllGather", PAIRS, exmod_in, exmod_out, reads=[B_modin], writes=[B_mod], inc=1)
        for l in ([] if EXCH else range(DEPTH)):
            P.dma("sp", badab[0:2, :], b_ada[l, :].partition_broadcast(2),
                  reads=[d_const], writes=[b_bada], sbuf=b_bada)
            P.dma("sp", nwb[0:2, 0:D], norm1_w[l, :].partition_broadcast(2), reads=[d_const], writes=[b_nwb], sbuf=b_nwb)
            P.dma("sp", nwb[0:2, D:2 * D], norm2_w[l, :].partition_broadcast(2), reads=[d_const], writes=[b_nwb], sbuf=b_nwb)
            for cg in range(3):
                for kc in range(16):
                    wt, bw = wada[wn % 2]
                    wn += 1
                    P.dma("sp", wt, w_ada[l][kc * 128:(kc + 1) * 128, cg * 4096:(cg + 1) * 4096],
                          reads=[d_const], writes=[bw], sbuf=bw)
                    for b in range(8):
                        P.op("pe", lambda e, b=b, kc=kc, wt=wt: e.matmul(P.ps(b)[0:2, :], lhsT=cT[:, kc, :], rhs=wt[:, b * 512:(b + 1) * 512],
                                                                         start=(kc == 0), stop=(kc == 15)),
                             reads=[b_cT, bw], writes=[P.pbuf[b]], signal=(kc == 15 or b == 7))
                for b in range(8):
                    c0 = cg * 4096 + b * 512
                    P.op("dve", lambda e, b=b, c0=c0: e.tensor_tensor(out=modsb[0:2, c0:c0 + 512], in0=P.ps(b)[0:2, :],
                                                                      in1=badab[0:2, c0:c0 + 512], op=ALU.add),
                         reads=[P.pbuf[b], b_bada], writes=[b_modsb])
            for slot, off in ((1, 0), (4, D)):
                P.op("dve", lambda e, slot=slot, off=off: e.scalar_tensor_tensor(
                    out=modsb[0:2, slot * D:(slot + 1) * D], in0=modsb[0:2, slot * D:(slot + 1) * D], scalar=1.0,
                    in1=nwb[0:2, off:off + D], op0=ALU.add, op1=ALU.mult), reads=[b_modsb, b_nwb], writes=[b_modsb])
            P.dma("sp", mod_d[l], modsb[0:2, :], reads=[b_modsb], writes=[B_mod], sbuf=b_modsb)
        P.barrier()
        P.new_phase()
        P.top = m0

        def rstd_of(ss, n, ncols, tmp, out, b_ss, b_tmp, b_out):
            P.op("act", lambda e: e.activation(out=tmp, in_=ss, func=AF.Ln, scale=1.0 / n, bias=EPS), reads=[b_ss], writes=[b_tmp])
            P.op("act", lambda e: e.activation(out=out, in_=tmp, func=AF.Exp, scale=-0.5), reads=[b_tmp], writes=[b_out])

        ps_rot = [0]

        def transposes_to(src, b_src, nblk, dst_fn, b_dst, evac_engs=("act", "dve")):
            i = 0
            k = 0
            while i < nblk:
                n = min(8, nblk - i)
                bank = ps_rot[0] % 2
                ps_rot[0] += 1
                pst = P.ps(bank, BF16)
                for j in range(n):
                    P.op("pe", lambda e, j=j, i=i, pst=pst: e.transpose(out=pst[:, j * 128:(j + 1) * 128], in_=src[:, (i + j) * 128:(i + j + 1) * 128],
                                                                       identity=ident),
                         reads=[b_src, b_ident], writes=[P.pbuf[bank]], signal=(j == n - 1))
                eng = evac_engs[k % len(evac_engs)]
                k += 1
                dst = dst_fn(i, n)
                srcv = pst[:, 0:n * 128].rearrange("p (a b) -> p a b", a=n)
                if eng == "act":
                    P.op("act", lambda e, dst=dst, srcv=srcv: e.copy(out=dst, in_=srcv), reads=[P.pbuf[bank]], writes=[b_dst])
                else:
                    P.op("dve", lambda e, dst=dst, srcv=srcv: e.tensor_copy(out=dst, in_=srcv), reads=[P.pbuf[bank]], writes=[b_dst])
                i += n

        mm_rot = [0]

        def mm_bank():
            b = 2 + (mm_rot[0] % 6)
            mm_rot[0] += 1
            return b

        for l in range(DEPTH):
            last = (l == DEPTH - 1)
            if l + 1 < DEPTH:
                convert_layer(l + 1)
            x_src = x_in if l == 0 else xs_d
            B_xsrc = (lambda t: d_const) if l == 0 else (lambda t: B_xs[t])

            mA = P.top
            hTs = [P.alloc(f"hT{i}", [16, 512], BF16) for i in range(2)]
            xb = [P.alloc(f"xb{i}", [D]) for i in range(2)]
            h32, b_h32 = P.alloc("h32", [D])
            hbf = [P.alloc(f"hbf{i}", [D], BF16) for i in range(4)]
            junk, b_junk = P.alloc("junk", [D], BF16)
            modA = [(P.alloc(f"a1bc{i}", [D]), P.alloc(f"sh1bc{i}", [D])) for i in range(2)]
            small, b_small = P.alloc("smallA", [8])
            ropet, b_rope = P.alloc("ropet", [4, 128])
            dtbb, b_dtbb = P.alloc("dtbb", [32])
            stg = [P.alloc(f"stg{i}", [512], BF16) for i in range(6)]
            rt = [P.alloc(f"rt{i}", [4, 64]) for i in range(4)]
            dtst = [P.alloc(f"dtst{i}", [32]) for i in range(2)]
            P.dma("sp", dtbb, dt_bias[l, :].partition_broadcast(128), reads=[d_const], writes=[b_dtbb], sbuf=b_dtbb)
            stg_n = 0

            def prepA_nonpe(gi):
                tiles_, ci_ = groups[gi]
                (a1bc, b_a1), (sh1bc, b_sh1) = modA[gi % 2]
                P.dma("sp", a1bc, modrow(l, ci_, 1).partition_broadcast(128), reads=[B_mod], writes=[b_a1], sbuf=b_a1)
                P.dma("sp", sh1bc, modrow(l, ci_, 0).partition_broadcast(128), reads=[B_mod], writes=[b_sh1], sbuf=b_sh1)
                for j, tt in enumerate(tiles_):
                    pump_convert(3)
                    xt, bx = xb[j % 2]
                    hb, bh = hbf[j]
                    P.dma("sp", xt, x_src[tt * T:(tt + 1) * T, :], reads=[B_xsrc(tt)], writes=[bx], sbuf=bx)
                    P.op("dve", lambda e: e.memset(small[:, 0:1], 0.0), writes=[b_small])
                    P.op("act", lambda e, xt=xt: e.activation(out=junk, in_=xt, func=AF.Square, accum_out=small[:, 0:1]),
                         reads=[bx], writes=[b_junk, b_small])
                    rstd_of(small[:, 0:1], D, 1, small[:, 1:2], small[:, 2:3], b_small, b_small, b_small)
                    P.op("dve", lambda e, xt=xt: e.scalar_tensor_tensor(out=h32, in0=xt, scalar=small[:, 2:3], in1=a1bc,
                                                                        op0=ALU.mult, op1=ALU.mult),
                         reads=[bx, b_small, b_a1], writes=[b_h32])
                    P.op("pool", lambda e, hb=hb: e.tensor_tensor(out=hb, in0=h32, in1=sh1bc, op=ALU.add),
                         reads=[b_h32, b_sh1], writes=[bh])

            def prepA_pe(gi):
                tiles_, ci_ = groups[gi]
                hTn, b_hTn = hTs[gi % 2]
                for j, tt in enumerate(tiles_):
                    hb, bh = hbf[j]
                    transposes_to(hb, bh, 16, lambda i, n, j=j: hTn[:, i:i + n, j * T:(j + 1) * T], b_hTn)

            prepA_nonpe(0)
            prepA_pe(0)
            for gi, (tiles, ci) in enumerate(groups):
                G = len(tiles) * T
                hT, b_hT = hTs[gi % 2]
                if ci == 1:
                    r0 = (tiles[0] - NP * CP) * T
                    P.dma("sp", ropet[:, 0:len(tiles), :], rope_cs[r0:r0 + G, :].rearrange("(j p) c -> p j c", p=128),
                          reads=[d_const], writes=[b_rope], sbuf=b_rope)
                for cb in range(14):
                    if cb == 1 and gi + 1 < len(groups):
                        prepA_nonpe(gi + 1)
                    if cb == 9 and gi + 1 < len(groups):
                        prepA_pe(gi + 1)
                    wt, bw = next_w()
                    nco = 512 if cb < 13 else 32
                    for j, tt in enumerate(tiles):
                        bank = mm_bank()
                        psv = P.ps(bank)[:, 0:nco]
                        for kc in range(16):
                            P.op("pe", lambda e, kc=kc, j=j, wt=wt, psv=psv, nco=nco: e.matmul(
                                psv, lhsT=hT[:, kc, j * T:(j + 1) * T], rhs=wt[:, kc, 0:nco], start=(kc == 0), stop=(kc == 15)),
                                reads=[b_hT, bw], writes=[P.pbuf[bank]], signal=(kc == 15))
                        rows = slice(tt * T, (tt + 1) * T)
                        if cb < 13:
                            sgt, bs = stg[stg_n % 6]
                            stg_n += 1
                        if cb < 4:
                            if ci == 1:
                                ps3 = psv.rearrange("p (h n) -> p h n", h=4)
                                x1 = ps3[:, :, 0:64]
                                x2 = ps3[:, :, 64:128]
                                cs = bc_mid(ropet[:, j, 0:64], 4)
                                sn = bc_mid(ropet[:, j, 64:128], 4)
                                o3 = sgt.rearrange("p (h n) -> p h n", h=4)
                                (t1, bt1), (t2, bt2), (t3, bt3), (t4, bt4) = rt
                                P.op("dve", lambda e, x1=x1, cs=cs, t1=t1: e.tensor_tensor(out=t1, in0=x1, in1=cs, op=ALU.mult),
                                     reads=[P.pbuf[bank], b_rope], writes=[bt1])
                                P.op("dve", lambda e, x2=x2, sn=sn, t2=t2: e.tensor_tensor(out=t2, in0=x2, in1=sn, op=ALU.mult),
                                     reads=[P.pbuf[bank], b_rope], writes=[bt2])
                                P.op("dve", lambda e, x1=x1, sn=sn, t3=t3: e.tensor_tensor(out=t3, in0=x1, in1=sn, op=ALU.mult),
                                     reads=[P.pbuf[bank], b_rope], writes=[bt3])
                                P.op("dve", lambda e, x2=x2, cs=cs, t4=t4: e.tensor_tensor(out=t4, in0=x2, in1=cs, op=ALU.mult),
                                     reads=[P.pbuf[bank], b_rope], writes=[bt4])
                                P.op("pool", lambda e, o3=o3, t1=t1, t2=t2: e.tensor_tensor(out=o3[:, :, 0:64], in0=t1, in1=t2, op=ALU.subtract),
                                     reads=[bt1, bt2], writes=[bs])
                                P.op("pool", lambda e, o3=o3, t3=t3, t4=t4: e.tensor_tensor(out=o3[:, :, 64:128], in0=t3, in1=t4, op=ALU.add),
                                     reads=[bt3, bt4], writes=[bs])
                            else:
                                P.op("act", lambda e, sgt=sgt, psv=psv: e.copy(out=sgt, in_=psv), reads=[P.pbuf[bank]], writes=[bs])
                            P.dma("pool", qk_d[rows, cb * 512:(cb + 1) * 512], sgt, reads=[bs], writes=[B_qk[tt]], sbuf=bs)
                        elif cb < 6:
                            P.op("act", lambda e, sgt=sgt, psv=psv: e.copy(out=sgt, in_=psv), reads=[P.pbuf[bank]], writes=[bs])
                            P.dma("pool", v_d[rows, (cb - 4) * 512:(cb - 3) * 512], sgt, reads=[bs], writes=[B_v[tt]], sbuf=bs)
                        elif cb < 10:
                            P.op("act", lambda e, sgt=sgt, psv=psv: e.activation(out=sgt, in_=psv, func=AF.Silu),
                                 reads=[P.pbuf[bank]], writes=[bs])
                            if cb < 8:
                                P.dma("pool", sg_d[rows, (cb - 6) * 512:(cb - 5) * 512], sgt, reads=[bs], writes=[B_sg[tt]], sbuf=bs)
                            else:
                                P.dma("pool", sz_d[rows, (cb - 8) * 512:(cb - 7) * 512], sgt, reads=[bs], writes=[B_sz[tt]], sbuf=bs)
                        elif cb < 13:
                            P.op("dve", lambda e, sgt=sgt, psv=psv: e.tensor_copy(out=sgt, in_=psv), reads=[P.pbuf[bank]], writes=[bs])
                            si = [k for k, s in enumerate(seqs) if s[0] <= tt < s[0] + s[1]][0]
                            r = xrow(si, (tt - seqs[si][0]) * T)
                            P.dma("pool", xpre_d[r:r + T, (cb - 10) * 512:(cb - 9) * 512], sgt, reads=[bs], writes=[B_xpre], sbuf=bs)
                        else:
                            dtt, bdt = dtst[j % 2]
                            P.op("dve", lambda e, dtt=dtt, psv=psv: e.tensor_tensor(out=dtt, in0=psv, in1=dtbb, op=ALU.add),
                                 reads=[P.pbuf[bank], b_dtbb], writes=[bdt])
                            P.op("dve", lambda e, dtt=dtt: e.tensor_scalar_min(out=dtt, in0=dtt, scalar1=60.0), reads=[bdt], writes=[bdt])
                            P.op("act", lambda e, dtt=dtt: e.activation(out=dtt, in_=dtt, func=AF.Exp), reads=[bdt], writes=[bdt])
                            P.op("act", lambda e, dtt=dtt: e.activation(out=dtt, in_=dtt, func=AF.Ln, bias=1.0, scale=1.0), reads=[bdt], writes=[bdt])
                            P.dma("pool", dt_d[rows, :], dtt, reads=[bdt], writes=[B_dt[tt]], sbuf=bdt)
            P.barrier()
            P.new_phase()
            P.top = mA

            mB = P.top
            ldbc, b_ld = P.alloc("ldbc", [16])
            Mret, b_Mret = P.alloc("Mret", [8, 128])
            rtab, b_rtab = P.alloc("rtab", [5, 16])
            Abc, b_Abc = P.alloc("Abc", [32])
            dskb, b_dsk = P.alloc("dskb", [16])
            cwbc, b_cw = P.alloc("cwbc", [3, 1536])
            cbbc, b_cb = P.alloc("cbbc", [1536])
            snwb, b_snw = P.alloc("snwb", [1024])
            mt = [P.alloc(f"mt{i}", [128]) for i in range(4)]
            P.dma("sp", ldbc, rld[l, :].partition_broadcast(128), reads=[d_const], writes=[b_ld], sbuf=b_ld)
            P.dma("sp", Abc, a_log[l, :].partition_broadcast(128), reads=[d_const], writes=[b_Abc], sbuf=b_Abc)
            P.dma("sp", dskb, d_skip[l, :].partition_broadcast(128), reads=[d_const], writes=[b_dsk], sbuf=b_dsk)
            P.dma("sp", cwbc, conv_w[l, :].partition_broadcast(128).rearrange("p (k c) -> p k c", k=3), reads=[d_const], writes=[b_cw], sbuf=b_cw)
            P.dma("sp", cbbc, conv_b[l, :].partition_broadcast(128), reads=[d_const], writes=[b_cb], sbuf=b_cb)
            P.dma("sp", snwb, ssd_nw[l, :].partition_broadcast(128), reads=[d_const], writes=[b_snw], sbuf=b_snw)
            P.op("act", lambda e: e.activation(out=Abc, in_=Abc, func=AF.Exp), reads=[b_Abc], writes=[b_Abc])
            P.op("dve", lambda e: e.tensor_scalar_mul(out=Abc, in0=Abc, scalar1=-1.0), reads=[b_Abc], writes=[b_Abc])
            P.op("act", lambda e: e.activation(out=rtab[:, 0, 0:8], in_=ldbc[:, 0:8], func=AF.Exp, scale=pidx[:, 0:1]), reads=[b_ld, b_pidx], writes=[b_rtab])
            P.op("act", lambda e: e.activation(out=rtab[:, 0, 8:16], in_=ldbc[:, 8:16], func=AF.Exp, scale=pidx[:, 2:3]), reads=[b_ld, b_pidx], writes=[b_rtab])
            P.op("dve", lambda e: e.tensor_scalar_mul(out=rtab[:, 0, :], in0=rtab[:, 0, :], scalar1=DKS), reads=[b_rtab], writes=[b_rtab])
            P.op("act", lambda e: e.activation(out=rtab[:, 1, 0:8], in_=ldbc[:, 0:8], func=AF.Exp, scale=pidx[:, 1:2]), reads=[b_ld, b_pidx], writes=[b_rtab])
            P.op("act", lambda e: e.activation(out=rtab[:, 1, 8:16], in_=ldbc[:, 8:16], func=AF.Exp, scale=pidx[:, 3:4]), reads=[b_ld, b_pidx], writes=[b_rtab])
            P.op("act", lambda e: e.activation(out=rtab[:, 2, :], in_=ldbc, func=AF.Exp, scale=float(T)), reads=[b_ld], writes=[b_rtab])
            (dpos, b_dpos), (dneg, b_dneg), (mtmp, b_mtmp), (mtmp2, b_mtmp2) = mt
            P.op("dve", lambda e: e.tensor_scalar_max(out=dpos, in0=diffm, scalar1=0.0), reads=[b_diff], writes=[b_dpos])
            P.op("dve", lambda e: e.tensor_scalar(out=dneg, in0=diffm, scalar1=-1.0, scalar2=0.0, op0=ALU.mult, op1=ALU.max), reads=[b_diff], writes=[b_dneg])
            for h in range(8):
                P.op("act", lambda e, h=h: e.activation(out=mtmp, in_=dpos, func=AF.Exp, scale=ldbc[:, h:h + 1]), reads=[b_dpos, b_ld], writes=[b_mtmp])
                P.op("act", lambda e, h=h: e.activation(out=mtmp2, in_=dneg, func=AF.Exp, scale=ldbc[:, 8 + h:9 + h]), reads=[b_dneg, b_ld], writes=[b_mtmp2])
                P.op("dve", lambda e: e.tensor_tensor(out=mtmp, in0=mtmp, in1=M_le, op=ALU.mult), reads=[b_mtmp, b_cmask], writes=[b_mtmp])
                P.op("dve", lambda e: e.tensor_tensor(out=mtmp2, in0=mtmp2, in1=M_ge, op=ALU.mult), reads=[b_mtmp2, b_cmask], writes=[b_mtmp2])
                P.op("dve", lambda e, h=h: e.scalar_tensor_tensor(out=Mret[:, h, :], in0=mtmp, scalar=DKS, in1=mtmp2, op0=ALU.mult, op1=ALU.add),
                     reads=[b_mtmp, b_mtmp2], writes=[b_Mret])
                P.op("dve", lambda e, h=h: e.scalar_tensor_tensor(out=Mret[:, h, :], in0=mtmp2, scalar=DKS - 1.0, in1=Mret[:, h, :], op0=ALU.mult, op1=ALU.add),
                     reads=[b_mtmp2, b_Mret], writes=[b_Mret])

            Sf = [P.alloc(f"Sf{i}", [1024]) for i in range(2)]
            Sb = [P.alloc(f"Sb{i}", [1024]) for i in range(2)]
            Sbb = [P.alloc(f"Sbb{i}", [1024], BF16) for i in range(2)]
            NB = 1
            qkt = [P.alloc(f"qkt{i}", [2048], BF16) for i in range(NB)]
            vt = [P.alloc(f"vt{i}", [1024], BF16) for i in range(NB)]
            xsh = [[P.alloc(f"xsh{i}_{k}", [1536], BF16) for k in range(3)] for i in range(1)]
            xpo = [P.alloc(f"xpo{i}", [1536], BF16) for i in range(NB)]
            dtb = [P.alloc(f"dtb{i}", [32]) for i in range(2)]
            sgz = [P.alloc(f"sgz{i}", [2048], BF16) for i in range(NB)]
            stfb = [P.alloc(f"stfb{i}", [2048], BF16) for i in range(NB)]
            cstb = [P.alloc(f"cstb{i}", [2048]) for i in range(NB)]
            cdbb = [P.alloc(f"cdbb{i}", [16]) for i in range(NB)]
            cacc, b_cacc = P.alloc("cacc", [1536])
            ct1, b_ct1 = P.alloc("ct1", [1536])
            ct2, b_ct2 = P.alloc("ct2", [1536])
            yw, b_yw = cacc[:, 0:1024], b_cacc
            yt, b_yt = ct1[:, 0:1024], b_ct1
            ysq, b_ysq = ct2[:, 0:1024], b_ct2
            av, b_av = P.alloc("av", [32])
            dec, b_dec = P.alloc("dec", [64])
            wgt, b_wgt = P.alloc("wgt", [32])
            xw = [P.alloc(f"xw{i}", [1024], BF16) for i in range(2)]
            kw = [P.alloc(f"kw{i}", [1024], BF16) for i in range(2)]
            stst = [P.alloc(f"stst{i}", [2048], BF16) for i in range(1)]
            cst_st = [P.alloc(f"cst_st{i}", [2048]) for i in range(1)]
            cdst = [P.alloc(f"cdst{i}", [16]) for i in range(2)]
            qT, b_qT = P.alloc("qT", [8, 128], BF16)
            kT, b_kT = P.alloc("kT", [8, 128], BF16)
            bcT, b_bcT = P.alloc("bcT", [4, 128], BF16)
            AT, b_AT = P.alloc("AT", [8, 128], BF16)
            Xd = [P.alloc(f"Xd{i}", [16, 128], BF16) for i in range(2)]
            ahl, b_ahl = P.alloc("ahl", [2, 32], BF16)
            cmb, b_cmb = P.alloc("cmb", [2, 128], BF16)
            P.op("dve", lambda e: e.tensor_copy(out=cmb[:, 0, :], in_=M_le), reads=[b_cmask], writes=[b_cmb])
            P.op("dve", lambda e: e.tensor_copy(out=cmb[:, 1, :], in_=M_ge), reads=[b_cmask], writes=[b_cmb])
            Ed = [P.alloc(f"Ed{i}", [16, 128], BF16) for i in range(2)]
            Md = Ed
            Gm, b_Gm = P.alloc("Gm", [4, 128], BF16)
            ysm, b_ysm = P.alloc("ysm", [32])
            yst = [P.alloc(f"yst{i}", [2048], BF16) for i in range(1)]

            cnt1 = [0]
            cnt2 = [0]
            def run_pass1(seq):
                (t0, nck, is_s, sidx) = seq
                for ty in range(2):
                    sv, bsv = Sf[ty]
                    if is_s:
                        src = (st_ret if ty == 0 else st_ssd)[l, 0]
                        P.dma("sp", sv, src, reads=[d_const], writes=[bsv], sbuf=bsv)
                    else:
                        P.op("pool", lambda e, sv=sv: e.memset(sv, 0.0), writes=[bsv])
                for c in range(nck):
                    tt = t0 + c
                    i = cnt1[0] % NB
                    cnt1[0] += 1
                    rows = slice(tt * T, (tt + 1) * T)
                    kt_, bkt = qkt[i]
                    vt_, bvt = vt[i]
                    dtt, bdtt = dtb[c % 2]
                    P.dma("sp", kt_[:, 1024:2048], qk_d[rows, 1024:2048], reads=[B_qk[tt]], writes=[bkt], sbuf=bkt)
                    P.dma("sp", vt_, v_d[rows, :], reads=[B_v[tt]], writes=[bvt], sbuf=bvt)
                    P.dma("sp", dtt, dt_d[rows, :], reads=[B_dt[tt]], writes=[bdtt], sbuf=bdtt)
                    r = xrow(sidx, c * T)
                    for k3 in range(3):
                        xv, bxv = xsh[0][k3]
                        P.dma("sp", xv, xpre_d[r - 1 + k3:r - 1 + k3 + T, :], reads=[B_xpre], writes=[bxv], sbuf=bxv)
                    P.op("dve", lambda e, xv=xsh[0][1][0]: e.tensor_tensor(out=cacc, in0=xv, in1=cwbc[:, 1, :], op=ALU.mult),
                         reads=[xsh[0][1][1], b_cw], writes=[b_cacc])
                    P.op("pool", lambda e, xv=xsh[0][0][0]: e.tensor_tensor(out=ct1, in0=xv, in1=cwbc[:, 0, :], op=ALU.mult),
                         reads=[xsh[0][0][1], b_cw], writes=[b_ct1])
                    P.op("pool", lambda e, xv=xsh[0][2][0]: e.tensor_tensor(out=ct2, in0=xv, in1=cwbc[:, 2, :], op=ALU.mult),
                         reads=[xsh[0][2][1], b_cw], writes=[b_ct2])
                    P.op("pool", lambda e: e.tensor_tensor(out=ct1, in0=ct1, in1=cbbc, op=ALU.add), reads=[b_ct1, b_cb], writes=[b_ct1])
                    P.op("dve", lambda e: e.tensor_tensor(out=cacc, in0=cacc, in1=ct2, op=ALU.add), reads=[b_cacc, b_ct2], writes=[b_cacc])
                    P.op("dve", lambda e: e.tensor_tensor(out=cacc, in0=cacc, in1=ct1, op=ALU.add), reads=[b_cacc, b_ct1], writes=[b_cacc])
                    xp_, bxp = xpo[i]
                    P.op("act", lambda e, xp_=xp_: e.activation(out=xp_, in_=cacc, func=AF.Silu), reads=[b_cacc], writes=[bxp])
                    P.dma("pool", xpost_d[rows, :], xp_, reads=[bxp], writes=[B_xpost[tt]], sbuf=bxp)
                    xs3 = xp_[:, 0:1024].rearrange("p (h c) -> p h c", h=16)
                    Btok = xp_[:, 1024:1280].rearrange("p (g n) -> p g n", g=2)
                    P.op("dve", lambda e, dtt=dtt: e.tensor_tensor(out=av, in0=dtt, in1=Abc, op=ALU.mult), reads=[bdtt, b_Abc], writes=[b_av])
                    bk = mm_bank()
                    pc = P.ps(bk)
                    P.op("pe", lambda e, pc=pc: e.matmul(pc[:, 0:16], lhsT=M_gt, rhs=av[:, 0:16], start=True, stop=True), reads=[b_cmask, b_av], writes=[P.pbuf[bk]], signal=False)
                    P.op("pe", lambda e, pc=pc: e.matmul(pc[:, 16:32], lhsT=M_lt, rhs=av[:, 16:32], start=True, stop=True), reads=[b_cmask, b_av], writes=[P.pbuf[bk]], signal=False)
                    P.op("pe", lambda e, pc=pc: e.matmul(pc[:, 32:64], lhsT=M_one, rhs=av[:, 0:32], start=True, stop=True), reads=[b_cmask, b_av], writes=[P.pbuf[bk]])
                    P.op("act", lambda e, pc=pc: e.activation(out=dec, in_=pc[:, 0:64], func=AF.Exp), reads=[P.pbuf[bk]], writes=[b_dec])
                    P.op("dve", lambda e, dtt=dtt: e.tensor_tensor(out=wgt, in0=dtt, in1=dec[:, 0:32], op=ALU.mult), reads=[bdtt, b_dec], writes=[b_wgt])
                    k3v = kt_[:, 1024:2048].rearrange("p (h n) -> p h n", h=8)
                    v3v = vt_.rearrange("p (h c) -> p h c", h=8)
                    for d in range(2):
                        xw_, bxw = xw[d]
                        kw_, bkw = kw[d]
                        P.op("dve", lambda e, d=d, xw_=xw_: e.tensor_tensor(out=xw_.rearrange("p (h c) -> p h c", h=16), in0=xs3,
                                                                          in1=bc_last(wgt[:, d * 16:(d + 1) * 16], 64), op=ALU.mult),
                             reads=[bxp, b_wgt], writes=[bxw])
                        P.op("pool", lambda e, d=d, kw_=kw_: e.tensor_tensor(out=kw_.rearrange("p (h n) -> p h n", h=8), in0=k3v,
                                                                           in1=bc_last(rtab[:, 1, d * 8:(d + 1) * 8], 128), op=ALU.mult),
                             reads=[bkt, b_rtab], writes=[bkw])
                    def chunk_state_mms(d):
                        res = {}
                        for ty in range(2):
                            b0, b1 = mm_bank(), mm_bank()
                            res[ty] = (b0, b1)
                            if ty == 0:
                                kw3 = kw[d][0].rearrange("p (h n) -> p h n", h=8)
                                for h in range(8):
                                    bnk = (b0, b1)[h // 4]
                                    P.op("pe", lambda e, h=h, bnk=bnk, kw3=kw3: e.matmul(P.ps(bnk)[:, (h % 4) * 128:(h % 4 + 1) * 128], lhsT=kw3[:, h, :],
                                                                                       rhs=v3v[:, h, :], start=True, stop=True),
                                         reads=[kw[d][1], bvt], writes=[P.pbuf[bnk]], signal=(h % 4 == 3))
                            else:
                                for g in range(2):
                                    bnk = (b0, b1)[g]
                                    P.op("pe", lambda e, g=g, bnk=bnk, d=d: e.matmul(P.ps(bnk), lhsT=Btok[:, g, :], rhs=xw[d][0][:, g * 512:(g + 1) * 512],
                                                                                    start=True, stop=True),
                                         reads=[bxp, xw[d][1]], writes=[P.pbuf[bnk]])
                        return res
                    sst, bsst = stst[0]
                    for ty in range(2):
                        sv, bsv = Sf[ty]
                        P.op("act", lambda e, ty=ty, sv=sv, sst=sst: e.copy(out=sst[:, ty * 1024:(ty + 1) * 1024], in_=sv), reads=[bsv], writes=[bsst])
                    P.dma("pool", stf_d[tt], sst, reads=[bsst], writes=[B_stf[tt]], sbuf=bsst)
                    psf = chunk_state_mms(0)
                    for ty in range(2):
                        sv, bsv = Sf[ty]
                        b0, b1 = psf[ty]
                        if ty == 0:
                            P.op("dve", lambda e, sv=sv: e.tensor_tensor(out=sv.rearrange("p (h c) -> p h c", h=8), in0=sv.rearrange("p (h c) -> p h c", h=8),
                                                                        in1=bc_last(rtab[:, 2, 0:8], 128), op=ALU.mult), reads=[bsv, b_rtab], writes=[bsv])
                        else:
                            P.op("dve", lambda e, sv=sv: e.tensor_tensor(out=sv.rearrange("p (h c) -> p h c", h=16), in0=sv.rearrange("p (h c) -> p h c", h=16),
                                                                        in1=bc_last(dec[:, 32:48], 64), op=ALU.mult), reads=[bsv, b_dec], writes=[bsv])
                        for hb_, bnk in enumerate((b0, b1)):
                            P.op("dve", lambda e, sv=sv, hb_=hb_, bnk=bnk: e.tensor_tensor(out=sv[:, hb_ * 512:(hb_ + 1) * 512], in0=sv[:, hb_ * 512:(hb_ + 1) * 512],
                                                                                       in1=P.ps(bnk), op=ALU.add), reads=[bsv, P.pbuf[bnk]], writes=[bsv])
                    psbk = chunk_state_mms(1)
                    cs_, bcs = cst_st[0]
                    for ty in range(2):
                        b0, b1 = psbk[ty]
                        for hb_, bnk in enumerate((b0, b1)):
                            P.op("act", lambda e, ty=ty, hb_=hb_, bnk=bnk, cs_=cs_: e.copy(out=cs_[:, ty * 1024 + hb_ * 512: ty * 1024 + (hb_ + 1) * 512], in_=P.ps(bnk)),
                                 reads=[P.pbuf[bnk]], writes=[bcs])
                    P.dma("pool", cstb_d[tt], cs_, reads=[bcs], writes=[B_cstb[tt]], sbuf=bcs)
                    cd_, bcd = cdst[c % 2]
                    P.op("act", lambda e, cd_=cd_: e.copy(out=cd_, in_=dec[:, 48:64]), reads=[b_dec], writes=[bcd])
                    P.dma("pool", cdb_d[tt], cd_, reads=[bcd], writes=[B_cdb[tt]], sbuf=bcd)
                if not is_s:
                    for ty in range(2):
                        sv, bsv = Sf[ty]
                        dst = (ns_ret if ty == 0 else ns_ssd)[sidx, l, 0]
                        P.dma("pool", dst, sv, reads=[bsv], writes=[B_out], sbuf=bsv)

                if is_s and EXCH:
                    for ty in range(2):
                        sv, bsv = Sf[ty]
                        P.dma("sp", exst_in[:, ty * 1024:(ty + 1) * 1024], sv, reads=[bsv], writes=[B_exin], sbuf=bsv)
                    P.collective("AllGather", PAIRS, exst_in, exst_out, reads=[B_exin], writes=[B_exout], inc=1)

            def run_pass2(seq):
                (t0, nck, is_s, sidx) = seq
                for ty in range(2):
                    sv, bsv = Sb[ty]
                    if is_s and EXCH:
                        ev, bev = cstb[0]
                        od, bod = cst_st[0]
                        P.dma("sp", ev[:, 0:1024], exst_out[0:128, ty * 1024:(ty + 1) * 1024], reads=[B_exout], writes=[bev], sbuf=bev)
                        P.dma("sp", od[:, 0:1024], exst_out[128:256, ty * 1024:(ty + 1) * 1024], reads=[B_exout], writes=[bod], sbuf=bod)
                        P.op("dve", lambda e, sv=sv, ev=ev: e.tensor_scalar(out=sv, in0=ev[:, 0:1024], scalar1=psel[:, 0:1], scalar2=None, op0=ALU.mult),
                             reads=[bev, b_psel], writes=[bsv])
                        P.op("dve", lambda e, sv=sv, od=od: e.scalar_tensor_tensor(out=sv, in0=od[:, 0:1024], scalar=psel[:, 1:2], in1=sv, op0=ALU.mult, op1=ALU.add),
                             reads=[bod, b_psel, bsv], writes=[bsv])
                    elif is_s:
                        src = (st_ret if ty == 0 else st_ssd)[l, 1]
                        P.dma("sp", sv, src, reads=[d_const], writes=[bsv], sbuf=bsv)
                    else:
                        P.op("pool", lambda e, sv=sv: e.memset(sv, 0.0), writes=[bsv])
                for ty in range(2):
                    P.op("act", lambda e, ty=ty: e.copy(out=Sbb[ty][0], in_=Sb[ty][0]), reads=[Sb[ty][1]], writes=[Sbb[ty][1]])
                for c in range(nck - 1, -1, -1):
                    tt = t0 + c
                    i = cnt2[0] % NB
                    cnt2[0] += 1
                    rows = slice(tt * T, (tt + 1) * T)
                    qk_, bqk = qkt[i]
                    vt_, bvt = vt[i]
                    dtt, bdtt = dtb[c % 2]
                    xp_, bxp = xpo[i]
                    sgz_, bsgz = sgz[i]
                    stf_, bstf = stfb[i]
                    csb_, bcsb = cstb[i]
                    cdb_, bcdb = cdbb[i]
                    P.dma("sp", qk_, qk_d[rows, :], reads=[B_qk[tt]], writes=[bqk], sbuf=bqk)
                    P.dma("sp", vt_, v_d[rows, :], reads=[B_v[tt]], writes=[bvt], sbuf=bvt)
                    P.dma("sp", dtt, dt_d[rows, :], reads=[B_dt[tt]], writes=[bdtt], sbuf=bdtt)
                    P.dma("sp", xp_, xpost_d[rows, :], reads=[B_xpost[tt]], writes=[bxp], sbuf=bxp)
                    P.dma("sp", sgz_[:, 0:1024], sg_d[rows, :], reads=[B_sg[tt]], writes=[bsgz], sbuf=bsgz)
                    P.dma("sp", sgz_[:, 1024:2048], sz_d[rows, :], reads=[B_sz[tt]], writes=[bsgz], sbuf=bsgz)
                    P.dma("sp", stf_, stf_d[tt], reads=[B_stf[tt]], writes=[bstf], sbuf=bstf)
                    P.dma("sp", csb_, cstb_d[tt], reads=[B_cstb[tt]], writes=[bcsb], sbuf=bcsb)
                    P.dma("sp", cdb_, cdb_d[tt], reads=[B_cdb[tt]], writes=[bcdb], sbuf=bcdb)
                    v3v = vt_.rearrange("p (h c) -> p h c", h=8)
                    xs3 = xp_[:, 0:1024].rearrange("p (h c) -> p h c", h=16)
                    transposes_to(qk_[:, 0:1024], bqk, 8, lambda i0, n: qT[:, i0:i0 + n, :], b_qT)
                    transposes_to(qk_[:, 1024:2048], bqk, 8, lambda i0, n: kT[:, i0:i0 + n, :], b_kT)
                    transposes_to(xp_[:, 1024:1536], bxp, 4, lambda i0, n: bcT[:, i0:i0 + n, :], b_bcT)
                    pb = (mm_bank(), mm_bank())
                    for h in range(8):
                        bnk = pb[h // 4]
                        P.op("pe", lambda e, h=h, bnk=bnk: e.matmul(P.ps(bnk)[:, (h % 4) * 128:(h % 4 + 1) * 128], lhsT=kT[:, h, :], rhs=qT[:, h, :],
                                                                  start=True, stop=True), reads=[b_kT, b_qT], writes=[P.pbuf[bnk]], signal=(h % 4 == 3))
                    for hb_ in range(2):
                        P.op("dve", lambda e, hb_=hb_: e.tensor_tensor(out=AT[:, hb_ * 4:(hb_ + 1) * 4, :], in0=P.ps(pb[hb_]).rearrange("p (h n) -> p h n", h=4),
                                                                     in1=Mret[:, hb_ * 4:(hb_ + 1) * 4, :], op=ALU.mult),
                             reads=[P.pbuf[pb[hb_]], b_Mret], writes=[b_AT])
                    pin = (mm_bank(), mm_bank())
                    pof = (mm_bank(), mm_bank())
                    for h in range(8):
                        bnk = pin[h // 4]
                        P.op("pe", lambda e, h=h, bnk=bnk: e.matmul(P.ps(bnk)[:, (h % 4) * 128:(h % 4 + 1) * 128], lhsT=AT[:, h, :], rhs=v3v[:, h, :],
                                                                  start=True, stop=True), reads=[b_AT, bvt], writes=[P.pbuf[bnk]], signal=(h % 4 == 3))
                    for h in range(8):
                        bnk = pof[h // 4]
                        P.op("pe", lambda e, h=h, bnk=bnk, stf_=stf_: e.matmul(P.ps(bnk)[:, (h % 4) * 128:(h % 4 + 1) * 128], lhsT=qT[:, h, :],
                                                                             rhs=stf_[:, h * 128:(h + 1) * 128], start=True, stop=True),
                             reads=[b_qT, bstf], writes=[P.pbuf[bnk]], signal=(h % 4 == 3))
                    yw3 = yw.rearrange("p (h c) -> p h c", h=8)
                    yt3 = yt.rearrange("p (h c) -> p h c", h=8)
                    for hb_ in range(2):
                        sl = slice(hb_ * 4, (hb_ + 1) * 4)
                        P.op("dve", lambda e, hb_=hb_, sl=sl: e.tensor_tensor(out=yt3[:, sl, :], in0=P.ps(pof[hb_]).rearrange("p (h n) -> p h n", h=4),
                                                                            in1=bc_last(rtab[:, 0, sl], 128), op=ALU.mult),
                             reads=[P.pbuf[pof[hb_]], b_rtab], writes=[b_yt])
                        P.op("dve", lambda e, hb_=hb_, sl=sl: e.tensor_tensor(out=yw3[:, sl, :], in0=P.ps(pin[hb_]).rearrange("p (h n) -> p h n", h=4),
                                                                            in1=yt3[:, sl, :], op=ALU.add),
                             reads=[P.pbuf[pin[hb_]], b_yt], writes=[b_yw])
                    pob = (mm_bank(), mm_bank())
                    for h in range(8):
                        bnk = pob[h // 4]
                        P.op("pe", lambda e, h=h, bnk=bnk: e.matmul(P.ps(bnk)[:, (h % 4) * 128:(h % 4 + 1) * 128], lhsT=qT[:, h, :],
                                                                  rhs=Sbb[0][0][:, h * 128:(h + 1) * 128], start=True, stop=True),
                             reads=[b_qT, Sbb[0][1]], writes=[P.pbuf[bnk]], signal=(h % 4 == 3))
                    for hb_ in range(2):
                        sl = slice(hb_ * 4, (hb_ + 1) * 4)
                        P.op("dve", lambda e, hb_=hb_, sl=sl: e.tensor_tensor(out=yt3[:, sl, :], in0=P.ps(pob[hb_]).rearrange("p (h n) -> p h n", h=4),
                                                                            in1=bc_last(rtab[:, 0, 8 + hb_ * 4:8 + (hb_ + 1) * 4], 128), op=ALU.mult),
                             reads=[P.pbuf[pob[hb_]], b_rtab], writes=[b_yt])
                    P.op("pool", lambda e: e.tensor_tensor(out=yw, in0=yw, in1=yt, op=ALU.add), reads=[b_yw, b_yt], writes=[b_yw])
                    P.op("act", lambda e: e.activation(out=ysq, in_=yw, func=AF.Square), reads=[b_yw], writes=[b_ysq])
                    P.op("dve", lambda e: e.tensor_reduce(out=ysm[:, 0:8], in_=ysq.rearrange("p (h c) -> p h c", h=8), axis=AX.X, op=ALU.add),
                         reads=[b_ysq], writes=[b_ysm])
                    rstd_of(ysm[:, 0:8], 128, 8, ysm[:, 8:16], ysm[:, 16:24], b_ysm, b_ysm, b_ysm)
                    yo_, byo = yst[0]
                    P.op("dve", lambda e: e.tensor_tensor(out=yw3, in0=yw3, in1=bc_last(ysm[:, 16:24], 128), op=ALU.mult), reads=[b_yw, b_ysm], writes=[b_yw])
                    P.op("pool", lambda e, yo_=yo_, sgz_=sgz_: e.tensor_tensor(out=yo_[:, 0:1024], in0=yw, in1=sgz_[:, 0:1024], op=ALU.mult),
                         reads=[b_yw, bsgz], writes=[byo])
                    P.op("dve", lambda e, dtt=dtt: e.tensor_tensor(out=av, in0=dtt, in1=Abc, op=ALU.mult), reads=[bdtt, b_Abc], writes=[b_av])
                    bk = mm_bank()
                    pc = P.ps(bk)
                    P.op("pe", lambda e, pc=pc: e.matmul(pc[:, 0:16], lhsT=M_le, rhs=av[:, 0:16], start=True, stop=True), reads=[b_cmask, b_av], writes=[P.pbuf[bk]], signal=False)
                    P.op("pe", lambda e, pc=pc: e.matmul(pc[:, 16:32], lhsT=M_ge, rhs=av[:, 16:32], start=True, stop=True), reads=[b_cmask, b_av], writes=[P.pbuf[bk]])
                    P.op("act", lambda e, pc=pc: e.activation(out=dec[:, 0:32], in_=pc[:, 0:32], func=AF.Exp), reads=[P.pbuf[bk]], writes=[b_dec])
                    for d in range(2):
                        xw_, bxw = xw[d]
                        P.op("dve", lambda e, d=d, xw_=xw_, dtt=dtt: e.tensor_tensor(out=xw_.rearrange("p (h c) -> p h c", h=16), in0=xs3,
                                                                                   in1=bc_last(dtt[:, d * 16:(d + 1) * 16], 64), op=ALU.mult),
                             reads=[bxp, bdtt], writes=[bxw])
                    bg = mm_bank()
                    for g in range(2):
                        P.op("pe", lambda e, g=g: e.matmul(P.ps(bg)[:, g * 128:(g + 1) * 128], lhsT=bcT[:, g, :], rhs=bcT[:, 2 + g, :], start=True, stop=True),
                             reads=[b_bcT], writes=[P.pbuf[bg]], signal=(g == 1))
                    P.op("dve", lambda e: e.tensor_tensor(out=Gm[:, 0:2, :], in0=P.ps(bg)[:, 0:256].rearrange("p (g n) -> p g n", g=2),
                                                          in1=bc_mid(M_le, 2), op=ALU.mult), reads=[P.pbuf[bg], b_cmask], writes=[b_Gm])
                    P.op("dve", lambda e: e.tensor_tensor(out=Gm[:, 2:4, :], in0=P.ps(bg)[:, 0:256].rearrange("p (g n) -> p g n", g=2),
                                                          in1=bc_mid(M_ge, 2), op=ALU.mult), reads=[P.pbuf[bg], b_cmask], writes=[b_Gm])
                    P.op("dve", lambda e: e.tensor_copy(out=ahl[:, 0, :], in_=av), reads=[b_av], writes=[b_ahl])
                    P.op("dve", lambda e: e.tensor_tensor(out=ahl[:, 1, :], in0=av, in1=ahl[:, 0, :], op=ALU.subtract), reads=[b_av, b_ahl], writes=[b_ahl])
                    for d in range(2):
                        Ed_, bEd = Ed[d]
                        Md_, bMd = Md[d]
                        um = cmb[:, d, :]
                        msk = M_gt if d == 0 else M_lt
                        for hl in range(2):
                            Xq, bXq = Xd[hl]
                            P.op("pool" if hl == 0 else "dve", lambda e, d=d, hl=hl, Xq=Xq, msk=msk: e.tensor_tensor(
                                out=Xq, in0=bc_last(ahl[:, hl, d * 16:(d + 1) * 16], 128), in1=bc_mid(msk, 16), op=ALU.mult),
                                reads=[b_ahl, b_cmask], writes=[bXq])
                        for q4 in range(4):
                            bnk = mm_bank()
                            for jj in range(4):
                                j = q4 * 4 + jj
                                for hl in range(2):
                                    P.op("pe", lambda e, j=j, jj=jj, bnk=bnk, hl=hl, um=um: e.matmul(P.ps(bnk)[:, jj * 128:(jj + 1) * 128], lhsT=Xd[hl][0][:, j, :], rhs=um,
                                                                                                   start=(hl == 0), stop=(hl == 1)),
                                         reads=[Xd[hl][1], b_cmb], writes=[P.pbuf[bnk]], signal=(jj == 3 and hl == 1))
                            P.op("act", lambda e, q4=q4, bnk=bnk, Ed_=Ed_: e.activation(out=Ed_[:, q4 * 4:(q4 + 1) * 4, :],
                                                                                       in_=P.ps(bnk).rearrange("p (h n) -> p h n", h=4), func=AF.Exp),
                                 reads=[P.pbuf[bnk]], writes=[bEd])
                        for g in range(2):
                            P.op("dve", lambda e, g=g, d=d, Ed_=Ed_, Md_=Md_: e.tensor_tensor(out=Md_[:, g * 8:(g + 1) * 8, :], in0=Ed_[:, g * 8:(g + 1) * 8, :],
                                                                                           in1=bc_mid(Gm[:, d * 2 + g, :], 8), op=ALU.mult),
                                 reads=[bEd, b_Gm], writes=[bMd])
                    pin = (mm_bank(), mm_bank())
                    for j in range(16):
                        bnk = pin[j // 8]
                        o = P.ps(bnk)[:, (j % 8) * 64:(j % 8 + 1) * 64]
                        P.op("pe", lambda e, j=j, o=o: e.matmul(o, lhsT=Md[0][0][:, j, :], rhs=xw[0][0][:, j * 64:(j + 1) * 64], start=True, stop=False),
                             reads=[Md[0][1], xw[0][1]], writes=[P.pbuf[bnk]], signal=False)
                        P.op("pe", lambda e, j=j, o=o: e.matmul(o, lhsT=Md[1][0][:, j, :], rhs=xw[1][0][:, j * 64:(j + 1) * 64], start=False, stop=True),
                             reads=[Md[1][1], xw[1][1]], writes=[P.pbuf[bnk]], signal=(j % 8 == 7))
                    yw16 = yw.rearrange("p (h c) -> p h c", h=16)
                    yt16 = yt.rearrange("p (h c) -> p h c", h=16)
                    for d in range(2):
                        po = (mm_bank(), mm_bank())
                        srcS = stf_[:, 1024:2048] if d == 0 else Sbb[1][0]
                        bsrc = bstf if d == 0 else Sbb[1][1]
                        for g in range(2):
                            P.op("pe", lambda e, g=g, po=po, srcS=srcS: e.matmul(P.ps(po[g]), lhsT=bcT[:, 2 + g, :], rhs=srcS[:, g * 512:(g + 1) * 512], start=True, stop=True),
                                 reads=[b_bcT, bsrc], writes=[P.pbuf[po[g]]])
                        for g in range(2):
                            sl = slice(g * 8, (g + 1) * 8)
                            P.op("dve", lambda e, g=g, d=d, sl=sl, po=po: e.tensor_tensor(out=yt16[:, sl, :], in0=P.ps(po[g]).rearrange("p (h c) -> p h c", h=8),
                                                                                        in1=bc_last(dec[:, d * 16 + g * 8:d * 16 + (g + 1) * 8], 64), op=ALU.mult),
                                 reads=[P.pbuf[po[g]], b_dec], writes=[b_yt])
                            if d == 0:
                                P.op("dve", lambda e, g=g, sl=sl: e.tensor_tensor(out=yw16[:, sl, :], in0=P.ps(pin[g]).rearrange("p (h c) -> p h c", h=8),
                                                                                in1=yt16[:, sl, :], op=ALU.add),
                                     reads=[P.pbuf[pin[g]], b_yt], writes=[b_yw])
                        if d == 1:
                            P.op("pool", lambda e: e.tensor_tensor(out=yw, in0=yw, in1=yt, op=ALU.add), reads=[b_yw, b_yt], writes=[b_yw])
                    P.op("dve", lambda e: e.tensor_tensor(out=yt16, in0=xs3, in1=bc_last(dskb, 64), op=ALU.mult), reads=[bxp, b_dsk], writes=[b_yt])
                    P.op("pool", lambda e: e.tensor_tensor(out=yw, in0=yw, in1=yt, op=ALU.add), reads=[b_yw, b_yt], writes=[b_yw])
                    P.op("dve", lambda e, sgz_=sgz_: e.tensor_tensor(out=yw, in0=yw, in1=sgz_[:, 1024:2048], op=ALU.mult), reads=[b_yw, bsgz], writes=[b_yw])
                    P.op("dve", lambda e: e.memset(ysm[:, 24:25], 0.0), writes=[b_ysm])
                    P.op("act", lambda e: e.activation(out=ysq, in_=yw, func=AF.Square, accum_out=ysm[:, 24:25]), reads=[b_yw], writes=[b_ysq, b_ysm])
                    rstd_of(ysm[:, 24:25], 1024, 1, ysm[:, 25:26], ysm[:, 26:27], b_ysm, b_ysm, b_ysm)
                    P.op("dve", lambda e, yo_=yo_: e.scalar_tensor_tensor(out=yo_[:, 1024:2048], in0=yw, scalar=ysm[:, 26:27], in1=snwb, op0=ALU.mult, op1=ALU.mult),
                         reads=[b_yw, b_ysm, b_snw], writes=[byo])
                    P.dma("pool", y_d[rows, :], yo_, reads=[byo], writes=[B_y[tt]], sbuf=byo)
                    for ty in range(2):
                        sv, bsv = Sb[ty]
                        if ty == 0:
                            P.op("dve", lambda e, sv=sv: e.tensor_tensor(out=sv.rearrange("p (h c) -> p h c", h=8), in0=sv.rearrange("p (h c) -> p h c", h=8),
                                                                        in1=bc_last(rtab[:, 2, 8:16], 128), op=ALU.mult), reads=[bsv, b_rtab], writes=[bsv])
                        else:
                            P.op("dve", lambda e, sv=sv, cdb_=cdb_: e.tensor_tensor(out=sv.rearrange("p (h c) -> p h c", h=16), in0=sv.rearrange("p (h c) -> p h c", h=16),
                                                                                   in1=bc_last(cdb_, 64), op=ALU.mult), reads=[bsv, bcdb], writes=[bsv])
                        P.op("pool", lambda e, sv=sv, ty=ty, csb_=csb_: e.tensor_tensor(out=sv, in0=sv, in1=csb_[:, ty * 1024:(ty + 1) * 1024], op=ALU.add),
                             reads=[bsv, bcsb], writes=[bsv])
                        P.op("act", lambda e, ty=ty, sv=sv: e.copy(out=Sbb[ty][0], in_=sv), reads=[bsv], writes=[Sbb[ty][1]])
                if not is_s:
                    for ty in range(2):
                        sv, bsv = Sb[ty]
                        dst = (ns_ret if ty == 0 else ns_ssd)[sidx, l, 1]
                        P.dma("pool", dst, sv, reads=[bsv], writes=[B_out], sbuf=bsv)

            if EXCH:
                rl = xrow(NSEQ - 1, CS * T - 1)
                P.dma("sp", exrow_in, xpre_d[rl:rl + 1, :], reads=[B_xpre], writes=[B_exrin], sem_key=("x", "exrow"))
                P.collective("AllGather", PAIRS, exrow_in, exrow_out, reads=[B_exrin], writes=[B_exrout], inc=1)
                r0_, b0_ = xsh[0][0]
                r1_, b1_ = xsh[0][1]
                r2_, b2_ = xsh[0][2]
                P.dma("sp", r0_[0:1, :], exrow_out[0:1, :], reads=[B_exrout], writes=[b0_], sbuf=b0_)
                P.dma("sp", r1_[0:1, :], exrow_out[1:2, :], reads=[B_exrout], writes=[b1_], sbuf=b1_)
                P.op("dve", lambda e: e.tensor_scalar(out=r2_[0:1, :], in0=r0_[0:1, :], scalar1=psel[0:1, 0:1], scalar2=None, op0=ALU.mult),
                     reads=[b0_, b_psel], writes=[b2_])
                P.op("dve", lambda e: e.scalar_tensor_tensor(out=r2_[0:1, :], in0=r1_[0:1, :], scalar=psel[0:1, 1:2], in1=r2_[0:1, :], op0=ALU.mult, op1=ALU.add),
                     reads=[b1_, b_psel, b2_], writes=[b2_])
                P.dma("sp", xpre_d[rl + 1:rl + 2, :], r2_[0:1, :], reads=[b2_], writes=[B_xpre], sbuf=b2_)
                run_pass1(seqs[-1])
                for sq in seqs[:-1]:
                    run_pass1(sq)
                    run_pass2(sq)
                run_pass2(seqs[-1])
            else:
                for sq in seqs:
                    run_pass1(sq)
                    run_pass2(sq)
            P.barrier()
            P.new_phase()
            P.top = mB

            mC = P.top
            hT, b_hT = P.alloc("hTc", [16, 512], BF16)
            uT, b_uT = P.alloc("uT", [64, 512], BF16)
            xg, b_xg0 = P.alloc("xg", [4, D])
            b_xg = [Buf(f"xg{j}") for j in range(4)]
            modc = [P.alloc(f"modc{i}", [D], BF16) for i in range(4)]
            modl, b_modl = P.alloc("modl", [D])
            yb = [P.alloc(f"yb{i}", [D], BF16) for i in range(1)]
            h2b, b_h2b = P.alloc("h2b", [D], BF16)
            tmpc = [P.alloc(f"tmpc{i}", [512]) for i in range(2)]
            junk, b_junk = h2b, b_h2b
            small, b_small = P.alloc("smallC", [8])
            if last:
                fnb, b_fnb = P.alloc("fnb", [D], BF16)
                P.dma("sp", modl, fnw[0, :].partition_broadcast(128), reads=[d_const], writes=[b_modl], sbuf=b_modl)
                P.op("dve", lambda e: e.tensor_copy(out=fnb, in_=modl), reads=[b_modl], writes=[b_fnb])
            cur_ci = -1
            tn = 0
            for (tiles, ci) in groups:
                G = len(tiles) * T
                if ci != cur_ci:
                    cur_ci = ci
                    for k, slot in enumerate((2, 4, 3, 5)):
                        P.dma("sp", modl, modrow(l, ci, slot).partition_broadcast(128), reads=[B_mod], writes=[b_modl], sbuf=b_modl)
                        P.op("dve", lambda e, k=k: e.tensor_copy(out=modc[k][0], in_=modl), reads=[b_modl], writes=[modc[k][1]])
                (g1, bg1), (a2, ba2), (sh2, bsh2), (g2, bg2) = modc
                for j, tt in enumerate(tiles):
                    yv, byv = yb[0]
                    P.dma("sp", yv, y_d[tt * T:(tt + 1) * T, :], reads=[B_y[tt]], writes=[byv], sbuf=byv)
                    P.dma("sp", xg[:, j, :], x_src[tt * T:(tt + 1) * T, :], reads=[B_xsrc(tt)], writes=[b_xg[j]], sbuf=b_xg[j])
                    transposes_to(yv, byv, 16, lambda i, n, j=j: hT[:, i:i + n, j * T:(j + 1) * T], b_hT)
                for cb in range(4):
                    wt, bw = next_w()
                    for j, tt in enumerate(tiles):
                        bank = mm_bank()
                        for kc in range(16):
                            P.op("pe", lambda e, kc=kc, j=j, wt=wt, bank=bank: e.matmul(P.ps(bank), lhsT=hT[:, kc, j * T:(j + 1) * T], rhs=wt[:, kc, :],
                                                                                       start=(kc == 0), stop=(kc == 15)),
                                 reads=[b_hT, bw], writes=[P.pbuf[bank]], signal=(kc == 15))
                        tm, btm = tmpc[tn % 2]
                        tn += 1
                        P.op("dve", lambda e, tm=tm, bank=bank, cb=cb: e.tensor_tensor(out=tm, in0=P.ps(bank), in1=g1[:, cb * 512:(cb + 1) * 512], op=ALU.mult),
                             reads=[P.pbuf[bank], bg1], writes=[btm])
                        P.op("pool", lambda e, tm=tm, j=j, cb=cb: e.tensor_tensor(out=xg[:, j, cb * 512:(cb + 1) * 512], in0=xg[:, j, cb * 512:(cb + 1) * 512], in1=tm, op=ALU.add),
                             reads=[btm, b_xg[j]], writes=[b_xg[j]])
                for j, tt in enumerate(tiles):
                    xj = xg[:, j, :]
                    P.op("dve", lambda e: e.memset(small[:, 0:1], 0.0), writes=[b_small])
                    P.op("act", lambda e, xj=xj: e.activation(out=junk, in_=xj, func=AF.Square, accum_out=small[:, 0:1]), reads=[b_xg[j]], writes=[b_junk, b_small])
                    rstd_of(small[:, 0:1], D, 1, small[:, 1:2], small[:, 2:3], b_small, b_small, b_small)
                    P.op("dve", lambda e, xj=xj: e.scalar_tensor_tensor(out=modl, in0=xj, scalar=small[:, 2:3], in1=a2, op0=ALU.mult, op1=ALU.mult),
                         reads=[b_xg[j], b_small, ba2], writes=[b_modl])
                    P.op("pool", lambda e: e.tensor_tensor(out=h2b, in0=modl, in1=sh2, op=ALU.add), reads=[b_modl, bsh2], writes=[b_h2b])
                    transposes_to(h2b, b_h2b, 16, lambda i, n, j=j: hT[:, i:i + n, j * T:(j + 1) * T], b_hT)
                for fb in range(16):
                    if fb % 2 == 0:
                        pump_convert(2)
                    wt, bw = next_w()
                    for fc in range(4):
                        bank = mm_bank()
                        for kc in range(16):
                            P.op("pe", lambda e, kc=kc, fc=fc, wt=wt, bank=bank, G=G: e.matmul(P.ps(bank)[:, 0:G], lhsT=wt[:, kc, fc * 128:(fc + 1) * 128], rhs=hT[:, kc, 0:G],
                                                                                              start=(kc == 0), stop=(kc == 15)),
                                 reads=[b_hT, bw], writes=[P.pbuf[bank]], signal=(kc == 15))
                        P.op("act" if fc % 2 == 0 else "dve",
                             (lambda e, bank=bank, fb=fb, fc=fc, G=G: e.activation(out=uT[:, fb * 4 + fc, 0:G], in_=P.ps(bank)[:, 0:G], func=AF.Relu)) if fc % 2 == 0 else
                             (lambda e, bank=bank, fb=fb, fc=fc, G=G: e.tensor_scalar_max(out=uT[:, fb * 4 + fc, 0:G], in0=P.ps(bank)[:, 0:G], scalar1=0.0)),
                             reads=[P.pbuf[bank]], writes=[b_uT])
                        P.op("pool", lambda e, fb=fb, fc=fc, G=G: e.tensor_tensor(out=uT[:, fb * 4 + fc, 0:G], in0=uT[:, fb * 4 + fc, 0:G], in1=uT[:, fb * 4 + fc, 0:G], op=ALU.mult),
                             reads=[b_uT], writes=[b_uT])
                for cb in range(4):
                    banks = [2 + ((cb * 4 + j) % 6) for j in range(len(tiles))]
                    for kp in range(4):
                        wt, bw = next_w()
                        for j, tt in enumerate(tiles):
                            bank = banks[j]
                            for kc in range(16):
                                P.op("pe", lambda e, kc=kc, j=j, wt=wt, bank=bank, kp=kp: e.matmul(P.ps(bank), lhsT=uT[:, kp * 16 + kc, j * T:(j + 1) * T], rhs=wt[:, kc, :],
                                                                                                 start=(kp == 0 and kc == 0), stop=(kp == 3 and kc == 15)),
                                     reads=[b_uT, bw], writes=[P.pbuf[bank]], signal=(kc == 15))
                    for j, tt in enumerate(tiles):
                        bank = banks[j]
                        tm, btm = tmpc[tn % 2]
                        tn += 1
                        P.op("dve", lambda e, tm=tm, bank=bank, cb=cb: e.tensor_tensor(out=tm, in0=P.ps(bank), in1=g2[:, cb * 512:(cb + 1) * 512], op=ALU.mult),
                             reads=[P.pbuf[bank], bg2], writes=[btm])
                        P.op("pool", lambda e, tm=tm, j=j, cb=cb: e.tensor_tensor(out=xg[:, j, cb * 512:(cb + 1) * 512], in0=xg[:, j, cb * 512:(cb + 1) * 512], in1=tm, op=ALU.add),
                             reads=[btm, b_xg[j]], writes=[b_xg[j]])
                for j, tt in enumerate(tiles):
                    xj = xg[:, j, :]
                    if not last:
                        P.dma("pool", xs_d[tt * T:(tt + 1) * T, :], xj, reads=[b_xg[j]], writes=[B_xs[tt]], sbuf=b_xg[j])
                    else:
                        P.op("dve", lambda e: e.memset(small[:, 0:1], 0.0), writes=[b_small])
                        P.op("act", lambda e, xj=xj: e.activation(out=junk, in_=xj, func=AF.Square, accum_out=small[:, 0:1]), reads=[b_xg[j]], writes=[b_junk, b_small])
                        rstd_of(small[:, 0:1], D, 1, small[:, 1:2], small[:, 2:3], b_small, b_small, b_small)
                        P.op("dve", lambda e, xj=xj: e.scalar_tensor_tensor(out=xj, in0=xj, scalar=small[:, 2:3], in1=fnb, op0=ALU.mult, op1=ALU.mult),
                             reads=[b_xg[j], b_small, b_fnb], writes=[b_xg[j]])
                        P.dma("pool", y_out[tt * T:(tt + 1) * T, :], xj, reads=[b_xg[j]], writes=[B_out], sbuf=b_xg[j])
            pump_convert(10 ** 6)
            P.barrier()
            P.new_phase()
            P.top = mC

        P.barrier()
        print("instruction counts", P.n_inst, "sems", P.nsem)
        P.emit()
    return nc


def _consts():
    s = np.arange(128)
    tt, ss = s[:, None], s[None, :]
    cm = np.stack([(tt > ss), (tt <= ss), (tt < ss), (tt >= ss), np.ones((128, 128), bool)], 1).astype(np.float32)
    diff = (ss - tt).astype(np.float32)
    pidx = np.stack([s + 1, 128 - 1 - s, 128 - s, s], 1).astype(np.float32)
    return cm, diff, pidx


def _rope(length):
    GRID_W = 64
    pos = np.arange(length)
    row = (pos // GRID_W).astype(np.float32)
    col = (pos % GRID_W).astype(np.float32)
    half = 64
    inv = (1.0 / (np.float32(10000.0) ** (np.arange(0, half, 2, dtype=np.float32) / np.float32(half)))).astype(np.float32)
    ang = np.concatenate([row[:, None] * inv, col[:, None] * inv], -1).astype(np.float32)
    return np.concatenate([np.cos(ang), np.sin(ang)], -1).astype(np.float32)


_NC_CACHE = {}
DEBUG_SCRATCH = False
_LAST_RES = [None]


def kernel(x_prompt, x_sample, state_ret, state_ssd, c, c_ctx, w_ada, b_ada, norm1_w, w_in,
           ret_log_decay, conv_w, conv_b, dt_bias, a_log, d_skip, ssd_norm_w, w_out, norm2_w,
           w_ff1, w_ff2, final_norm_w, _n_cores=8):
    f = lambda a: np.ascontiguousarray(np.asarray(a, dtype=np.float32))
    x_prompt, x_sample, state_ret, state_ssd, c, c_ctx = map(f, (x_prompt, x_sample, state_ret, state_ssd, c, c_ctx))
    BP, PL, _ = x_prompt.shape
    BS, SLEN, _ = x_sample.shape
    DEPTH = w_ada.shape[0]
    n_cores = _n_cores
    EXCH = (n_cores == 2 * BS) and (BP % n_cores == 0) and (SLEN % 256 == 0)
    if EXCH:
        n_work = n_cores
        NP = BP // n_cores
        SL = SLEN // 2
    else:
        n_work = BS
        NP = BP // n_work
        SL = SLEN
    key = (DEPTH, NP, PL, SL, EXCH)
    if key not in _NC_CACHE:
        _NC_CACHE[key] = build_program(DEPTH, NP, PL, SL, EXCH)
    nc = _NC_CACHE[key]
    cm, diff, pidx = _consts()
    rope = _rope(SLEN)
    w_in = f(w_in)
    rld = f(ret_log_decay)
    cw = f(conv_w)
    dtb = f(dt_bias)
    alg = f(a_log)
    base = dict(
        cmask=cm, diffm=diff, pidx=pidx, zrow=np.zeros((1, 1536), ml_dtypes.bfloat16),
        conv_b=f(conv_b), d_skip=f(d_skip),
        ssd_norm_w=f(ssd_norm_w), w_out=f(w_out), w_ff1=f(w_ff1), w_ff2=f(w_ff2),
        final_norm_w=f(final_norm_w).reshape(1, D),
    )
    w_ada = f(w_ada)
    b_ada = f(b_ada)
    if EXCH:
        ada_half = [dict(w_ada_h=np.ascontiguousarray(w_ada[:, :, h * 3 * D:(h + 1) * 3 * D]),
                         b_ada_h=np.ascontiguousarray(b_ada[:, h * 3 * D:(h + 1) * 3 * D]),
                         nw_h=f(norm1_w) if h == 0 else f(norm2_w)) for h in range(2)]
    else:
        base.update(w_ada=w_ada, b_ada=b_ada, norm1_w=f(norm1_w), norm2_w=f(norm2_w))
    variants = {}
    for flip in ((False, True) if EXCH else (False,)):
        if not flip:
            v = dict(w_in=w_in, ret_log_decay=rld.reshape(DEPTH, 16), conv_w=cw.reshape(DEPTH, 3 * 1536),
                     dt_bias=dtb.reshape(DEPTH, 32), a_log=alg.reshape(DEPTH, 32))
        else:
            w2 = w_in.copy()
            w2[:, :, 6656:6672] = w_in[:, :, 6672:6688]
            w2[:, :, 6672:6688] = w_in[:, :, 6656:6672]
            v = dict(w_in=w2, ret_log_decay=np.ascontiguousarray(rld[:, ::-1]).reshape(DEPTH, 16),
                     conv_w=np.ascontiguousarray(cw[:, ::-1]).reshape(DEPTH, 3 * 1536),
                     dt_bias=np.ascontiguousarray(dtb[:, ::-1]).reshape(DEPTH, 32),
                     a_log=np.ascontiguousarray(alg[:, ::-1]).reshape(DEPTH, 32))
        variants[flip] = v
    tr = lambda s_: np.ascontiguousarray(s_.transpose(0, 1, 4, 2, 3)).reshape(DEPTH, 2, 128, 1024)
    in_maps = []
    meta = []
    for core in range(n_cores):
        if EXCH:
            b, half = core // 2, core % 2
            flip = (half == 1)
            plist = list(range(core * NP, (core + 1) * NP))
            xs_ = x_sample[b, half * SL:(half + 1) * SL]
            pos = np.arange(half * SL, (half + 1) * SL)
            sr, ss = tr(state_ret[b]), tr(state_ssd[b])
            if flip:
                xs_ = xs_[::-1]
                pos = pos[::-1]
                sr, ss = np.ascontiguousarray(sr[:, ::-1]), np.ascontiguousarray(ss[:, ::-1])
            xps = [x_prompt[p][::-1] if flip else x_prompt[p] for p in plist]
            psel = np.zeros((128, 2), np.float32)
            psel[:, 1 - half] = 1.0
        else:
            w = core % n_work
            b, half, flip = w, 0, False
            plist = list(range(w * NP, (w + 1) * NP))
            xs_ = x_sample[b]
            pos = np.arange(SLEN)
            sr, ss = tr(state_ret[b]), tr(state_ssd[b])
            xps = [x_prompt[p] for p in plist]
            psel = np.zeros((128, 2), np.float32)
        m = dict(base)
        m.update(variants[flip])
        if EXCH:
            m.update(ada_half[half])
        m.update(x_in=np.ascontiguousarray(np.concatenate(xps + [xs_], 0)), cond=np.ascontiguousarray(np.stack([c_ctx, c[b]], 0)),
                 st_ret=sr, st_ssd=ss, rope_cs=np.ascontiguousarray(rope[pos]), psel=psel)
        in_maps.append(m)
        meta.append((b, half, flip, plist))
    res = run_bass_kernel_spmd(nc, in_maps, core_ids=list(range(n_cores)))
    _LAST_RES[0] = res
    y_prompt = np.zeros((BP, PL, D), np.float32)
    y_sample = np.zeros((BS, SLEN, D), np.float32)
    nsr = np.zeros((BP, DEPTH, 2, 8, 128, 128), np.float32)
    nss = np.zeros((BP, DEPTH, 2, 16, 64, 128), np.float32)
    for core in range(n_work):
        b, half, flip, plist = meta[core]
        r = res.results[core]
        yo = r["y_out"]
        a = r["ns_ret"].reshape(NP, DEPTH, 2, 128, 8, 128).transpose(0, 1, 2, 4, 5, 3)
        bb = r["ns_ssd"].reshape(NP, DEPTH, 2, 128, 16, 64).transpose(0, 1, 2, 4, 5, 3)
        for i, p in enumerate(plist):
            yp = yo[i * PL:(i + 1) * PL]
            y_prompt[p] = yp[::-1] if flip else yp
            nsr[p] = a[i][:, ::-1] if flip else a[i]
            nss[p] = bb[i][:, ::-1] if flip else bb[i]
        ys = yo[NP * PL:]
        y_sample[b, half * SL:(half + 1) * SL] = ys[::-1] if flip else ys
    return (y_prompt, y_sample, nsr, nss)
```

```python
import types
import numpy as np
import ml_dtypes
from contextlib import ExitStack
import concourse.bass as bass
import concourse.mybir as mybir
from concourse.bass_utils import run_bass_kernel_spmd

F32 = mybir.dt.float32
BF16 = mybir.dt.bfloat16
ALU = mybir.AluOpType
AF = mybir.ActivationFunctionType
AX = mybir.AxisListType

EPOCH = 30000


class Tok:
    __slots__ = ("key", "val", "eng")

    def __init__(self, eng):
        self.key = None
        self.val = None
        self.eng = eng


class Buf:
    __slots__ = ("name", "w", "r", "dsem")

    def __init__(self, name):
        self.name = name
        self.w = None
        self.r = []
        self.dsem = None


class Prog:
    CE = ("pe", "act", "dve", "pool")
    ALLE = ("pe", "act", "dve", "pool", "sp")

    def __init__(self, nc, stack, arena_words=53200):
        self.nc = nc
        self.stack = stack
        self.ops = {e: [] for e in self.ALLE}
        self.sems = {}
        self.ecount = {e: 0 for e in self.CE}
        self.waited = {e: {} for e in self.ALLE}
        self.pe_pending = []
        self.pe_last_rec = None
        self.dma_sems = {}
        self.nsem = 0
        self.AW = arena_words
        self.arena = stack.enter_context(nc.sbuf_tensor("arena", [128, arena_words], F32))
        self.top = 0
        self.psum = [stack.enter_context(nc.psum_tensor(f"psb{i}", [128, 512], F32)) for i in range(8)]
        self.pbuf = [Buf(f"psum{i}") for i in range(8)]
        self.n_inst = {e: 0 for e in self.ALLE}
        self.dsem_ctr = 0
        self.dsem_base = 0

    def pin(self, buf):
        buf.dsem = ("d", self.dsem_ctr)
        self.dsem_ctr += 1
        self.dsem_base = self.dsem_ctr

    def new_phase(self):
        self.dsem_ctr = self.dsem_base

    def alloc(self, name, free_shape, dtype=F32):
        n = int(np.prod(free_shape))
        nw = n if dtype == F32 else (n + 1) // 2
        nw = (nw + 7) // 8 * 8
        off = self.top
        self.top += nw
        assert self.top <= self.AW, f"arena overflow at {name}: {self.top}"
        v = self.arena[:, off:off + nw]
        if dtype != F32:
            v = v.bitcast(dtype)
        v = v[:, 0:n]
        if len(free_shape) == 2:
            v = v.rearrange("p (a b) -> p a b", a=free_shape[0])
        elif len(free_shape) == 3:
            v = v.rearrange("p (a b c) -> p a b c", a=free_shape[0], b=free_shape[1])
        return v, Buf(name)

    def ps(self, i, dtype=F32):
        v = self.psum[i][:, :]
        if dtype != F32:
            v = v.bitcast(dtype)
        return v

    def _sem(self, key):
        if key not in self.sems:
            self.nsem += 1
            self.sems[key] = self.stack.enter_context(self.nc.semaphore(f"s{self.nsem}"))
        return self.sems[key]

    def _new_signal(self, eng):
        c = self.ecount[eng]
        self.ecount[eng] = c + 1
        key = ("e", eng, c // EPOCH)
        self._sem(key)
        return key, (c % EPOCH) + 1

    def _need(self, eng, tok, raw):
        if tok is None:
            return
        if tok.eng == eng and eng in self.CE and not raw:
            return
        if tok.key is None:
            self._force_pe_signal()
        w = self.waited[eng]
        if w.get(tok.key, 0) >= tok.val:
            return
        w[tok.key] = tok.val
        sem = self.sems[tok.key]
        val = tok.val
        self.ops[eng].append(lambda e, sem=sem, val=val: e.wait_ge(sem, val))
        self.n_inst[eng] += 1

    def _force_pe_signal(self):
        rec = self.pe_last_rec
        assert rec["sig"] is None
        key, val = self._new_signal("pe")
        rec["sig"] = (self.sems[key], 1)
        for t in self.pe_pending:
            t.key, t.val = key, val
        self.pe_pending = []

    def _deps(self, eng, reads, writes):
        for b in reads:
            self._need(eng, b.w, True)
        for b in writes:
            self._need(eng, b.w, False)
            for t in b.r:
                self._need(eng, t, False)

    def _commit(self, tok, reads, writes):
        for b in reads:
            if len(b.r) > 24:
                d = {}
                rest = []
                for t in b.r:
                    if t.key is None:
                        rest.append(t)
                    elif t.key not in d or d[t.key].val < t.val:
                        d[t.key] = t
                b.r = rest + list(d.values())
            b.r.append(tok)
        for b in writes:
            b.w = tok
            b.r = []

    @staticmethod
    def _freeze(fn):
        if fn.__closure__ is None:
            return fn
        cells = []
        for c in fn.__closure__:
            try:
                cells.append(types.CellType(c.cell_contents))
            except ValueError:
                cells.append(c)
        g = types.FunctionType(fn.__code__, fn.__globals__, fn.__name__, fn.__defaults__, tuple(cells))
        g.__kwdefaults__ = fn.__kwdefaults__
        return g

    def op(self, eng, fn, reads=(), writes=(), signal=True):
        fn = self._freeze(fn)
        self._deps(eng, reads, writes)
        tok = Tok(eng)
        rec = {"sig": None}
        if signal:
            key, val = self._new_signal(eng)
            tok.key, tok.val = key, val
            rec["sig"] = (self.sems[key], 1)
            if eng == "pe":
                for t in self.pe_pending:
                    t.key, t.val = key, val
                self.pe_pending = []
        else:
            assert eng == "pe"
            self.pe_pending.append(tok)
        if eng == "pe":
            self.pe_last_rec = rec

        def run(e, fn=fn, rec=rec):
            ins = fn(e)
            if rec["sig"] is not None:
                ins.then_inc(rec["sig"][0], rec["sig"][1])
        self.ops[eng].append(run)
        self.n_inst[eng] += 1
        self._commit(tok, reads, writes)
        return tok

    def dma(self, q, out, in_, reads=(), writes=(), sbuf=None, sem_key=None, **kw):
        self._deps(q, reads, writes)
        if sem_key is None:
            if sbuf.dsem is None:
                sbuf.dsem = ("d", self.dsem_ctr)
                self.dsem_ctr += 1
            sem_key = sbuf.dsem
        self._sem(sem_key)
        cnt = self.dma_sems.get(sem_key, 0) + 16
        self.dma_sems[sem_key] = cnt
        tok = Tok(None)
        tok.key, tok.val = sem_key, cnt
        sem = self.sems[sem_key]

        def run(e, out=out, in_=in_, sem=sem, kw=kw):
            e.dma_start(out=out, in_=in_, **kw).then_inc(sem, 16)
        self.ops[q].append(run)
        self.n_inst[q] += 1
        self._commit(tok, reads, writes)
        return tok

    def collective(self, kind, groups, in_ap, out_ap, reads=(), writes=(), inc=16):
        q = "pool"
        self._deps(q, reads, writes)
        sem_key = ("cc",)
        self._sem(sem_key)
        cnt = self.dma_sems.get(sem_key, 0) + inc
        self.dma_sems[sem_key] = cnt
        tok = Tok(None)
        tok.key, tok.val = sem_key, cnt
        sem = self.sems[sem_key]

        def run(e, kind=kind, groups=groups, in_ap=in_ap, out_ap=out_ap, sem=sem, inc=inc):
            e.collective_compute(kind, ALU.bypass, replica_groups=groups, ins=[in_ap], outs=[out_ap]).then_inc(sem, inc)
        self.ops[q].append(run)
        self.n_inst[q] += 1
        self._commit(tok, reads, writes)
        return tok

    def barrier(self):
        toks = []
        for e in self.CE:
            if e == "pe" and self.pe_pending:
                self._force_pe_signal()
            c = self.ecount[e]
            if c > 0:
                t = Tok(None)
                t.key = ("e", e, (c - 1) // EPOCH)
                t.val = ((c - 1) % EPOCH) + 1
                toks.append(t)
        for k, cnt in self.dma_sems.items():
            t = Tok(None)
            t.key, t.val = k, cnt
            toks.append(t)
        for e in self.ALLE:
            for t in toks:
                self._need(e, t, True)

    def emit(self):
        ops = self.ops
        with self.nc.Block() as block:
            @block.tensor
            def _(e):
                for f in ops["pe"]:
                    f(e)

            @block.scalar
            def _(e):
                for f in ops["act"]:
                    f(e)

            @block.vector
            def _(e):
                for f in ops["dve"]:
                    f(e)

            @block.gpsimd
            def _(e):
                for f in ops["pool"]:
                    f(e)

            @block.sync
            def _(e):
                for f in ops["sp"]:
                    f(e)


D = 2048
DIN = 6688
HR = 8
HS = 16
PSD = 64
T = 128
DFF = 8192
EPS = 1e-6
DKS = 128 ** -0.5


def bc_last(ap, n):
    return ap.unsqueeze(2).broadcast_to([ap.shape[0], ap.shape[1], n])


def bc_mid(ap, n):
    return ap.unsqueeze(1).broadcast_to([ap.shape[0], n, ap.shape[1]])


def build_program(DEPTH, NP, PL, SL, EXCH=False):
    PAIRS = [[0, 1], [2, 3], [4, 5], [6, 7]]
    NTOK = NP * PL + SL
    NT = NTOK // T
    CP = PL // T
    CS = SL // T
    seqs = [(i * CP, CP, False, i) for i in range(NP)] + [(NP * CP, CS, True, NP)]
    NSEQ = len(seqs)
    groups = []
    pt = list(range(NP * CP))
    for i in range(0, len(pt), 4):
        groups.append((pt[i:i + 4], 0))
    stl = list(range(NP * CP, NT))
    for i in range(0, len(stl), 4):
        groups.append((stl[i:i + 4], 1))

    nc = bass.Bass("TRN2", target_bir_lowering=False)

    def din(name, shape, dt=F32):
        return nc.dram_tensor(name, list(shape), dt, kind="ExternalInput").ap()

    def dout(name, shape, dt=F32):
        return nc.dram_tensor(name, list(shape), dt, kind="ExternalOutput").ap()

    def dscr(name, shape, dt=F32):
        return nc.dram_tensor(name, list(shape), dt, kind=("ExternalOutput" if (DEBUG_SCRATCH and not name.startswith("wbf")) else "Internal")).ap()

    x_in = din("x_in", [NTOK, D])
    cond = din("cond", [2, D])
    st_ret = din("st_ret", [DEPTH, 2, 128, 1024])
    st_ssd = din("st_ssd", [DEPTH, 2, 128, 1024])
    rope_cs = din("rope_cs", [SL, 128])
    cmask_d = din("cmask", [128, 5, 128])
    diff_d = din("diffm", [128, 128])
    pidx_d = din("pidx", [128, 4])
    zrow_d = din("zrow", [1, 1536], BF16)
    psel_d = din("psel", [128, 2])
    if not EXCH:
        w_ada = din("w_ada", [DEPTH, D, 6 * D])
        b_ada = din("b_ada", [DEPTH, 6 * D])
        norm1_w = din("norm1_w", [DEPTH, D])
        norm2_w = din("norm2_w", [DEPTH, D])
    w_in = din("w_in", [DEPTH, D, DIN])
    rld = din("ret_log_decay", [DEPTH, 16])
    conv_w = din("conv_w", [DEPTH, 3 * 1536])
    conv_b = din("conv_b", [DEPTH, 1536])
    dt_bias = din("dt_bias", [DEPTH, 32])
    a_log = din("a_log", [DEPTH, 32])
    d_skip = din("d_skip", [DEPTH, 16])
    ssd_nw = din("ssd_norm_w", [DEPTH, 1024])
    w_out = din("w_out", [DEPTH, D, D])
    w_ff1 = din("w_ff1", [DEPTH, D, DFF])
    w_ff2 = din("w_ff2", [DEPTH, DFF, D])
    fnw = din("final_norm_w", [1, D])

    y_out = dout("y_out", [NTOK, D])
    ns_ret = dout("ns_ret", [NP, DEPTH, 2, 128, 1024])
    ns_ssd = dout("ns_ssd", [NP, DEPTH, 2, 128, 1024])

    xs_d = dscr("xs_d", [NTOK, D])
    qk_d = dscr("qk_d", [NTOK, 2048], BF16)
    v_d = dscr("v_d", [NTOK, 1024], BF16)
    sg_d = dscr("sg_d", [NTOK, 1024], BF16)
    sz_d = dscr("sz_d", [NTOK, 1024], BF16)
    xpre_d = dscr("xpre_d", [NTOK + 2 * NSEQ, 1536], BF16)
    xpost_d = dscr("xpost_d", [NTOK, 1536], BF16)
    dt_d = dscr("dt_d", [NTOK, 32])
    y_d = dscr("y_d", [NTOK, 2048], BF16)
    stf_d = dscr("stf_d", [NT, 128, 2048], BF16)
    cstb_d = dscr("cstb_d", [NT, 128, 2048])
    cdb_d = dscr("cdb_d", [NT, 128, 16])
    mod_d = dscr("mod_d", [DEPTH, 2, 6 * D])
    NWT = 14 + 4 + 16 + 16
    if DEBUG_SCRATCH:
        dbg_h = dout("dbg_h", [NTOK, 2048], BF16)
        dbg_x = dout("dbg_x", [NTOK, 2048])
        dbg_a = dout("dbg_a", [NTOK, 2048])
        dbg_h32 = dout("dbg_h32", [NTOK, 2048])
    wbf = [dict(w0=dscr(f"wbf{i}_in", [D, DIN], BF16), w1=dscr(f"wbf{i}_out", [D, D], BF16),
                w2=dscr(f"wbf{i}_ff1", [D, DFF], BF16), w3=dscr(f"wbf{i}_ff2", [DFF, D], BF16)) for i in range(2)]
    exst_in = dscr("exst_in", [128, 2048])
    exst_out = dscr("exst_out", [256, 2048])
    exrow_in = dscr("exrow_in", [1, 1536], BF16)
    exrow_out = dscr("exrow_out", [2, 1536], BF16)
    if EXCH:
        w_ada_h = din("w_ada_h", [DEPTH, D, 3 * D])
        b_ada_h = din("b_ada_h", [DEPTH, 3 * D])
        nw_h = din("nw_h", [DEPTH, D])
        exmod_in = dscr("exmod_in", [DEPTH * 2, 3 * D])
        exmod_out = dscr("exmod_out", [2 * DEPTH * 2, 3 * D])

    def modrow(l, ci, slot):
        if EXCH:
            r = (slot // 3) * DEPTH * 2 + l * 2 + ci
            return exmod_out[r, (slot % 3) * D:(slot % 3 + 1) * D]
        return mod_d[l, ci, slot * D:(slot + 1) * D]

    def xrow(si, t):
        t0, ncks, _, _ = seqs[si]
        return t0 * T + 2 * si + 1 + t

    with ExitStack() as st:
        P = Prog(nc, st)
        ident, b_ident = P.alloc("ident", [128], BF16)
        cmask, b_cmask = P.alloc("cmask", [5, 128])
        diffm, b_diff = P.alloc("diffm", [128])
        pidx, b_pidx = P.alloc("pidx", [4])
        psel, b_psel = P.alloc("psel", [2])
        NRING = 3
        wring = [P.alloc(f"wring{i}", [16, 512], BF16) for i in range(NRING)]
        base_top = P.top
        for b_ in (b_cmask, b_diff, b_pidx, b_psel, wring[0][1], wring[1][1], wring[2][1]):
            P.pin(b_)
        M_gt, M_le, M_lt, M_ge, M_one = [cmask[:, i, :] for i in range(5)]

        d_const = Buf("d_const")
        B_xs = [Buf(f"xs{t}") for t in range(NT)]
        B_qk = [Buf(f"qk{t}") for t in range(NT)]
        B_v = [Buf(f"v{t}") for t in range(NT)]
        B_sg = [Buf(f"sg{t}") for t in range(NT)]
        B_sz = [Buf(f"sz{t}") for t in range(NT)]
        B_xpre = Buf("xpre")
        B_xpost = [Buf(f"xpost{t}") for t in range(NT)]
        B_dt = [Buf(f"dt{t}") for t in range(NT)]
        B_y = [Buf(f"y{t}") for t in range(NT)]
        B_stf = [Buf(f"stf{t}") for t in range(NT)]
        B_cstb = [Buf(f"cstb{t}") for t in range(NT)]
        B_cdb = [Buf(f"cdb{t}") for t in range(NT)]
        B_mod = Buf("mod")
        B_wbf = [[Buf(f"wbf{i}_{k}") for k in range(4)] for i in range(2)]
        B_out = Buf("outs")
        B_exin = Buf("exin"); B_exout = Buf("exout"); B_exrin = Buf("exrin"); B_exrout = Buf("exrout")

        P.dma("sp", cmask, cmask_d, reads=[d_const], writes=[b_cmask], sbuf=b_cmask)
        P.dma("sp", diffm, diff_d, reads=[d_const], writes=[b_diff], sbuf=b_diff)
        P.dma("sp", pidx, pidx_d, reads=[d_const], writes=[b_pidx], sbuf=b_pidx)
        P.dma("sp", psel, psel_d, reads=[d_const], writes=[b_psel], sbuf=b_psel)
        P.op("pool", lambda e: e.memset(ident, 0.0), writes=[b_ident])
        P.op("pool", lambda e: e.affine_select(out=ident, in_=ident, pattern=[[-1, 128]],
                                               compare_op=ALU.not_equal, fill=1.0, base=0,
                                               channel_multiplier=1), reads=[b_ident], writes=[b_ident])
        for si in range(NSEQ):
            for t in (-1, seqs[si][1] * T):
                r = xrow(si, t)
                P.dma("sp", xpre_d[r:r + 1, :], zrow_d, reads=[d_const], writes=[B_xpre], sem_key=("x", "zrow"))

        def wtile_src(par, i):
            if i < 14:
                nco = 512 if i < 13 else 32
                return 0, wbf[par]["w0"][:, i * 512:i * 512 + nco].rearrange("(kc p) c -> p kc c", p=128), nco
            i -= 14
            if i < 4:
                return 1, wbf[par]["w1"][:, i * 512:(i + 1) * 512].rearrange("(kc p) c -> p kc c", p=128), 512
            i -= 4
            if i < 16:
                return 2, wbf[par]["w2"][:, i * 512:(i + 1) * 512].rearrange("(kc p) c -> p kc c", p=128), 512
            i -= 16
            cb, kp = i // 4, i % 4
            return 3, wbf[par]["w3"][kp * 2048:(kp + 1) * 2048, cb * 512:(cb + 1) * 512].rearrange("(kc p) c -> p kc c", p=128), 512

        conv_jobs = []

        def convert_layer(l):
            par = l % 2
            for kind, (src, R) in enumerate(((w_in[l], D), (w_out[l], D), (w_ff1[l], D), (w_ff2[l], DFF))):
                dst = wbf[par]["w%d" % kind]
                for r0 in range(0, R, 128):
                    conv_jobs.append((par, kind, dst[r0:r0 + 128, :], src[r0:r0 + 128, :]))

        def pump_convert(n):
            for _ in range(min(n, len(conv_jobs))):
                par, kind, dst, src = conv_jobs.pop(0)
                P.dma("pool", dst, src, reads=[d_const], writes=[B_wbf[par][kind]], sem_key=("wc", par, kind))

        class WStream:
            def __init__(self):
                self.seq = []
                self.issued = 0
                self.slot_n = 0

            def extend(self, l, idxs):
                for i in idxs:
                    self.seq.append((l, i))

            def prefetch(self, upto):
                while self.issued < min(upto, len(self.seq)):
                    l, i = self.seq[self.issued]
                    kind, src, nco = wtile_src(l % 2, i)
                    slot = self.issued % NRING
                    wt, bw = wring[slot]
                    P.dma("sp", wt[:, :, 0:nco], src, reads=[B_wbf[l % 2][kind]], writes=[bw], sbuf=bw)
                    self.issued += 1

            def get(self, n):
                self.prefetch(n + NRING)
                return wring[n % NRING]

        WS = WStream()
        wcount = [0]

        def next_w():
            r = WS.get(wcount[0])
            wcount[0] += 1
            return r

        for l in range(DEPTH):
            for g in groups:
                WS.extend(l, range(14))
            for g in groups:
                WS.extend(l, range(14, NWT))

        convert_layer(0)
        pump_convert(10 ** 6)
        m0 = P.top
        cT, b_cT = P.alloc("cT", [16, 2])
        modsb, b_modsb = P.alloc("modsb", [6 * D])
        badab, b_bada = P.alloc("badab", [6 * D])
        nwb, b_nwb = P.alloc("nwb", [2 * D])
        wada = [P.alloc(f"wada{i}", [4096]) for i in range(2)]
        for ci_ in range(2):
            P.dma("sp", cT[:, :, ci_], cond[ci_, :].rearrange("(kc p) -> p kc", p=128), reads=[d_const], writes=[b_cT], sbuf=b_cT,
                  allow_slow_non_contiguous=True)
        P.op("act", lambda e: e.activation(out=cT, in_=cT, func=AF.Silu), reads=[b_cT], writes=[b_cT])
        wn = 0
        for l in (range(DEPTH) if EXCH else []):
            P.dma("sp", badab[0:2, 0:3 * D], b_ada_h[l, :].partition_broadcast(2), reads=[d_const], writes=[b_bada], sbuf=b_bada)
            P.dma("sp", nwb[0:2, 0:D], nw_h[l, :].partition_broadcast(2), reads=[d_const], writes=[b_nwb], sbuf=b_nwb)
            for cg in range(2):
                ncol = 4096 if cg == 0 else 2048
                nbk = ncol // 512
                for kc in range(16):
                    wt, bw = wada[wn % 2]
                    wn += 1
                    P.dma("sp", wt[:, 0:ncol], w_ada_h[l][kc * 128:(kc + 1) * 128, cg * 4096:cg * 4096 + ncol],
                          reads=[d_const], writes=[bw], sbuf=bw)
                    for b in range(nbk):
                        P.op("pe", lambda e, b=b, kc=kc, wt=wt: e.matmul(P.ps(b)[0:2, :], lhsT=cT[:, kc, :], rhs=wt[:, b * 512:(b + 1) * 512],
                                                                         start=(kc == 0), stop=(kc == 15)),
                             reads=[b_cT, bw], writes=[P.pbuf[b]], signal=(kc == 15 or b == nbk - 1))
                for b in range(nbk):
                    c0 = cg * 4096 + b * 512
                    P.op("dve", lambda e, b=b, c0=c0: e.tensor_tensor(out=modsb[0:2, c0:c0 + 512], in0=P.ps(b)[0:2, :],
                                                                      in1=badab[0:2, c0:c0 + 512], op=ALU.add),
                         reads=[P.pbuf[b], b_bada], writes=[b_modsb])
            P.op("dve", lambda e: e.scalar_tensor_tensor(out=modsb[0:2, D:2 * D], in0=modsb[0:2, D:2 * D], scalar=1.0,
                                                         in1=nwb[0:2, 0:D], op0=ALU.add, op1=ALU.mult), reads=[b_modsb, b_nwb], writes=[b_modsb])
            P.dma("sp", exmod_in[l * 2:l * 2 + 2, :], modsb[0:2, 0:3 * D], reads=[b_modsb], writes=[B_mod], sbuf=b_modsb)
        if EXCH:
            B_modin = B_mod
            B_mod = Buf("modout")
            P.collective("AllGather", PAIRS, exmod_in, exmod_out, reads=[B_modin], writes=[B_mod], inc=1)
        for l in ([] if EXCH else range(DEPTH)):
            P.dma("sp", badab[0:2, :], b_ada[l, :].partition_broadcast(2),
                  reads=[d_const], writes=[b_bada], sbuf=b_bada)
            P.dma("sp", nwb[0:2, 0:D], norm1_w[l, :].partition_broadcast(2), reads=[d_const], writes=[b_nwb], sbuf=b_nwb)
            P.dma("sp", nwb[0:2, D:2 * D], norm2_w[l, :].partition_broadcast(2), reads=[d_const], writes=[b_nwb], sbuf=b_nwb)
            for cg in range(3):
                for kc in range(16):
                    wt, bw = wada[wn % 2]
                    wn += 1
                    P.dma("sp", wt, w_ada[l][kc * 128:(kc + 1) * 128, cg * 4096:(cg + 1) * 4096],
                          reads=[d_const], writes=[bw], sbuf=bw)
                    for b in range(8):
                        P.op("pe", lambda e, b=b, kc=kc, wt=wt: e.matmul(P.ps(b)[0:2, :], lhsT=cT[:, kc, :], rhs=wt[:, b * 512:(b + 1) * 512],
                                                                         start=(kc == 0), stop=(kc == 15)),
                             reads=[b_cT, bw], writes=[P.pbuf[b]], signal=(kc == 15 or b == 7))
                for b in range(8):
                    c0 = cg * 4096 + b * 512
                    P.op("dve", lambda e, b=b, c0=c0: e.tensor_tensor(out=modsb[0:2, c0:c0 + 512], in0=P.ps(b)[0:2, :],
                                                                      in1=badab[0:2, c0:c0 + 512], op=ALU.add),
                         reads=[P.pbuf[b], b_bada], writes=[b_modsb])
            for slot, off in ((1, 0), (4, D)):
                P.op("dve", lambda e, slot=slot, off=off: e.scalar_tensor_tensor(
                    out=modsb[0:2, slot * D:(slot + 1) * D], in0=modsb[0:2, slot * D:(slot + 1) * D], scalar=1.0,
                    in1=nwb[0:2, off:off + D], op0=ALU.add, op1=ALU.mult), reads=[b_modsb, b_nwb], writes=[b_modsb])
            P.dma("sp", mod_d[l], modsb[0:2, :], reads=[b_modsb], writes=[B_mod], sbuf=b_modsb)
        P.barrier()
        P.new_phase()
        P.top = m0

        def rstd_of(ss, n, ncols, tmp, out, b_ss, b_tmp, b_out):
            P.op("act", lambda e: e.activation(out=tmp, in_=ss, func=AF.Ln, scale=1.0 / n, bias=EPS), reads=[b_ss], writes=[b_tmp])
            P.op("act", lambda e: e.activation(out=out, in_=tmp, func=AF.Exp, scale=-0.5), reads=[b_tmp], writes=[b_out])

        ps_rot = [0]

        def transposes_to(src, b_src, nblk, dst_fn, b_dst, evac_engs=("act", "dve")):
            i = 0
            k = 0
            while i < nblk:
                n = min(8, nblk - i)
                bank = ps_rot[0] % 2
                ps_rot[0] += 1
                pst = P.ps(bank, BF16)
                for j in range(n):
                    P.op("pe", lambda e, j=j, i=i, pst=pst: e.transpose(out=pst[:, j * 128:(j + 1) * 128], in_=src[:, (i + j) * 128:(i + j + 1) * 128],
                                                                       identity=ident),
                         reads=[b_src, b_ident], writes=[P.pbuf[bank]], signal=(j == n - 1))
                eng = evac_engs[k % len(evac_engs)]
                k += 1
                dst = dst_fn(i, n)
                srcv = pst[:, 0:n * 128].rearrange("p (a b) -> p a b", a=n)
                if eng == "act":
                    P.op("act", lambda e, dst=dst, srcv=srcv: e.copy(out=dst, in_=srcv), reads=[P.pbuf[bank]], writes=[b_dst])
                else:
                    P.op("dve", lambda e, dst=dst, srcv=srcv: e.tensor_copy(out=dst, in_=srcv), reads=[P.pbuf[bank]], writes=[b_dst])
                i += n

        mm_rot = [0]

        def mm_bank():
            b = 2 + (mm_rot[0] % 6)
            mm_rot[0] += 1
            return b

        for l in range(DEPTH):
            last = (l == DEPTH - 1)
            if l + 1 < DEPTH:
                convert_layer(l + 1)
            x_src = x_in if l == 0 else xs_d
            B_xsrc = (lambda t: d_const) if l == 0 else (lambda t: B_xs[t])

            mA = P.top
            hTs = [P.alloc(f"hT{i}", [16, 512], BF16) for i in range(2)]
            xb = [P.alloc(f"xb{i}", [D]) for i in range(2)]
            h32, b_h32 = P.alloc("h32", [D])
            hbf = [P.alloc(f"hbf{i}", [D], BF16) for i in range(4)]
            junk, b_junk = P.alloc("junk", [D], BF16)
            modA = [(P.alloc(f"a1bc{i}", [D]), P.alloc(f"sh1bc{i}", [D])) for i in range(2)]
            small, b_small = P.alloc("smallA", [8])
            ropet, b_rope = P.alloc("ropet", [4, 128])
            dtbb, b_dtbb = P.alloc("dtbb", [32])
            stg = [P.alloc(f"stg{i}", [512], BF16) for i in range(6)]
            rt = [P.alloc(f"rt{i}", [4, 64]) for i in range(4)]
            dtst = [P.alloc(f"dtst{i}", [32]) for i in range(2)]
            P.dma("sp", dtbb, dt_bias[l, :].partition_broadcast(128), reads=[d_const], writes=[b_dtbb], sbuf=b_dtbb)
            stg_n = 0

            def prepA_nonpe(gi):
                tiles_, ci_ = groups[gi]
                (a1bc, b_a1), (sh1bc, b_sh1) = modA[gi % 2]
                P.dma("sp", a1bc, modrow(l, ci_, 1).partition_broadcast(128), reads=[B_mod], writes=[b_a1], sbuf=b_a1)
                P.dma("sp", sh1bc, modrow(l, ci_, 0).partition_broadcast(128), reads=[B_mod], writes=[b_sh1], sbuf=b_sh1)
                for j, tt in enumerate(tiles_):
                    pump_convert(1)
                    xt, bx = xb[j % 2]
                    hb, bh = hbf[j]
                    P.dma("sp", xt, x_src[tt * T:(tt + 1) * T, :], reads=[B_xsrc(tt)], writes=[bx], sbuf=bx)
                    P.op("dve", lambda e: e.memset(small[:, 0:1], 0.0), writes=[b_small])
                    P.op("act", lambda e, xt=xt: e.activation(out=junk, in_=xt, func=AF.Square, accum_out=small[:, 0:1]),
                         reads=[bx], writes=[b_junk, b_small])
                    rstd_of(small[:, 0:1], D, 1, small[:, 1:2], small[:, 2:3], b_small, b_small, b_small)
                    P.op("dve", lambda e, xt=xt: e.scalar_tensor_tensor(out=h32, in0=xt, scalar=small[:, 2:3], in1=a1bc,
                                                                        op0=ALU.mult, op1=ALU.mult),
                         reads=[bx, b_small, b_a1], writes=[b_h32])
                    P.op("pool", lambda e, hb=hb: e.tensor_tensor(out=hb, in0=h32, in1=sh1bc, op=ALU.add),
                         reads=[b_h32, b_sh1], writes=[bh])

            def prepA_pe(gi):
                tiles_, ci_ = groups[gi]
                hTn, b_hTn = hTs[gi % 2]
                for j, tt in enumerate(tiles_):
                    hb, bh = hbf[j]
                    transposes_to(hb, bh, 16, lambda i, n, j=j: hTn[:, i:i + n, j * T:(j + 1) * T], b_hTn)

            prepA_nonpe(0)
            prepA_pe(0)
            for gi, (tiles, ci) in enumerate(groups):
                G = len(tiles) * T
                hT, b_hT = hTs[gi % 2]
                if ci == 1:
                    r0 = (tiles[0] - NP * CP) * T
                    P.dma("sp", ropet[:, 0:len(tiles), :], rope_cs[r0:r0 + G, :].rearrange("(j p) c -> p j c", p=128),
                          reads=[d_const], writes=[b_rope], sbuf=b_rope)
                for cb in range(14):
                    if cb == 1 and gi + 1 < len(groups):
                        prepA_nonpe(gi + 1)
                    if cb == 9 and gi + 1 < len(groups):
                        prepA_pe(gi + 1)
                    wt, bw = next_w()
                    nco = 512 if cb < 13 else 32
                    for j, tt in enumerate(tiles):
                        bank = mm_bank()
                        psv = P.ps(bank)[:, 0:nco]
                        for kc in range(16):
                            P.op("pe", lambda e, kc=kc, j=j, wt=wt, psv=psv, nco=nco: e.matmul(
                                psv, lhsT=hT[:, kc, j * T:(j + 1) * T], rhs=wt[:, kc, 0:nco], start=(kc == 0), stop=(kc == 15)),
                                reads=[b_hT, bw], writes=[P.pbuf[bank]], signal=(kc == 15))
                        rows = slice(tt * T, (tt + 1) * T)
                        if cb < 13:
                            sgt, bs = stg[stg_n % 6]
                            stg_n += 1
                        if cb < 4:
                            if ci == 1:
                                ps3 = psv.rearrange("p (h n) -> p h n", h=4)
                                x1 = ps3[:, :, 0:64]
                                x2 = ps3[:, :, 64:128]
                                cs = bc_mid(ropet[:, j, 0:64], 4)
                                sn = bc_mid(ropet[:, j, 64:128], 4)
                                o3 = sgt.rearrange("p (h n) -> p h n", h=4)
                                (t1, bt1), (t2, bt2), (t3, bt3), (t4, bt4) = rt
                                P.op("dve", lambda e, x1=x1, cs=cs, t1=t1: e.tensor_tensor(out=t1, in0=x1, in1=cs, op=ALU.mult),
                                     reads=[P.pbuf[bank], b_rope], writes=[bt1])
                                P.op("dve", lambda e, x2=x2, sn=sn, t2=t2: e.tensor_tensor(out=t2, in0=x2, in1=sn, op=ALU.mult),
                                     reads=[P.pbuf[bank], b_rope], writes=[bt2])
                                P.op("dve", lambda e, x1=x1, sn=sn, t3=t3: e.tensor_tensor(out=t3, in0=x1, in1=sn, op=ALU.mult),
                                     reads=[P.pbuf[bank], b_rope], writes=[bt3])
                                P.op("dve", lambda e, x2=x2, cs=cs, t4=t4: e.tensor_tensor(out=t4, in0=x2, in1=cs, op=ALU.mult),
                                     reads=[P.pbuf[bank], b_rope], writes=[bt4])
                                P.op("pool", lambda e, o3=o3, t1=t1, t2=t2: e.tensor_tensor(out=o3[:, :, 0:64], in0=t1, in1=t2, op=ALU.subtract),
                                     reads=[bt1, bt2], writes=[bs])
                                P.op("pool", lambda e, o3=o3, t3=t3, t4=t4: e.tensor_tensor(out=o3[:, :, 64:128], in0=t3, in1=t4, op=ALU.add),
                                     reads=[bt3, bt4], writes=[bs])
                            else:
                                P.op("act", lambda e, sgt=sgt, psv=psv: e.copy(out=sgt, in_=psv), reads=[P.pbuf[bank]], writes=[bs])
                            P.dma("pool", qk_d[rows, cb * 512:(cb + 1) * 512], sgt, reads=[bs], writes=[B_qk[tt]], sbuf=bs)
                        elif cb < 6:
                            P.op("act", lambda e, sgt=sgt, psv=psv: e.copy(out=sgt, in_=psv), reads=[P.pbuf[bank]], writes=[bs])
                            P.dma("act", v_d[rows, (cb - 4) * 512:(cb - 3) * 512], sgt, reads=[bs], writes=[B_v[tt]], sbuf=bs)
                        elif cb < 10:
                            P.op("act", lambda e, sgt=sgt, psv=psv: e.activation(out=sgt, in_=psv, func=AF.Silu),
                                 reads=[P.pbuf[bank]], writes=[bs])
                            if cb < 8:
                                P.dma("act", sg_d[rows, (cb - 6) * 512:(cb - 5) * 512], sgt, reads=[bs], writes=[B_sg[tt]], sbuf=bs)
                            else:
                                P.dma("act", sz_d[rows, (cb - 8) * 512:(cb - 7) * 512], sgt, reads=[bs], writes=[B_sz[tt]], sbuf=bs)
                        elif cb < 13:
                            P.op("dve", lambda e, sgt=sgt, psv=psv: e.tensor_copy(out=sgt, in_=psv), reads=[P.pbuf[bank]], writes=[bs])
                            si = [k for k, s in enumerate(seqs) if s[0] <= tt < s[0] + s[1]][0]
                            r = xrow(si, (tt - seqs[si][0]) * T)
                            P.dma("pool", xpre_d[r:r + T, (cb - 10) * 512:(cb - 9) * 512], sgt, reads=[bs], writes=[B_xpre], sbuf=bs)
                        else:
                            dtt, bdt = dtst[j % 2]
                            P.op("dve", lambda e, dtt=dtt, psv=psv: e.tensor_tensor(out=dtt, in0=psv, in1=dtbb, op=ALU.add),
                                 reads=[P.pbuf[bank], b_dtbb], writes=[bdt])
                            P.op("dve", lambda e, dtt=dtt: e.tensor_scalar_min(out=dtt, in0=dtt, scalar1=60.0), reads=[bdt], writes=[bdt])
                            P.op("act", lambda e, dtt=dtt: e.activation(out=dtt, in_=dtt, func=AF.Exp), reads=[bdt], writes=[bdt])
                            P.op("act", lambda e, dtt=dtt: e.activation(out=dtt, in_=dtt, func=AF.Ln, bias=1.0, scale=1.0), reads=[bdt], writes=[bdt])
                            P.dma("pool", dt_d[rows, :], dtt, reads=[bdt], writes=[B_dt[tt]], sbuf=bdt)
            P.barrier()
            P.new_phase()
            P.top = mA

            mB = P.top
            ldbc, b_ld = P.alloc("ldbc", [16])
            Mret, b_Mret = P.alloc("Mret", [8, 128])
            rtab, b_rtab = P.alloc("rtab", [5, 16])
            Abc, b_Abc = P.alloc("Abc", [32])
            dskb, b_dsk = P.alloc("dskb", [16])
            cwbc, b_cw = P.alloc("cwbc", [3, 1536])
            cbbc, b_cb = P.alloc("cbbc", [1536])
            snwb, b_snw = P.alloc("snwb", [1024])
            mt = [P.alloc(f"mt{i}", [128]) for i in range(4)]
            P.dma("sp", ldbc, rld[l, :].partition_broadcast(128), reads=[d_const], writes=[b_ld], sbuf=b_ld)
            P.dma("sp", Abc, a_log[l, :].partition_broadcast(128), reads=[d_const], writes=[b_Abc], sbuf=b_Abc)
            P.dma("sp", dskb, d_skip[l, :].partition_broadcast(128), reads=[d_const], writes=[b_dsk], sbuf=b_dsk)
            P.dma("sp", cwbc, conv_w[l, :].partition_broadcast(128).rearrange("p (k c) -> p k c", k=3), reads=[d_const], writes=[b_cw], sbuf=b_cw)
            P.dma("sp", cbbc, conv_b[l, :].partition_broadcast(128), reads=[d_const], writes=[b_cb], sbuf=b_cb)
            P.dma("sp", snwb, ssd_nw[l, :].partition_broadcast(128), reads=[d_const], writes=[b_snw], sbuf=b_snw)
            P.op("act", lambda e: e.activation(out=Abc, in_=Abc, func=AF.Exp), reads=[b_Abc], writes=[b_Abc])
            P.op("dve", lambda e: e.tensor_scalar_mul(out=Abc, in0=Abc, scalar1=-1.0), reads=[b_Abc], writes=[b_Abc])
            P.op("act", lambda e: e.activation(out=rtab[:, 0, 0:8], in_=ldbc[:, 0:8], func=AF.Exp, scale=pidx[:, 0:1]), reads=[b_ld, b_pidx], writes=[b_rtab])
            P.op("act", lambda e: e.activation(out=rtab[:, 0, 8:16], in_=ldbc[:, 8:16], func=AF.Exp, scale=pidx[:, 2:3]), reads=[b_ld, b_pidx], writes=[b_rtab])
            P.op("dve", lambda e: e.tensor_scalar_mul(out=rtab[:, 0, :], in0=rtab[:, 0, :], scalar1=DKS), reads=[b_rtab], writes=[b_rtab])
            P.op("act", lambda e: e.activation(out=rtab[:, 1, 0:8], in_=ldbc[:, 0:8], func=AF.Exp, scale=pidx[:, 1:2]), reads=[b_ld, b_pidx], writes=[b_rtab])
            P.op("act", lambda e: e.activation(out=rtab[:, 1, 8:16], in_=ldbc[:, 8:16], func=AF.Exp, scale=pidx[:, 3:4]), reads=[b_ld, b_pidx], writes=[b_rtab])
            P.op("act", lambda e: e.activation(out=rtab[:, 2, :], in_=ldbc, func=AF.Exp, scale=float(T)), reads=[b_ld], writes=[b_rtab])
            (dpos, b_dpos), (dneg, b_dneg), (mtmp, b_mtmp), (mtmp2, b_mtmp2) = mt
            P.op("dve", lambda e: e.tensor_scalar_max(out=dpos, in0=diffm, scalar1=0.0), reads=[b_diff], writes=[b_dpos])
            P.op("dve", lambda e: e.tensor_scalar(out=dneg, in0=diffm, scalar1=-1.0, scalar2=0.0, op0=ALU.mult, op1=ALU.max), reads=[b_diff], writes=[b_dneg])
            for h in range(8):
                P.op("act", lambda e, h=h: e.activation(out=mtmp, in_=dpos, func=AF.Exp, scale=ldbc[:, h:h + 1]), reads=[b_dpos, b_ld], writes=[b_mtmp])
                P.op("act", lambda e, h=h: e.activation(out=mtmp2, in_=dneg, func=AF.Exp, scale=ldbc[:, 8 + h:9 + h]), reads=[b_dneg, b_ld], writes=[b_mtmp2])
                P.op("dve", lambda e: e.tensor_tensor(out=mtmp, in0=mtmp, in1=M_le, op=ALU.mult), reads=[b_mtmp, b_cmask], writes=[b_mtmp])
                P.op("dve", lambda e: e.tensor_tensor(out=mtmp2, in0=mtmp2, in1=M_ge, op=ALU.mult), reads=[b_mtmp2, b_cmask], writes=[b_mtmp2])
                P.op("dve", lambda e, h=h: e.scalar_tensor_tensor(out=Mret[:, h, :], in0=mtmp, scalar=DKS, in1=mtmp2, op0=ALU.mult, op1=ALU.add),
                     reads=[b_mtmp, b_mtmp2], writes=[b_Mret])
                P.op("dve", lambda e, h=h: e.scalar_tensor_tensor(out=Mret[:, h, :], in0=mtmp2, scalar=DKS - 1.0, in1=Mret[:, h, :], op0=ALU.mult, op1=ALU.add),
                     reads=[b_mtmp2, b_Mret], writes=[b_Mret])

            Sf = [P.alloc(f"Sf{i}", [1024]) for i in range(2)]
            Sb = [P.alloc(f"Sb{i}", [1024]) for i in range(2)]
            Sbb = [P.alloc(f"Sbb{i}", [1024], BF16) for i in range(2)]
            NB = 1
            qkt = [P.alloc(f"qkt{i}", [2048], BF16) for i in range(NB)]
            vt = [P.alloc(f"vt{i}", [1024], BF16) for i in range(NB)]
            xsh = [[P.alloc(f"xsh{i}_{k}", [1536], BF16) for k in range(3)] for i in range(1)]
            xpo = [P.alloc(f"xpo{i}", [1536], BF16) for i in range(NB)]
            dtb = [P.alloc(f"dtb{i}", [32]) for i in range(2)]
            sgz = [P.alloc(f"sgz{i}", [2048], BF16) for i in range(NB)]
            stfb = [P.alloc(f"stfb{i}", [2048], BF16) for i in range(NB)]
            cstb = [P.alloc(f"cstb{i}", [2048]) for i in range(NB)]
            cdbb = [P.alloc(f"cdbb{i}", [16]) for i in range(NB)]
            cacc, b_cacc = P.alloc("cacc", [1536])
            ct1, b_ct1 = P.alloc("ct1", [1536])
            ct2, b_ct2 = P.alloc("ct2", [1536])
            yw, b_yw = cacc[:, 0:1024], b_cacc
            yt, b_yt = ct1[:, 0:1024], b_ct1
            ysq, b_ysq = ct2[:, 0:1024], b_ct2
            av, b_av = P.alloc("av", [32])
            dec, b_dec = P.alloc("dec", [64])
            wgt, b_wgt = P.alloc("wgt", [32])
            xw = [P.alloc(f"xw{i}", [1024], BF16) for i in range(2)]
            kw = [P.alloc(f"kw{i}", [1024], BF16) for i in range(2)]
            stst = [P.alloc(f"stst{i}", [2048], BF16) for i in range(1)]
            cst_st = [P.alloc(f"cst_st{i}", [2048]) for i in range(1)]
            cdst = [P.alloc(f"cdst{i}", [16]) for i in range(2)]
            qT, b_qT = P.alloc("qT", [8, 128], BF16)
            kT, b_kT = P.alloc("kT", [8, 128], BF16)
            bcT, b_bcT = P.alloc("bcT", [4, 128], BF16)
            AT, b_AT = P.alloc("AT", [8, 128], BF16)
            Xd = [P.alloc(f"Xd{i}", [16, 128], BF16) for i in range(2)]
            ahl, b_ahl = P.alloc("ahl", [2, 32], BF16)
            cmb, b_cmb = P.alloc("cmb", [2, 128], BF16)
            P.op("dve", lambda e: e.tensor_copy(out=cmb[:, 0, :], in_=M_le), reads=[b_cmask], writes=[b_cmb])
            P.op("dve", lambda e: e.tensor_copy(out=cmb[:, 1, :], in_=M_ge), reads=[b_cmask], writes=[b_cmb])
            Ed = [P.alloc(f"Ed{i}", [16, 128], BF16) for i in range(2)]
            Md = Ed
            Gm, b_Gm = P.alloc("Gm", [4, 128], BF16)
            ysm, b_ysm = P.alloc("ysm", [32])
            yst = [P.alloc(f"yst{i}", [2048], BF16) for i in range(1)]

            cnt1 = [0]
            cnt2 = [0]
            def run_pass1(seq):
                (t0, nck, is_s, sidx) = seq
                for ty in range(2):
                    sv, bsv = Sf[ty]
                    if is_s:
                        src = (st_ret if ty == 0 else st_ssd)[l, 0]
                        P.dma("sp", sv, src, reads=[d_const], writes=[bsv], sbuf=bsv)
                    else:
                        P.op("pool", lambda e, sv=sv: e.memset(sv, 0.0), writes=[bsv])
                for c in range(nck):
                    pump_convert(1)
                    tt = t0 + c
                    i = cnt1[0] % NB
                    cnt1[0] += 1
                    rows = slice(tt * T, (tt + 1) * T)
                    kt_, bkt = qkt[i]
                    vt_, bvt = vt[i]
                    dtt, bdtt = dtb[c % 2]
                    P.dma("sp", kt_[:, 1024:2048], qk_d[rows, 1024:2048], reads=[B_qk[tt]], writes=[bkt], sbuf=bkt)
                    P.dma("sp", vt_, v_d[rows, :], reads=[B_v[tt]], writes=[bvt], sbuf=bvt)
                    P.dma("sp", dtt, dt_d[rows, :], reads=[B_dt[tt]], writes=[bdtt], sbuf=bdtt)
                    r = xrow(sidx, c * T)
                    for k3 in range(3):
                        xv, bxv = xsh[0][k3]
                        P.dma("sp", xv, xpre_d[r - 1 + k3:r - 1 + k3 + T, :], reads=[B_xpre], writes=[bxv], sbuf=bxv)
                    P.op("dve", lambda e, xv=xsh[0][1][0]: e.tensor_tensor(out=cacc, in0=xv, in1=cwbc[:, 1, :], op=ALU.mult),
                         reads=[xsh[0][1][1], b_cw], writes=[b_cacc])
                    P.op("pool", lambda e, xv=xsh[0][0][0]: e.tensor_tensor(out=ct1, in0=xv, in1=cwbc[:, 0, :], op=ALU.mult),
                         reads=[xsh[0][0][1], b_cw], writes=[b_ct1])
                    P.op("pool", lambda e, xv=xsh[0][2][0]: e.tensor_tensor(out=ct2, in0=xv, in1=cwbc[:, 2, :], op=ALU.mult),
                         reads=[xsh[0][2][1], b_cw], writes=[b_ct2])
                    P.op("pool", lambda e: e.tensor_tensor(out=ct1, in0=ct1, in1=cbbc, op=ALU.add), reads=[b_ct1, b_cb], writes=[b_ct1])
                    P.op("dve", lambda e: e.tensor_tensor(out=cacc, in0=cacc, in1=ct2, op=ALU.add), reads=[b_cacc, b_ct2], writes=[b_cacc])
                    P.op("dve", lambda e: e.tensor_tensor(out=cacc, in0=cacc, in1=ct1, op=ALU.add), reads=[b_cacc, b_ct1], writes=[b_cacc])
                    xp_, bxp = xpo[i]
                    P.op("act", lambda e, xp_=xp_: e.activation(out=xp_, in_=cacc, func=AF.Silu), reads=[b_cacc], writes=[bxp])
                    P.dma("pool", xpost_d[rows, :], xp_, reads=[bxp], writes=[B_xpost[tt]], sbuf=bxp)
                    xs3 = xp_[:, 0:1024].rearrange("p (h c) -> p h c", h=16)
                    Btok = xp_[:, 1024:1280].rearrange("p (g n) -> p g n", g=2)
                    P.op("dve", lambda e, dtt=dtt: e.tensor_tensor(out=av, in0=dtt, in1=Abc, op=ALU.mult), reads=[bdtt, b_Abc], writes=[b_av])
                    bk = mm_bank()
                    pc = P.ps(bk)
                    P.op("pe", lambda e, pc=pc: e.matmul(pc[:, 0:16], lhsT=M_gt, rhs=av[:, 0:16], start=True, stop=True), reads=[b_cmask, b_av], writes=[P.pbuf[bk]], signal=False)
                    P.op("pe", lambda e, pc=pc: e.matmul(pc[:, 16:32], lhsT=M_lt, rhs=av[:, 16:32], start=True, stop=True), reads=[b_cmask, b_av], writes=[P.pbuf[bk]], signal=False)
                    P.op("pe", lambda e, pc=pc: e.matmul(pc[:, 32:64], lhsT=M_one, rhs=av[:, 0:32], start=True, stop=True), reads=[b_cmask, b_av], writes=[P.pbuf[bk]])
                    P.op("act", lambda e, pc=pc: e.activation(out=dec, in_=pc[:, 0:64], func=AF.Exp), reads=[P.pbuf[bk]], writes=[b_dec])
                    P.op("dve", lambda e, dtt=dtt: e.tensor_tensor(out=wgt, in0=dtt, in1=dec[:, 0:32], op=ALU.mult), reads=[bdtt, b_dec], writes=[b_wgt])
                    k3v = kt_[:, 1024:2048].rearrange("p (h n) -> p h n", h=8)
                    v3v = vt_.rearrange("p (h c) -> p h c", h=8)
                    for d in range(2):
                        xw_, bxw = xw[d]
                        kw_, bkw = kw[d]
                        P.op("dve", lambda e, d=d, xw_=xw_: e.tensor_tensor(out=xw_.rearrange("p (h c) -> p h c", h=16), in0=xs3,
                                                                          in1=bc_last(wgt[:, d * 16:(d + 1) * 16], 64), op=ALU.mult),
                             reads=[bxp, b_wgt], writes=[bxw])
                        P.op("pool", lambda e, d=d, kw_=kw_: e.tensor_tensor(out=kw_.rearrange("p (h n) -> p h n", h=8), in0=k3v,
                                                                           in1=bc_last(rtab[:, 1, d * 8:(d + 1) * 8], 128), op=ALU.mult),
                             reads=[bkt, b_rtab], writes=[bkw])
                    def chunk_state_mms(d):
                        res = {}
                        for ty in range(2):
                            b0, b1 = mm_bank(), mm_bank()
                            res[ty] = (b0, b1)
                            if ty == 0:
                                kw3 = kw[d][0].rearrange("p (h n) -> p h n", h=8)
                                for h in range(8):
                                    bnk = (b0, b1)[h // 4]
                                    P.op("pe", lambda e, h=h, bnk=bnk, kw3=kw3: e.matmul(P.ps(bnk)[:, (h % 4) * 128:(h % 4 + 1) * 128], lhsT=kw3[:, h, :],
                                                                                       rhs=v3v[:, h, :], start=True, stop=True),
                                         reads=[kw[d][1], bvt], writes=[P.pbuf[bnk]], signal=(h % 4 == 3))
                            else:
                                for g in range(2):
                                    bnk = (b0, b1)[g]
                                    P.op("pe", lambda e, g=g, bnk=bnk, d=d: e.matmul(P.ps(bnk), lhsT=Btok[:, g, :], rhs=xw[d][0][:, g * 512:(g + 1) * 512],
                                                                                    start=True, stop=True),
                                         reads=[bxp, xw[d][1]], writes=[P.pbuf[bnk]])
                        return res
                    sst, bsst = stst[0]
                    for ty in range(2):
                        sv, bsv = Sf[ty]
                        P.op("act", lambda e, ty=ty, sv=sv, sst=sst: e.copy(out=sst[:, ty * 1024:(ty + 1) * 1024], in_=sv), reads=[bsv], writes=[bsst])
                    P.dma("pool", stf_d[tt], sst, reads=[bsst], writes=[B_stf[tt]], sbuf=bsst)
                    psf = chunk_state_mms(0)
                    for ty in range(2):
                        sv, bsv = Sf[ty]
                        b0, b1 = psf[ty]
                        if ty == 0:
                            P.op("dve", lambda e, sv=sv: e.tensor_tensor(out=sv.rearrange("p (h c) -> p h c", h=8), in0=sv.rearrange("p (h c) -> p h c", h=8),
                                                                        in1=bc_last(rtab[:, 2, 0:8], 128), op=ALU.mult), reads=[bsv, b_rtab], writes=[bsv])
                        else:
                            P.op("dve", lambda e, sv=sv: e.tensor_tensor(out=sv.rearrange("p (h c) -> p h c", h=16), in0=sv.rearrange("p (h c) -> p h c", h=16),
                                                                        in1=bc_last(dec[:, 32:48], 64), op=ALU.mult), reads=[bsv, b_dec], writes=[bsv])
                        for hb_, bnk in enumerate((b0, b1)):
                            P.op("dve", lambda e, sv=sv, hb_=hb_, bnk=bnk: e.tensor_tensor(out=sv[:, hb_ * 512:(hb_ + 1) * 512], in0=sv[:, hb_ * 512:(hb_ + 1) * 512],
                                                                                       in1=P.ps(bnk), op=ALU.add), reads=[bsv, P.pbuf[bnk]], writes=[bsv])
                    psbk = chunk_state_mms(1)
                    cs_, bcs = cst_st[0]
                    for ty in range(2):
                        b0, b1 = psbk[ty]
                        for hb_, bnk in enumerate((b0, b1)):
                            P.op("act", lambda e, ty=ty, hb_=hb_, bnk=bnk, cs_=cs_: e.copy(out=cs_[:, ty * 1024 + hb_ * 512: ty * 1024 + (hb_ + 1) * 512], in_=P.ps(bnk)),
                                 reads=[P.pbuf[bnk]], writes=[bcs])
                    P.dma("pool", cstb_d[tt], cs_, reads=[bcs], writes=[B_cstb[tt]], sbuf=bcs)
                    cd_, bcd = cdst[c % 2]
                    P.op("act", lambda e, cd_=cd_: e.copy(out=cd_, in_=dec[:, 48:64]), reads=[b_dec], writes=[bcd])
                    P.dma("pool", cdb_d[tt], cd_, reads=[bcd], writes=[B_cdb[tt]], sbuf=bcd)
                if not is_s:
                    for ty in range(2):
                        sv, bsv = Sf[ty]
                        dst = (ns_ret if ty == 0 else ns_ssd)[sidx, l, 0]
                        P.dma("pool", dst, sv, reads=[bsv], writes=[B_out], sbuf=bsv)

                if is_s and EXCH:
                    for ty in range(2):
                        sv, bsv = Sf[ty]
                        P.dma("sp", exst_in[:, ty * 1024:(ty + 1) * 1024], sv, reads=[bsv], writes=[B_exin], sbuf=bsv)
                    P.collective("AllGather", PAIRS, exst_in, exst_out, reads=[B_exin], writes=[B_exout], inc=1)

            def run_pass2(seq):
                (t0, nck, is_s, sidx) = seq
                for ty in range(2):
                    sv, bsv = Sb[ty]
                    if is_s and EXCH:
                        ev, bev = cstb[0]
                        od, bod = cst_st[0]
                        P.dma("sp", ev[:, 0:1024], exst_out[0:128, ty * 1024:(ty + 1) * 1024], reads=[B_exout], writes=[bev], sbuf=bev)
                        P.dma("sp", od[:, 0:1024], exst_out[128:256, ty * 1024:(ty + 1) * 1024], reads=[B_exout], writes=[bod], sbuf=bod)
                        P.op("dve", lambda e, sv=sv, ev=ev: e.tensor_scalar(out=sv, in0=ev[:, 0:1024], scalar1=psel[:, 0:1], scalar2=None, op0=ALU.mult),
                             reads=[bev, b_psel], writes=[bsv])
                        P.op("dve", lambda e, sv=sv, od=od: e.scalar_tensor_tensor(out=sv, in0=od[:, 0:1024], scalar=psel[:, 1:2], in1=sv, op0=ALU.mult, op1=ALU.add),
                             reads=[bod, b_psel, bsv], writes=[bsv])
                    elif is_s:
                        src = (st_ret if ty == 0 else st_ssd)[l, 1]
                        P.dma("sp", sv, src, reads=[d_const], writes=[bsv], sbuf=bsv)
                    else:
                        P.op("pool", lambda e, sv=sv: e.memset(sv, 0.0), writes=[bsv])
                for ty in range(2):
                    P.op("act", lambda e, ty=ty: e.copy(out=Sbb[ty][0], in_=Sb[ty][0]), reads=[Sb[ty][1]], writes=[Sbb[ty][1]])
                for c in range(nck - 1, -1, -1):
                    pump_convert(2)
                    tt = t0 + c
                    i = cnt2[0] % NB
                    cnt2[0] += 1
                    rows = slice(tt * T, (tt + 1) * T)
                    qk_, bqk = qkt[i]
                    vt_, bvt = vt[i]
                    dtt, bdtt = dtb[c % 2]
                    xp_, bxp = xpo[i]
                    sgz_, bsgz = sgz[i]
                    stf_, bstf = stfb[i]
                    csb_, bcsb = cstb[i]
                    cdb_, bcdb = cdbb[i]
                    P.dma("sp", qk_, qk_d[rows, :], reads=[B_qk[tt]], writes=[bqk], sbuf=bqk)
                    P.dma("sp", vt_, v_d[rows, :], reads=[B_v[tt]], writes=[bvt], sbuf=bvt)
                    P.dma("sp", dtt, dt_d[rows, :], reads=[B_dt[tt]], writes=[bdtt], sbuf=bdtt)
                    P.dma("sp", xp_, xpost_d[rows, :], reads=[B_xpost[tt]], writes=[bxp], sbuf=bxp)
                    P.dma("sp", sgz_[:, 0:1024], sg_d[rows, :], reads=[B_sg[tt]], writes=[bsgz], sbuf=bsgz)
                    P.dma("sp", sgz_[:, 1024:2048], sz_d[rows, :], reads=[B_sz[tt]], writes=[bsgz], sbuf=bsgz)
                    P.dma("sp", stf_, stf_d[tt], reads=[B_stf[tt]], writes=[bstf], sbuf=bstf)
                    P.dma("sp", csb_, cstb_d[tt], reads=[B_cstb[tt]], writes=[bcsb], sbuf=bcsb)
                    P.dma("sp", cdb_, cdb_d[tt], reads=[B_cdb[tt]], writes=[bcdb], sbuf=bcdb)
                    v3v = vt_.rearrange("p (h c) -> p h c", h=8)
                    xs3 = xp_[:, 0:1024].rearrange("p (h c) -> p h c", h=16)
                    transposes_to(qk_[:, 0:1024], bqk, 8, lambda i0, n: qT[:, i0:i0 + n, :], b_qT)
                    transposes_to(qk_[:, 1024:2048], bqk, 8, lambda i0, n: kT[:, i0:i0 + n, :], b_kT)
                    transposes_to(xp_[:, 1024:1536], bxp, 4, lambda i0, n: bcT[:, i0:i0 + n, :], b_bcT)
                    pb = (mm_bank(), mm_bank())
                    for h in range(8):
                        bnk = pb[h // 4]
                        P.op("pe", lambda e, h=h, bnk=bnk: e.matmul(P.ps(bnk)[:, (h % 4) * 128:(h % 4 + 1) * 128], lhsT=kT[:, h, :], rhs=qT[:, h, :],
                                                                  start=True, stop=True), reads=[b_kT, b_qT], writes=[P.pbuf[bnk]], signal=(h % 4 == 3))
                    for hb_ in range(2):
                        P.op("dve", lambda e, hb_=hb_: e.tensor_tensor(out=AT[:, hb_ * 4:(hb_ + 1) * 4, :], in0=P.ps(pb[hb_]).rearrange("p (h n) -> p h n", h=4),
                                                                     in1=Mret[:, hb_ * 4:(hb_ + 1) * 4, :], op=ALU.mult),
                             reads=[P.pbuf[pb[hb_]], b_Mret], writes=[b_AT])
                    pin = (mm_bank(), mm_bank())
                    pof = (mm_bank(), mm_bank())
                    for h in range(8):
                        bnk = pin[h // 4]
                        P.op("pe", lambda e, h=h, bnk=bnk: e.matmul(P.ps(bnk)[:, (h % 4) * 128:(h % 4 + 1) * 128], lhsT=AT[:, h, :], rhs=v3v[:, h, :],
                                                                  start=True, stop=True), reads=[b_AT, bvt], writes=[P.pbuf[bnk]], signal=(h % 4 == 3))
                    for h in range(8):
                        bnk = pof[h // 4]
                        P.op("pe", lambda e, h=h, bnk=bnk, stf_=stf_: e.matmul(P.ps(bnk)[:, (h % 4) * 128:(h % 4 + 1) * 128], lhsT=qT[:, h, :],
                                                                             rhs=stf_[:, h * 128:(h + 1) * 128], start=True, stop=True),
                             reads=[b_qT, bstf], writes=[P.pbuf[bnk]], signal=(h % 4 == 3))
                    yw3 = yw.rearrange("p (h c) -> p h c", h=8)
                    yt3 = yt.rearrange("p (h c) -> p h c", h=8)
                    for hb_ in range(2):
                        sl = slice(hb_ * 4, (hb_ + 1) * 4)
                        P.op("dve", lambda e, hb_=hb_, sl=sl: e.tensor_tensor(out=yt3[:, sl, :], in0=P.ps(pof[hb_]).rearrange("p (h n) -> p h n", h=4),
                                                                            in1=bc_last(rtab[:, 0, sl], 128), op=ALU.mult),
                             reads=[P.pbuf[pof[hb_]], b_rtab], writes=[b_yt])
                        P.op("dve", lambda e, hb_=hb_, sl=sl: e.tensor_tensor(out=yw3[:, sl, :], in0=P.ps(pin[hb_]).rearrange("p (h n) -> p h n", h=4),
                                                                            in1=yt3[:, sl, :], op=ALU.add),
                             reads=[P.pbuf[pin[hb_]], b_yt], writes=[b_yw])
                    pob = (mm_bank(), mm_bank())
                    for h in range(8):
                        bnk = pob[h // 4]
                        P.op("pe", lambda e, h=h, bnk=bnk: e.matmul(P.ps(bnk)[:, (h % 4) * 128:(h % 4 + 1) * 128], lhsT=qT[:, h, :],
                                                                  rhs=Sbb[0][0][:, h * 128:(h + 1) * 128], start=True, stop=True),
                             reads=[b_qT, Sbb[0][1]], writes=[P.pbuf[bnk]], signal=(h % 4 == 3))
                    for hb_ in range(2):
                        sl = slice(hb_ * 4, (hb_ + 1) * 4)
                        P.op("dve", lambda e, hb_=hb_, sl=sl: e.tensor_tensor(out=yt3[:, sl, :], in0=P.ps(pob[hb_]).rearrange("p (h n) -> p h n", h=4),
                                                                            in1=bc_last(rtab[:, 0, 8 + hb_ * 4:8 + (hb_ + 1) * 4], 128), op=ALU.mult),
                             reads=[P.pbuf[pob[hb_]], b_rtab], writes=[b_yt])
                    P.op("pool", lambda e: e.tensor_tensor(out=yw, in0=yw, in1=yt, op=ALU.add), reads=[b_yw, b_yt], writes=[b_yw])
                    P.op("act", lambda e: e.activation(out=ysq, in_=yw, func=AF.Square), reads=[b_yw], writes=[b_ysq])
                    P.op("dve", lambda e: e.tensor_reduce(out=ysm[:, 0:8], in_=ysq.rearrange("p (h c) -> p h c", h=8), axis=AX.X, op=ALU.add),
                         reads=[b_ysq], writes=[b_ysm])
                    rstd_of(ysm[:, 0:8], 128, 8, ysm[:, 8:16], ysm[:, 16:24], b_ysm, b_ysm, b_ysm)
                    yo_, byo = yst[0]
                    P.op("dve", lambda e: e.tensor_tensor(out=yw3, in0=yw3, in1=bc_last(ysm[:, 16:24], 128), op=ALU.mult), reads=[b_yw, b_ysm], writes=[b_yw])
                    P.op("pool", lambda e, yo_=yo_, sgz_=sgz_: e.tensor_tensor(out=yo_[:, 0:1024], in0=yw, in1=sgz_[:, 0:1024], op=ALU.mult),
                         reads=[b_yw, bsgz], writes=[byo])
                    P.op("dve", lambda e, dtt=dtt: e.tensor_tensor(out=av, in0=dtt, in1=Abc, op=ALU.mult), reads=[bdtt, b_Abc], writes=[b_av])
                    bk = mm_bank()
                    pc = P.ps(bk)
                    P.op("pe", lambda e, pc=pc: e.matmul(pc[:, 0:16], lhsT=M_le, rhs=av[:, 0:16], start=True, stop=True), reads=[b_cmask, b_av], writes=[P.pbuf[bk]], signal=False)
                    P.op("pe", lambda e, pc=pc: e.matmul(pc[:, 16:32], lhsT=M_ge, rhs=av[:, 16:32], start=True, stop=True), reads=[b_cmask, b_av], writes=[P.pbuf[bk]])
                    P.op("act", lambda e, pc=pc: e.activation(out=dec[:, 0:32], in_=pc[:, 0:32], func=AF.Exp), reads=[P.pbuf[bk]], writes=[b_dec])
                    for d in range(2):
                        xw_, bxw = xw[d]
                        P.op("dve", lambda e, d=d, xw_=xw_, dtt=dtt: e.tensor_tensor(out=xw_.rearrange("p (h c) -> p h c", h=16), in0=xs3,
                                                                                   in1=bc_last(dtt[:, d * 16:(d + 1) * 16], 64), op=ALU.mult),
                             reads=[bxp, bdtt], writes=[bxw])
                    bg = mm_bank()
                    for g in range(2):
                        P.op("pe", lambda e, g=g: e.matmul(P.ps(bg)[:, g * 128:(g + 1) * 128], lhsT=bcT[:, g, :], rhs=bcT[:, 2 + g, :], start=True, stop=True),
                             reads=[b_bcT], writes=[P.pbuf[bg]], signal=(g == 1))
                    P.op("dve", lambda e: e.tensor_tensor(out=Gm[:, 0:2, :], in0=P.ps(bg)[:, 0:256].rearrange("p (g n) -> p g n", g=2),
                                                          in1=bc_mid(M_le, 2), op=ALU.mult), reads=[P.pbuf[bg], b_cmask], writes=[b_Gm])
                    P.op("dve", lambda e: e.tensor_tensor(out=Gm[:, 2:4, :], in0=P.ps(bg)[:, 0:256].rearrange("p (g n) -> p g n", g=2),
                                                          in1=bc_mid(M_ge, 2), op=ALU.mult), reads=[P.pbuf[bg], b_cmask], writes=[b_Gm])
                    P.op("dve", lambda e: e.tensor_copy(out=ahl[:, 0, :], in_=av), reads=[b_av], writes=[b_ahl])
                    P.op("dve", lambda e: e.tensor_tensor(out=ahl[:, 1, :], in0=av, in1=ahl[:, 0, :], op=ALU.subtract), reads=[b_av, b_ahl], writes=[b_ahl])
                    for d in range(2):
                        Ed_, bEd = Ed[d]
                        Md_, bMd = Md[d]
                        um = cmb[:, d, :]
                        msk = M_gt if d == 0 else M_lt
                        for hl in range(2):
                            Xq, bXq = Xd[hl]
                            P.op("pool" if hl == 0 else "dve", lambda e, d=d, hl=hl, Xq=Xq, msk=msk: e.tensor_tensor(
                                out=Xq, in0=bc_last(ahl[:, hl, d * 16:(d + 1) * 16], 128), in1=bc_mid(msk, 16), op=ALU.mult),
                                reads=[b_ahl, b_cmask], writes=[bXq])
                        for q4 in range(4):
                            bnk = mm_bank()
                            for jj in range(4):
                                j = q4 * 4 + jj
                                for hl in range(2):
                                    P.op("pe", lambda e, j=j, jj=jj, bnk=bnk, hl=hl, um=um: e.matmul(P.ps(bnk)[:, jj * 128:(jj + 1) * 128], lhsT=Xd[hl][0][:, j, :], rhs=um,
                                                                                                   start=(hl == 0), stop=(hl == 1)),
                                         reads=[Xd[hl][1], b_cmb], writes=[P.pbuf[bnk]], signal=(jj == 3 and hl == 1))
                            P.op("act", lambda e, q4=q4, bnk=bnk, Ed_=Ed_: e.activation(out=Ed_[:, q4 * 4:(q4 + 1) * 4, :],
                                                                                       in_=P.ps(bnk).rearrange("p (h n) -> p h n", h=4), func=AF.Exp),
                                 reads=[P.pbuf[bnk]], writes=[bEd])
                        for g in range(2):
                            P.op("dve", lambda e, g=g, d=d, Ed_=Ed_, Md_=Md_: e.tensor_tensor(out=Md_[:, g * 8:(g + 1) * 8, :], in0=Ed_[:, g * 8:(g + 1) * 8, :],
                                                                                           in1=bc_mid(Gm[:, d * 2 + g, :], 8), op=ALU.mult),
                                 reads=[bEd, b_Gm], writes=[bMd])
                    pin = (mm_bank(), mm_bank())
                    for j in range(16):
                        bnk = pin[j // 8]
                        o = P.ps(bnk)[:, (j % 8) * 64:(j % 8 + 1) * 64]
                        P.op("pe", lambda e, j=j, o=o: e.matmul(o, lhsT=Md[0][0][:, j, :], rhs=xw[0][0][:, j * 64:(j + 1) * 64], start=True, stop=False),
                             reads=[Md[0][1], xw[0][1]], writes=[P.pbuf[bnk]], signal=False)
                        P.op("pe", lambda e, j=j, o=o: e.matmul(o, lhsT=Md[1][0][:, j, :], rhs=xw[1][0][:, j * 64:(j + 1) * 64], start=False, stop=True),
                             reads=[Md[1][1], xw[1][1]], writes=[P.pbuf[bnk]], signal=(j % 8 == 7))
                    yw16 = yw.rearrange("p (h c) -> p h c", h=16)
                    yt16 = yt.rearrange("p (h c) -> p h c", h=16)
                    for d in range(2):
                        po = (mm_bank(), mm_bank())
                        srcS = stf_[:, 1024:2048] if d == 0 else Sbb[1][0]
                        bsrc = bstf if d == 0 else Sbb[1][1]
                        for g in range(2):
                            P.op("pe", lambda e, g=g, po=po, srcS=srcS: e.matmul(P.ps(po[g]), lhsT=bcT[:, 2 + g, :], rhs=srcS[:, g * 512:(g + 1) * 512], start=True, stop=True),
                                 reads=[b_bcT, bsrc], writes=[P.pbuf[po[g]]])
                        for g in range(2):
                            sl = slice(g * 8, (g + 1) * 8)
                            P.op("dve", lambda e, g=g, d=d, sl=sl, po=po: e.tensor_tensor(out=yt16[:, sl, :], in0=P.ps(po[g]).rearrange("p (h c) -> p h c", h=8),
                                                                                        in1=bc_last(dec[:, d * 16 + g * 8:d * 16 + (g + 1) * 8], 64), op=ALU.mult),
                                 reads=[P.pbuf[po[g]], b_dec], writes=[b_yt])
                            if d == 0:
                                P.op("dve", lambda e, g=g, sl=sl: e.tensor_tensor(out=yw16[:, sl, :], in0=P.ps(pin[g]).rearrange("p (h c) -> p h c", h=8),
                                                                                in1=yt16[:, sl, :], op=ALU.add),
                                     reads=[P.pbuf[pin[g]], b_yt], writes=[b_yw])
                        if d == 1:
                            P.op("pool", lambda e: e.tensor_tensor(out=yw, in0=yw, in1=yt, op=ALU.add), reads=[b_yw, b_yt], writes=[b_yw])
                    P.op("dve", lambda e: e.tensor_tensor(out=yt16, in0=xs3, in1=bc_last(dskb, 64), op=ALU.mult), reads=[bxp, b_dsk], writes=[b_yt])
                    P.op("pool", lambda e: e.tensor_tensor(out=yw, in0=yw, in1=yt, op=ALU.add), reads=[b_yw, b_yt], writes=[b_yw])
                    P.op("dve", lambda e, sgz_=sgz_: e.tensor_tensor(out=yw, in0=yw, in1=sgz_[:, 1024:2048], op=ALU.mult), reads=[b_yw, bsgz], writes=[b_yw])
                    P.op("dve", lambda e: e.memset(ysm[:, 24:25], 0.0), writes=[b_ysm])
                    P.op("act", lambda e: e.activation(out=ysq, in_=yw, func=AF.Square, accum_out=ysm[:, 24:25]), reads=[b_yw], writes=[b_ysq, b_ysm])
                    rstd_of(ysm[:, 24:25], 1024, 1, ysm[:, 25:26], ysm[:, 26:27], b_ysm, b_ysm, b_ysm)
                    P.op("dve", lambda e, yo_=yo_: e.scalar_tensor_tensor(out=yo_[:, 1024:2048], in0=yw, scalar=ysm[:, 26:27], in1=snwb, op0=ALU.mult, op1=ALU.mult),
                         reads=[b_yw, b_ysm, b_snw], writes=[byo])
                    P.dma("pool", y_d[rows, :], yo_, reads=[byo], writes=[B_y[tt]], sbuf=byo)
                    for ty in range(2):
                        sv, bsv = Sb[ty]
                        if ty == 0:
                            P.op("dve", lambda e, sv=sv: e.tensor_tensor(out=sv.rearrange("p (h c) -> p h c", h=8), in0=sv.rearrange("p (h c) -> p h c", h=8),
                                                                        in1=bc_last(rtab[:, 2, 8:16], 128), op=ALU.mult), reads=[bsv, b_rtab], writes=[bsv])
                        else:
                            P.op("dve", lambda e, sv=sv, cdb_=cdb_: e.tensor_tensor(out=sv.rearrange("p (h c) -> p h c", h=16), in0=sv.rearrange("p (h c) -> p h c", h=16),
                                                                                   in1=bc_last(cdb_, 64), op=ALU.mult), reads=[bsv, bcdb], writes=[bsv])
                        P.op("pool", lambda e, sv=sv, ty=ty, csb_=csb_: e.tensor_tensor(out=sv, in0=sv, in1=csb_[:, ty * 1024:(ty + 1) * 1024], op=ALU.add),
                             reads=[bsv, bcsb], writes=[bsv])
                        P.op("act", lambda e, ty=ty, sv=sv: e.copy(out=Sbb[ty][0], in_=sv), reads=[bsv], writes=[Sbb[ty][1]])
                if not is_s:
                    for ty in range(2):
                        sv, bsv = Sb[ty]
                        dst = (ns_ret if ty == 0 else ns_ssd)[sidx, l, 1]
                        P.dma("pool", dst, sv, reads=[bsv], writes=[B_out], sbuf=bsv)

            if EXCH:
                rl = xrow(NSEQ - 1, CS * T - 1)
                P.dma("sp", exrow_in, xpre_d[rl:rl + 1, :], reads=[B_xpre], writes=[B_exrin], sem_key=("x", "exrow"))
                P.collective("AllGather", PAIRS, exrow_in, exrow_out, reads=[B_exrin], writes=[B_exrout], inc=1)
                r0_, b0_ = xsh[0][0]
                r1_, b1_ = xsh[0][1]
                r2_, b2_ = xsh[0][2]
                P.dma("sp", r0_[0:1, :], exrow_out[0:1, :], reads=[B_exrout], writes=[b0_], sbuf=b0_)
                P.dma("sp", r1_[0:1, :], exrow_out[1:2, :], reads=[B_exrout], writes=[b1_], sbuf=b1_)
                P.op("dve", lambda e: e.tensor_scalar(out=r2_[0:1, :], in0=r0_[0:1, :], scalar1=psel[0:1, 0:1], scalar2=None, op0=ALU.mult),
                     reads=[b0_, b_psel], writes=[b2_])
                P.op("dve", lambda e: e.scalar_tensor_tensor(out=r2_[0:1, :], in0=r1_[0:1, :], scalar=psel[0:1, 1:2], in1=r2_[0:1, :], op0=ALU.mult, op1=ALU.add),
                     reads=[b1_, b_psel, b2_], writes=[b2_])
                P.dma("sp", xpre_d[rl + 1:rl + 2, :], r2_[0:1, :], reads=[b2_], writes=[B_xpre], sbuf=b2_)
                run_pass1(seqs[-1])
                for sq in seqs[:-1]:
                    run_pass1(sq)
                    run_pass2(sq)
                run_pass2(seqs[-1])
            else:
                for sq in seqs:
                    run_pass1(sq)
                    run_pass2(sq)
            P.barrier()
            P.new_phase()
            P.top = mB

            mC = P.top
            hT, b_hT = P.alloc("hTc", [16, 512], BF16)
            uT, b_uT = P.alloc("uT", [64, 512], BF16)
            xg, b_xg0 = P.alloc("xg", [4, D])
            b_xg = [Buf(f"xg{j}") for j in range(4)]
            modc = [P.alloc(f"modc{i}", [D], BF16) for i in range(4)]
            modl, b_modl = P.alloc("modl", [D])
            yb = [P.alloc(f"yb{i}", [D], BF16) for i in range(1)]
            h2b, b_h2b = P.alloc("h2b", [D], BF16)
            tmpc = [P.alloc(f"tmpc{i}", [512]) for i in range(2)]
            junk, b_junk = h2b, b_h2b
            small, b_small = P.alloc("smallC", [8])
            if last:
                fnb, b_fnb = P.alloc("fnb", [D], BF16)
                P.dma("sp", modl, fnw[0, :].partition_broadcast(128), reads=[d_const], writes=[b_modl], sbuf=b_modl)
                P.op("dve", lambda e: e.tensor_copy(out=fnb, in_=modl), reads=[b_modl], writes=[b_fnb])
            cur_ci = -1
            tn = 0
            for (tiles, ci) in groups:
                G = len(tiles) * T
                if ci != cur_ci:
                    cur_ci = ci
                    for k, slot in enumerate((2, 4, 3, 5)):
                        P.dma("sp", modl, modrow(l, ci, slot).partition_broadcast(128), reads=[B_mod], writes=[b_modl], sbuf=b_modl)
                        P.op("dve", lambda e, k=k: e.tensor_copy(out=modc[k][0], in_=modl), reads=[b_modl], writes=[modc[k][1]])
                (g1, bg1), (a2, ba2), (sh2, bsh2), (g2, bg2) = modc
                for j, tt in enumerate(tiles):
                    yv, byv = yb[0]
                    P.dma("sp", yv, y_d[tt * T:(tt + 1) * T, :], reads=[B_y[tt]], writes=[byv], sbuf=byv)
                    P.dma("sp", xg[:, j, :], x_src[tt * T:(tt + 1) * T, :], reads=[B_xsrc(tt)], writes=[b_xg[j]], sbuf=b_xg[j])
                    transposes_to(yv, byv, 16, lambda i, n, j=j: hT[:, i:i + n, j * T:(j + 1) * T], b_hT)
                for cb in range(4):
                    wt, bw = next_w()
                    for j, tt in enumerate(tiles):
                        bank = mm_bank()
                        for kc in range(16):
                            P.op("pe", lambda e, kc=kc, j=j, wt=wt, bank=bank: e.matmul(P.ps(bank), lhsT=hT[:, kc, j * T:(j + 1) * T], rhs=wt[:, kc, :],
                                                                                       start=(kc == 0), stop=(kc == 15)),
                                 reads=[b_hT, bw], writes=[P.pbuf[bank]], signal=(kc == 15))
                        tm, btm = tmpc[tn % 2]
                        tn += 1
                        P.op("dve", lambda e, tm=tm, bank=bank, cb=cb: e.tensor_tensor(out=tm, in0=P.ps(bank), in1=g1[:, cb * 512:(cb + 1) * 512], op=ALU.mult),
                             reads=[P.pbuf[bank], bg1], writes=[btm])
                        P.op("pool", lambda e, tm=tm, j=j, cb=cb: e.tensor_tensor(out=xg[:, j, cb * 512:(cb + 1) * 512], in0=xg[:, j, cb * 512:(cb + 1) * 512], in1=tm, op=ALU.add),
                             reads=[btm, b_xg[j]], writes=[b_xg[j]])
                for j, tt in enumerate(tiles):
                    xj = xg[:, j, :]
                    P.op("dve", lambda e: e.memset(small[:, 0:1], 0.0), writes=[b_small])
                    P.op("act", lambda e, xj=xj: e.activation(out=junk, in_=xj, func=AF.Square, accum_out=small[:, 0:1]), reads=[b_xg[j]], writes=[b_junk, b_small])
                    rstd_of(small[:, 0:1], D, 1, small[:, 1:2], small[:, 2:3], b_small, b_small, b_small)
                    P.op("dve", lambda e, xj=xj: e.scalar_tensor_tensor(out=modl, in0=xj, scalar=small[:, 2:3], in1=a2, op0=ALU.mult, op1=ALU.mult),
                         reads=[b_xg[j], b_small, ba2], writes=[b_modl])
                    P.op("pool", lambda e: e.tensor_tensor(out=h2b, in0=modl, in1=sh2, op=ALU.add), reads=[b_modl, bsh2], writes=[b_h2b])
                    transposes_to(h2b, b_h2b, 16, lambda i, n, j=j: hT[:, i:i + n, j * T:(j + 1) * T], b_hT)
                for fb in range(16):
                    if fb % 2 == 0:
                        pump_convert(2)
                    wt, bw = next_w()
                    for fc in range(4):
                        bank = mm_bank()
                        for kc in range(16):
                            P.op("pe", lambda e, kc=kc, fc=fc, wt=wt, bank=bank, G=G: e.matmul(P.ps(bank)[:, 0:G], lhsT=wt[:, kc, fc * 128:(fc + 1) * 128], rhs=hT[:, kc, 0:G],
                                                                                              start=(kc == 0), stop=(kc == 15)),
                                 reads=[b_hT, bw], writes=[P.pbuf[bank]], signal=(kc == 15))
                        P.op("act" if fc % 2 == 0 else "dve",
                             (lambda e, bank=bank, fb=fb, fc=fc, G=G: e.activation(out=uT[:, fb * 4 + fc, 0:G], in_=P.ps(bank)[:, 0:G], func=AF.Relu)) if fc % 2 == 0 else
                             (lambda e, bank=bank, fb=fb, fc=fc, G=G: e.tensor_scalar_max(out=uT[:, fb * 4 + fc, 0:G], in0=P.ps(bank)[:, 0:G], scalar1=0.0)),
                             reads=[P.pbuf[bank]], writes=[b_uT])
                        P.op("pool", lambda e, fb=fb, fc=fc, G=G: e.tensor_tensor(out=uT[:, fb * 4 + fc, 0:G], in0=uT[:, fb * 4 + fc, 0:G], in1=uT[:, fb * 4 + fc, 0:G], op=ALU.mult),
                             reads=[b_uT], writes=[b_uT])
                for cb in range(4):
                    banks = [2 + ((cb * 4 + j) % 6) for j in range(len(tiles))]
                    for kp in range(4):
                        wt, bw = next_w()
                        for j, tt in enumerate(tiles):
                            bank = banks[j]
                            for kc in range(16):
                                P.op("pe", lambda e, kc=kc, j=j, wt=wt, bank=bank, kp=kp: e.matmul(P.ps(bank), lhsT=uT[:, kp * 16 + kc, j * T:(j + 1) * T], rhs=wt[:, kc, :],
                                                                                                 start=(kp == 0 and kc == 0), stop=(kp == 3 and kc == 15)),
                                     reads=[b_uT, bw], writes=[P.pbuf[bank]], signal=(kc == 15))
                    for j, tt in enumerate(tiles):
                        bank = banks[j]
                        tm, btm = tmpc[tn % 2]
                        tn += 1
                        P.op("dve", lambda e, tm=tm, bank=bank, cb=cb: e.tensor_tensor(out=tm, in0=P.ps(bank), in1=g2[:, cb * 512:(cb + 1) * 512], op=ALU.mult),
                             reads=[P.pbuf[bank], bg2], writes=[btm])
                        P.op("pool", lambda e, tm=tm, j=j, cb=cb: e.tensor_tensor(out=xg[:, j, cb * 512:(cb + 1) * 512], in0=xg[:, j, cb * 512:(cb + 1) * 512], in1=tm, op=ALU.add),
                             reads=[btm, b_xg[j]], writes=[b_xg[j]])
                for j, tt in enumerate(tiles):
                    xj = xg[:, j, :]
                    if not last:
                        P.dma("pool", xs_d[tt * T:(tt + 1) * T, :], xj, reads=[b_xg[j]], writes=[B_xs[tt]], sbuf=b_xg[j])
                    else:
                        P.op("dve", lambda e: e.memset(small[:, 0:1], 0.0), writes=[b_small])
                        P.op("act", lambda e, xj=xj: e.activation(out=junk, in_=xj, func=AF.Square, accum_out=small[:, 0:1]), reads=[b_xg[j]], writes=[b_junk, b_small])
                        rstd_of(small[:, 0:1], D, 1, small[:, 1:2], small[:, 2:3], b_small, b_small, b_small)
                        P.op("dve", lambda e, xj=xj: e.scalar_tensor_tensor(out=xj, in0=xj, scalar=small[:, 2:3], in1=fnb, op0=ALU.mult, op1=ALU.mult),
                             reads=[b_xg[j], b_small, b_fnb], writes=[b_xg[j]])
                        P.dma("pool", y_out[tt * T:(tt + 1) * T, :], xj, reads=[b_xg[j]], writes=[B_out], sbuf=b_xg[j])
            pump_convert(10 ** 6)
            P.barrier()
            P.new_phase()
            P.top = mC

        P.barrier()
        print("instruction counts", P.n_inst, "sems", P.nsem)
        P.emit()
    return nc


def _consts():
    s = np.arange(128)
    tt, ss = s[:, None], s[None, :]
    cm = np.stack([(tt > ss), (tt <= ss), (tt < ss), (tt >= ss), np.ones((128, 128), bool)], 1).astype(np.float32)
    diff = (ss - tt).astype(np.float32)
    pidx = np.stack([s + 1, 128 - 1 - s, 128 - s, s], 1).astype(np.float32)
    return cm, diff, pidx


def _rope(length):
    GRID_W = 64
    pos = np.arange(length)
    row = (pos // GRID_W).astype(np.float32)
    col = (pos % GRID_W).astype(np.float32)
    half = 64
    inv = (1.0 / (np.float32(10000.0) ** (np.arange(0, half, 2, dtype=np.float32) / np.float32(half)))).astype(np.float32)
    ang = np.concatenate([row[:, None] * inv, col[:, None] * inv], -1).astype(np.float32)
    return np.concatenate([np.cos(ang), np.sin(ang)], -1).astype(np.float32)


_NC_CACHE = {}
DEBUG_SCRATCH = False
_LAST_RES = [None]


def kernel(x_prompt, x_sample, state_ret, state_ssd, c, c_ctx, w_ada, b_ada, norm1_w, w_in,
           ret_log_decay, conv_w, conv_b, dt_bias, a_log, d_skip, ssd_norm_w, w_out, norm2_w,
           w_ff1, w_ff2, final_norm_w, _n_cores=8):
    f = lambda a: np.ascontiguousarray(np.asarray(a, dtype=np.float32))
    x_prompt, x_sample, state_ret, state_ssd, c, c_ctx = map(f, (x_prompt, x_sample, state_ret, state_ssd, c, c_ctx))
    BP, PL, _ = x_prompt.shape
    BS, SLEN, _ = x_sample.shape
    DEPTH = w_ada.shape[0]
    n_cores = _n_cores
    EXCH = (n_cores == 2 * BS) and (BP % n_cores == 0) and (SLEN % 256 == 0)
    if EXCH:
        n_work = n_cores
        NP = BP // n_cores
        SL = SLEN // 2
    else:
        n_work = BS
        NP = BP // n_work
        SL = SLEN
    key = (DEPTH, NP, PL, SL, EXCH)
    if key not in _NC_CACHE:
        _NC_CACHE[key] = build_program(DEPTH, NP, PL, SL, EXCH)
    nc = _NC_CACHE[key]
    cm, diff, pidx = _consts()
    rope = _rope(SLEN)
    w_in = f(w_in)
    rld = f(ret_log_decay)
    cw = f(conv_w)
    dtb = f(dt_bias)
    alg = f(a_log)
    base = dict(
        cmask=cm, diffm=diff, pidx=pidx, zrow=np.zeros((1, 1536), ml_dtypes.bfloat16),
        conv_b=f(conv_b), d_skip=f(d_skip),
        ssd_norm_w=f(ssd_norm_w), w_out=f(w_out), w_ff1=f(w_ff1), w_ff2=f(w_ff2),
        final_norm_w=f(final_norm_w).reshape(1, D),
    )
    w_ada = f(w_ada)
    b_ada = f(b_ada)
    if EXCH:
        ada_half = [dict(w_ada_h=np.ascontiguousarray(w_ada[:, :, h * 3 * D:(h + 1) * 3 * D]),
                         b_ada_h=np.ascontiguousarray(b_ada[:, h * 3 * D:(h + 1) * 3 * D]),
                         nw_h=f(norm1_w) if h == 0 else f(norm2_w)) for h in range(2)]
    else:
        base.update(w_ada=w_ada, b_ada=b_ada, norm1_w=f(norm1_w), norm2_w=f(norm2_w))
    variants = {}
    for flip in ((False, True) if EXCH else (False,)):
        if not flip:
            v = dict(w_in=w_in, ret_log_decay=rld.reshape(DEPTH, 16), conv_w=cw.reshape(DEPTH, 3 * 1536),
                     dt_bias=dtb.reshape(DEPTH, 32), a_log=alg.reshape(DEPTH, 32))
        else:
            w2 = w_in.copy()
            w2[:, :, 6656:6672] = w_in[:, :, 6672:6688]
            w2[:, :, 6672:6688] = w_in[:, :, 6656:6672]
            v = dict(w_in=w2, ret_log_decay=np.ascontiguousarray(rld[:, ::-1]).reshape(DEPTH, 16),
                     conv_w=np.ascontiguousarray(cw[:, ::-1]).reshape(DEPTH, 3 * 1536),
                     dt_bias=np.ascontiguousarray(dtb[:, ::-1]).reshape(DEPTH, 32),
                     a_log=np.ascontiguousarray(alg[:, ::-1]).reshape(DEPTH, 32))
        variants[flip] = v
    tr = lambda s_: np.ascontiguousarray(s_.transpose(0, 1, 4, 2, 3)).reshape(DEPTH, 2, 128, 1024)
    in_maps = []
    meta = []
    for core in range(n_cores):
        if EXCH:
            b, half = core // 2, core % 2
            flip = (half == 1)
            plist = list(range(core * NP, (core + 1) * NP))
            xs_ = x_sample[b, half * SL:(half + 1) * SL]
            pos = np.arange(half * SL, (half + 1) * SL)
            sr, ss = tr(state_ret[b]), tr(state_ssd[b])
            if flip:
                xs_ = xs_[::-1]
                pos = pos[::-1]
                sr, ss = np.ascontiguousarray(sr[:, ::-1]), np.ascontiguousarray(ss[:, ::-1])
            xps = [x_prompt[p][::-1] if flip else x_prompt[p] for p in plist]
            psel = np.zeros((128, 2), np.float32)
            psel[:, 1 - half] = 1.0
        else:
            w = core % n_work
            b, half, flip = w, 0, False
            plist = list(range(w * NP, (w + 1) * NP))
            xs_ = x_sample[b]
            pos = np.arange(SLEN)
            sr, ss = tr(state_ret[b]), tr(state_ssd[b])
            xps = [x_prompt[p] for p in plist]
            psel = np.zeros((128, 2), np.float32)
        m = dict(base)
        m.update(variants[flip])
        if EXCH:
            m.update(ada_half[half])
        m.update(x_in=np.ascontiguousarray(np.concatenate(xps + [xs_], 0)), cond=np.ascontiguousarray(np.stack([c_ctx, c[b]], 0)),
                 st_ret=sr, st_ssd=ss, rope_cs=np.ascontiguousarray(rope[pos]), psel=psel)
        in_maps.append(m)
        meta.append((b, half, flip, plist))
    res = run_bass_kernel_spmd(nc, in_maps, core_ids=list(range(n_cores)))
    _LAST_RES[0] = res
    y_prompt = np.zeros((BP, PL, D), np.float32)
    y_sample = np.zeros((BS, SLEN, D), np.float32)
    nsr = np.zeros((BP, DEPTH, 2, 8, 128, 128), np.float32)
    nss = np.zeros((BP, DEPTH, 2, 16, 64, 128), np.float32)
    for core in range(n_work):
        b, half, flip, plist = meta[core]
        r = res.results[core]
        yo = r["y_out"]
        a = r["ns_ret"].reshape(NP, DEPTH, 2, 128, 8, 128).transpose(0, 1, 2, 4, 5, 3)
        bb = r["ns_ssd"].reshape(NP, DEPTH, 2, 128, 16, 64).transpose(0, 1, 2, 4, 5, 3)
        for i, p in enumerate(plist):
            yp = yo[i * PL:(i + 1) * PL]
            y_prompt[p] = yp[::-1] if flip else yp
            nsr[p] = a[i][:, ::-1] if flip else a[i]
            nss[p] = bb[i][:, ::-1] if flip else bb[i]
        ys = yo[NP * PL:]
        y_sample[b, half * SL:(half + 1) * SL] = ys[::-1] if flip else ys
    return (y_prompt, y_sample, nsr, nss)
```

```python
import types
import numpy as np
import ml_dtypes
from contextlib import ExitStack
import concourse.bass as bass
import concourse.mybir as mybir
from concourse.bass_utils import run_bass_kernel_spmd

F32 = mybir.dt.float32
BF16 = mybir.dt.bfloat16
ALU = mybir.AluOpType
AF = mybir.ActivationFunctionType
AX = mybir.AxisListType

EPOCH = 30000


class Tok:
    __slots__ = ("key", "val", "eng")

    def __init__(self, eng):
        self.key = None
        self.val = None
        self.eng = eng


class Buf:
    __slots__ = ("name", "w", "r", "dsem")

    def __init__(self, name):
        self.name = name
        self.w = None
        self.r = []
        self.dsem = None


class Prog:
    CE = ("pe", "act", "dve", "pool")
    ALLE = ("pe", "act", "dve", "pool", "sp")

    def __init__(self, nc, stack, arena_words=53200):
        self.nc = nc
        self.stack = stack
        self.ops = {e: [] for e in self.ALLE}
        self.sems = {}
        self.ecount = {e: 0 for e in self.CE}
        self.waited = {e: {} for e in self.ALLE}
        self.pe_pending = []
        self.pe_last_rec = None
        self.dma_sems = {}
        self.nsem = 0
        self.AW = arena_words
        self.arena = stack.enter_context(nc.sbuf_tensor("arena", [128, arena_words], F32))
        self.top = 0
        self.psum = [stack.enter_context(nc.psum_tensor(f"psb{i}", [128, 512], F32)) for i in range(8)]
        self.pbuf = [Buf(f"psum{i}") for i in range(8)]
        self.n_inst = {e: 0 for e in self.ALLE}
        self.dsem_ctr = 0
        self.dsem_base = 0

    def pin(self, buf):
        buf.dsem = ("d", self.dsem_ctr)
        self.dsem_ctr += 1
        self.dsem_base = self.dsem_ctr

    def new_phase(self):
        self.dsem_ctr = self.dsem_base

    def alloc(self, name, free_shape, dtype=F32):
        n = int(np.prod(free_shape))
        nw = n if dtype == F32 else (n + 1) // 2
        nw = (nw + 7) // 8 * 8
        off = self.top
        self.top += nw
        assert self.top <= self.AW, f"arena overflow at {name}: {self.top}"
        v = self.arena[:, off:off + nw]
        if dtype != F32:
            v = v.bitcast(dtype)
        v = v[:, 0:n]
        if len(free_shape) == 2:
            v = v.rearrange("p (a b) -> p a b", a=free_shape[0])
        elif len(free_shape) == 3:
            v = v.rearrange("p (a b c) -> p a b c", a=free_shape[0], b=free_shape[1])
        return v, Buf(name)

    def ps(self, i, dtype=F32):
        v = self.psum[i][:, :]
        if dtype != F32:
            v = v.bitcast(dtype)
        return v

    def _sem(self, key):
        if key not in self.sems:
            self.nsem += 1
            self.sems[key] = self.stack.enter_context(self.nc.semaphore(f"s{self.nsem}"))
        return self.sems[key]

    def _new_signal(self, eng):
        c = self.ecount[eng]
        self.ecount[eng] = c + 1
        key = ("e", eng, c // EPOCH)
        self._sem(key)
        return key, (c % EPOCH) + 1

    def _need(self, eng, tok, raw):
        if tok is None:
            return
        if tok.eng == eng and eng in self.CE and not raw:
            return
        if tok.key is None:
            self._force_pe_signal()
        w = self.waited[eng]
        if w.get(tok.key, 0) >= tok.val:
            return
        w[tok.key] = tok.val
        sem = self.sems[tok.key]
        val = tok.val
        self.ops[eng].append(lambda e, sem=sem, val=val: e.wait_ge(sem, val))
        self.n_inst[eng] += 1

    def _force_pe_signal(self):
        rec = self.pe_last_rec
        assert rec["sig"] is None
        key, val = self._new_signal("pe")
        rec["sig"] = (self.sems[key], 1)
        for t in self.pe_pending:
            t.key, t.val = key, val
        self.pe_pending = []

    def _deps(self, eng, reads, writes):
        for b in reads:
            self._need(eng, b.w, True)
        for b in writes:
            self._need(eng, b.w, False)
            for t in b.r:
                self._need(eng, t, False)

    def _commit(self, tok, reads, writes):
        for b in reads:
            if len(b.r) > 24:
                d = {}
                rest = []
                for t in b.r:
                    if t.key is None:
                        rest.append(t)
                    elif t.key not in d or d[t.key].val < t.val:
                        d[t.key] = t
                b.r = rest + list(d.values())
            b.r.append(tok)
        for b in writes:
            b.w = tok
            b.r = []

    @staticmethod
    def _freeze(fn):
        if fn.__closure__ is None:
            return fn
        cells = []
        for c in fn.__closure__:
            try:
                cells.append(types.CellType(c.cell_contents))
            except ValueError:
                cells.append(c)
        g = types.FunctionType(fn.__code__, fn.__globals__, fn.__name__, fn.__defaults__, tuple(cells))
        g.__kwdefaults__ = fn.__kwdefaults__
        return g

    def op(self, eng, fn, reads=(), writes=(), signal=True):
        fn = self._freeze(fn)
        self._deps(eng, reads, writes)
        tok = Tok(eng)
        rec = {"sig": None}
        if signal:
            key, val = self._new_signal(eng)
            tok.key, tok.val = key, val
            rec["sig"] = (self.sems[key], 1)
            if eng == "pe":
                for t in self.pe_pending:
                    t.key, t.val = key, val
                self.pe_pending = []
        else:
            assert eng == "pe"
            self.pe_pending.append(tok)
        if eng == "pe":
            self.pe_last_rec = rec

        def run(e, fn=fn, rec=rec):
            ins = fn(e)
            if rec["sig"] is not None:
                ins.then_inc(rec["sig"][0], rec["sig"][1])
        self.ops[eng].append(run)
        self.n_inst[eng] += 1
        self._commit(tok, reads, writes)
        return tok

    def dma(self, q, out, in_, reads=(), writes=(), sbuf=None, sem_key=None, **kw):
        self._deps(q, reads, writes)
        if sem_key is None:
            if sbuf.dsem is None:
                sbuf.dsem = ("d", self.dsem_ctr)
                self.dsem_ctr += 1
            sem_key = sbuf.dsem
        self._sem(sem_key)
        cnt = self.dma_sems.get(sem_key, 0) + 16
        self.dma_sems[sem_key] = cnt
        tok = Tok(None)
        tok.key, tok.val = sem_key, cnt
        sem = self.sems[sem_key]

        def run(e, out=out, in_=in_, sem=sem, kw=kw):
            e.dma_start(out=out, in_=in_, **kw).then_inc(sem, 16)
        self.ops[q].append(run)
        self.n_inst[q] += 1
        self._commit(tok, reads, writes)
        return tok

    def collective(self, kind, groups, in_ap, out_ap, reads=(), writes=(), inc=16):
        q = "pool"
        self._deps(q, reads, writes)
        sem_key = ("cc",)
        self._sem(sem_key)
        cnt = self.dma_sems.get(sem_key, 0) + inc
        self.dma_sems[sem_key] = cnt
        tok = Tok(None)
        tok.key, tok.val = sem_key, cnt
        sem = self.sems[sem_key]

        def run(e, kind=kind, groups=groups, in_ap=in_ap, out_ap=out_ap, sem=sem, inc=inc):
            e.collective_compute(kind, ALU.bypass, replica_groups=groups, ins=[in_ap], outs=[out_ap]).then_inc(sem, inc)
        self.ops[q].append(run)
        self.n_inst[q] += 1
        self._commit(tok, reads, writes)
        return tok

    def barrier(self):
        toks = []
        for e in self.CE:
            if e == "pe" and self.pe_pending:
                self._force_pe_signal()
            c = self.ecount[e]
            if c > 0:
                t = Tok(None)
                t.key = ("e", e, (c - 1) // EPOCH)
                t.val = ((c - 1) % EPOCH) + 1
                toks.append(t)
        for k, cnt in self.dma_sems.items():
            t = Tok(None)
            t.key, t.val = k, cnt
            toks.append(t)
        for e in self.ALLE:
            for t in toks:
                self._need(e, t, True)

    def emit(self):
        ops = self.ops
        with self.nc.Block() as block:
            @block.tensor
            def _(e):
                for f in ops["pe"]:
                    f(e)

            @block.scalar
            def _(e):
                for f in ops["act"]:
                    f(e)

            @block.vector
            def _(e):
                for f in ops["dve"]:
                    f(e)

            @block.gpsimd
            def _(e):
                for f in ops["pool"]:
                    f(e)

            @block.sync
            def _(e):
                for f in ops["sp"]:
                    f(e)


D = 2048
DIN = 6688
HR = 8
HS = 16
PSD = 64
T = 128
DFF = 8192
EPS = 1e-6
DKS = 128 ** -0.5


def bc_last(ap, n):
    return ap.unsqueeze(2).broadcast_to([ap.shape[0], ap.shape[1], n])


def bc_mid(ap, n):
    return ap.unsqueeze(1).broadcast_to([ap.shape[0], n, ap.shape[1]])


def build_program(DEPTH, NP, PL, SL, EXCH=False):
    PAIRS = [[0, 1], [2, 3], [4, 5], [6, 7]]
    NTOK = NP * PL + SL
    NT = NTOK // T
    CP = PL // T
    CS = SL // T
    seqs = [(i * CP, CP, False, i) for i in range(NP)] + [(NP * CP, CS, True, NP)]
    NSEQ = len(seqs)
    groups = []
    pt = list(range(NP * CP))
    for i in range(0, len(pt), 4):
        groups.append((pt[i:i + 4], 0))
    stl = list(range(NP * CP, NT))
    for i in range(0, len(stl), 4):
        groups.append((stl[i:i + 4], 1))

    nc = bass.Bass("TRN2", target_bir_lowering=False)

    def din(name, shape, dt=F32):
        return nc.dram_tensor(name, list(shape), dt, kind="ExternalInput").ap()

    def dout(name, shape, dt=F32):
        return nc.dram_tensor(name, list(shape), dt, kind="ExternalOutput").ap()

    def dscr(name, shape, dt=F32):
        return nc.dram_tensor(name, list(shape), dt, kind=("ExternalOutput" if (DEBUG_SCRATCH and not name.startswith("wbf")) else "Internal")).ap()

    x_in = din("x_in", [NTOK, D])
    cond = din("cond", [2, D])
    st_ret = din("st_ret", [DEPTH, 2, 128, 1024])
    st_ssd = din("st_ssd", [DEPTH, 2, 128, 1024])
    rope_cs = din("rope_cs", [SL, 128])
    cmask_d = din("cmask", [128, 5, 128])
    diff_d = din("diffm", [128, 128])
    pidx_d = din("pidx", [128, 4])
    zrow_d = din("zrow", [1, 1536], BF16)
    psel_d = din("psel", [128, 2])
    if not EXCH:
        w_ada = din("w_ada", [DEPTH, D, 6 * D])
        b_ada = din("b_ada", [DEPTH, 6 * D])
        norm1_w = din("norm1_w", [DEPTH, D])
        norm2_w = din("norm2_w", [DEPTH, D])
    w_in = din("w_in", [DEPTH, D, DIN])
    rld = din("ret_log_decay", [DEPTH, 16])
    conv_w = din("conv_w", [DEPTH, 3 * 1536])
    conv_b = din("conv_b", [DEPTH, 1536])
    dt_bias = din("dt_bias", [DEPTH, 32])
    a_log = din("a_log", [DEPTH, 32])
    d_skip = din("d_skip", [DEPTH, 16])
    ssd_nw = din("ssd_norm_w", [DEPTH, 1024])
    w_out = din("w_out", [DEPTH, D, D])
    w_ff1 = din("w_ff1", [DEPTH, D, DFF])
    w_ff2 = din("w_ff2", [DEPTH, DFF, D])
    fnw = din("final_norm_w", [1, D])

    y_out = dout("y_out", [NTOK, D])
    ns_ret = dout("ns_ret", [NP, DEPTH, 2, 128, 1024])
    ns_ssd = dout("ns_ssd", [NP, DEPTH, 2, 128, 1024])

    xs_d = dscr("xs_d", [NTOK, D])
    qk_d = dscr("qk_d", [NTOK, 2048], BF16)
    v_d = dscr("v_d", [NTOK, 1024], BF16)
    sg_d = dscr("sg_d", [NTOK, 1024], BF16)
    sz_d = dscr("sz_d", [NTOK, 1024], BF16)
    xpre_d = dscr("xpre_d", [NTOK + 2 * NSEQ, 1536], BF16)
    xpost_d = dscr("xpost_d", [NTOK, 1536], BF16)
    dt_d = dscr("dt_d", [NTOK, 32])
    y_d = dscr("y_d", [NTOK, 2048], BF16)
    stf_d = dscr("stf_d", [NT, 128, 2048], BF16)
    cstb_d = dscr("cstb_d", [NT, 128, 2048])
    cdb_d = dscr("cdb_d", [NT, 128, 16])
    mod_d = dscr("mod_d", [DEPTH, 2, 6 * D])
    NWT = 14 + 4 + 16 + 16
    if DEBUG_SCRATCH:
        dbg_h = dout("dbg_h", [NTOK, 2048], BF16)
        dbg_x = dout("dbg_x", [NTOK, 2048])
        dbg_a = dout("dbg_a", [NTOK, 2048])
        dbg_h32 = dout("dbg_h32", [NTOK, 2048])
    wbf = [dict(w0=dscr(f"wbf{i}_in", [D, DIN], BF16), w1=dscr(f"wbf{i}_out", [D, D], BF16),
                w2=dscr(f"wbf{i}_ff1", [D, DFF], BF16), w3=dscr(f"wbf{i}_ff2", [DFF, D], BF16)) for i in range(2)]
    exst_in = dscr("exst_in", [128, 2048])
    exst_out = dscr("exst_out", [256, 2048])
    exrow_in = dscr("exrow_in", [1, 1536], BF16)
    exrow_out = dscr("exrow_out", [2, 1536], BF16)
    if EXCH:
        w_ada_h = din("w_ada_h", [DEPTH, D, 3 * D])
        b_ada_h = din("b_ada_h", [DEPTH, 3 * D])
        nw_h = din("nw_h", [DEPTH, D])
        exmod_in = dscr("exmod_in", [DEPTH * 2, 3 * D])
        exmod_out = dscr("exmod_out", [2 * DEPTH * 2, 3 * D])

    def modrow(l, ci, slot):
        if EXCH:
            r = (slot // 3) * DEPTH * 2 + l * 2 + ci
            return exmod_out[r, (slot % 3) * D:(slot % 3 + 1) * D]
        return mod_d[l, ci, slot * D:(slot + 1) * D]

    def xrow(si, t):
        t0, ncks, _, _ = seqs[si]
        return t0 * T + 2 * si + 1 + t

    with ExitStack() as st:
        P = Prog(nc, st)
        ident, b_ident = P.alloc("ident", [128], BF16)
        cmask, b_cmask = P.alloc("cmask", [5, 128])
        diffm, b_diff = P.alloc("diffm", [128])
        pidx, b_pidx = P.alloc("pidx", [4])
        psel, b_psel = P.alloc("psel", [2])
        NRING = 3
        wring = [P.alloc(f"wring{i}", [16, 512], BF16) for i in range(NRING)]
        base_top = P.top
        for b_ in (b_cmask, b_diff, b_pidx, b_psel, wring[0][1], wring[1][1], wring[2][1]):
            P.pin(b_)
        M_gt, M_le, M_lt, M_ge, M_one = [cmask[:, i, :] for i in range(5)]

        d_const = Buf("d_const")
        B_xs = [Buf(f"xs{t}") for t in range(NT)]
        B_qk = [Buf(f"qk{t}") for t in range(NT)]
        B_v = [Buf(f"v{t}") for t in range(NT)]
        B_sg = [Buf(f"sg{t}") for t in range(NT)]
        B_sz = [Buf(f"sz{t}") for t in range(NT)]
        B_xpre = Buf("xpre")
        B_xpost = [Buf(f"xpost{t}") for t in range(NT)]
        B_dt = [Buf(f"dt{t}") for t in range(NT)]
        B_y = [Buf(f"y{t}") for t in range(NT)]
        B_stf = [Buf(f"stf{t}") for t in range(NT)]
        B_cstb = [Buf(f"cstb{t}") for t in range(NT)]
        B_cdb = [Buf(f"cdb{t}") for t in range(NT)]
        B_mod = Buf("mod")
        B_wbf = [[Buf(f"wbf{i}_{k}") for k in range(4)] for i in range(2)]
        B_out = Buf("outs")
        B_exin = Buf("exin"); B_exout = Buf("exout"); B_exrin = Buf("exrin"); B_exrout = Buf("exrout")

        P.dma("sp", cmask, cmask_d, reads=[d_const], writes=[b_cmask], sbuf=b_cmask)
        P.dma("sp", diffm, diff_d, reads=[d_const], writes=[b_diff], sbuf=b_diff)
        P.dma("sp", pidx, pidx_d, reads=[d_const], writes=[b_pidx], sbuf=b_pidx)
        P.dma("sp", psel, psel_d, reads=[d_const], writes=[b_psel], sbuf=b_psel)
        P.op("pool", lambda e: e.memset(ident, 0.0), writes=[b_ident])
        P.op("pool", lambda e: e.affine_select(out=ident, in_=ident, pattern=[[-1, 128]],
                                               compare_op=ALU.not_equal, fill=1.0, base=0,
                                               channel_multiplier=1), reads=[b_ident], writes=[b_ident])
        for si in range(NSEQ):
            for t in (-1, seqs[si][1] * T):
                r = xrow(si, t)
                P.dma("sp", xpre_d[r:r + 1, :], zrow_d, reads=[d_const], writes=[B_xpre], sem_key=("x", "zrow"))

        def wtile_src(par, i):
            if i < 14:
                nco = 512 if i < 13 else 32
                return 0, wbf[par]["w0"][:, i * 512:i * 512 + nco].rearrange("(kc p) c -> p kc c", p=128), nco
            i -= 14
            if i < 4:
                return 1, wbf[par]["w1"][:, i * 512:(i + 1) * 512].rearrange("(kc p) c -> p kc c", p=128), 512
            i -= 4
            if i < 16:
                return 2, wbf[par]["w2"][:, i * 512:(i + 1) * 512].rearrange("(kc p) c -> p kc c", p=128), 512
            i -= 16
            cb, kp = i // 4, i % 4
            return 3, wbf[par]["w3"][kp * 2048:(kp + 1) * 2048, cb * 512:(cb + 1) * 512].rearrange("(kc p) c -> p kc c", p=128), 512

        conv_jobs = []

        def convert_layer(l):
            par = l % 2
            for kind, (src, R) in enumerate(((w_in[l], D), (w_out[l], D), (w_ff1[l], D), (w_ff2[l], DFF))):
                dst = wbf[par]["w%d" % kind]
                for r0 in range(0, R, 128):
                    conv_jobs.append((par, kind, dst[r0:r0 + 128, :], src[r0:r0 + 128, :]))

        def pump_convert(n):
            for _ in range(min(n, len(conv_jobs))):
                par, kind, dst, src = conv_jobs.pop(0)
                P.dma("pool", dst, src, reads=[d_const], writes=[B_wbf[par][kind]], sem_key=("wc", par, kind))

        class WStream:
            def __init__(self):
                self.seq = []
                self.issued = 0
                self.slot_n = 0

            def extend(self, l, idxs):
                for i in idxs:
                    self.seq.append((l, i))

            def prefetch(self, upto):
                while self.issued < min(upto, len(self.seq)):
                    l, i = self.seq[self.issued]
                    kind, src, nco = wtile_src(l % 2, i)
                    slot = self.issued % NRING
                    wt, bw = wring[slot]
                    P.dma("sp", wt[:, :, 0:nco], src, reads=[B_wbf[l % 2][kind]], writes=[bw], sbuf=bw)
                    self.issued += 1

            def get(self, n):
                self.prefetch(n + NRING)
                return wring[n % NRING]

        WS = WStream()
        wcount = [0]

        def next_w():
            r = WS.get(wcount[0])
            wcount[0] += 1
            return r

        for l in range(DEPTH):
            for g in groups:
                WS.extend(l, range(14))
            for g in groups:
                WS.extend(l, range(14, NWT))

        convert_layer(0)
        pump_convert(10 ** 6)
        m0 = P.top
        cT, b_cT = P.alloc("cT", [16, 2])
        modsb, b_modsb = P.alloc("modsb", [6 * D])
        badab, b_bada = P.alloc("badab", [6 * D])
        nwb, b_nwb = P.alloc("nwb", [2 * D])
        wada = [P.alloc(f"wada{i}", [4096]) for i in range(2)]
        for ci_ in range(2):
            P.dma("sp", cT[:, :, ci_], cond[ci_, :].rearrange("(kc p) -> p kc", p=128), reads=[d_const], writes=[b_cT], sbuf=b_cT,
                  allow_slow_non_contiguous=True)
        P.op("act", lambda e: e.activation(out=cT, in_=cT, func=AF.Silu), reads=[b_cT], writes=[b_cT])
        wn = 0
        for l in (range(DEPTH) if EXCH else []):
            P.dma("sp", badab[0:2, 0:3 * D], b_ada_h[l, :].partition_broadcast(2), reads=[d_const], writes=[b_bada], sbuf=b_bada)
            P.dma("sp", nwb[0:2, 0:D], nw_h[l, :].partition_broadcast(2), reads=[d_const], writes=[b_nwb], sbuf=b_nwb)
            for cg in range(2):
                ncol = 4096 if cg == 0 else 2048
                nbk = ncol // 512
                for kc in range(16):
                    wt, bw = wada[wn % 2]
                    wn += 1
                    P.dma("sp", wt[:, 0:ncol], w_ada_h[l][kc * 128:(kc + 1) * 128, cg * 4096:cg * 4096 + ncol],
                          reads=[d_const], writes=[bw], sbuf=bw)
                    for b in range(nbk):
                        P.op("pe", lambda e, b=b, kc=kc, wt=wt: e.matmul(P.ps(b)[0:2, :], lhsT=cT[:, kc, :], rhs=wt[:, b * 512:(b + 1) * 512],
                                                                         start=(kc == 0), stop=(kc == 15)),
                             reads=[b_cT, bw], writes=[P.pbuf[b]], signal=(kc == 15 or b == nbk - 1))
                for b in range(nbk):
                    c0 = cg * 4096 + b * 512
                    P.op("dve", lambda e, b=b, c0=c0: e.tensor_tensor(out=modsb[0:2, c0:c0 + 512], in0=P.ps(b)[0:2, :],
                                                                      in1=badab[0:2, c0:c0 + 512], op=ALU.add),
                         reads=[P.pbuf[b], b_bada], writes=[b_modsb])
            P.op("dve", lambda e: e.scalar_tensor_tensor(out=modsb[0:2, D:2 * D], in0=modsb[0:2, D:2 * D], scalar=1.0,
                                                         in1=nwb[0:2, 0:D], op0=ALU.add, op1=ALU.mult), reads=[b_modsb, b_nwb], writes=[b_modsb])
            P.dma("sp", exmod_in[l * 2:l * 2 + 2, :], modsb[0:2, 0:3 * D], reads=[b_modsb], writes=[B_mod], sbuf=b_modsb)
        if EXCH:
            B_modin = B_mod
            B_mod = Buf("modout")
            P.collective("AllGather", PAIRS, exmod_in, exmod_out, reads=[B_modin], writes=[B_mod], inc=1)
        for l in ([] if EXCH else range(DEPTH)):
            P.dma("sp", badab[0:2, :], b_ada[l, :].partition_broadcast(2),
                  reads=[d_const], writes=[b_bada], sbuf=b_bada)
            P.dma("sp", nwb[0:2, 0:D], norm1_w[l, :].partition_broadcast(2), reads=[d_const], writes=[b_nwb], sbuf=b_nwb)
            P.dma("sp", nwb[0:2, D:2 * D], norm2_w[l, :].partition_broadcast(2), reads=[d_const], writes=[b_nwb], sbuf=b_nwb)
            for cg in range(3):
                for kc in range(16):
                    wt, bw = wada[wn % 2]
                    wn += 1
                    P.dma("sp", wt, w_ada[l][kc * 128:(kc + 1) * 128, cg * 4096:(cg + 1) * 4096],
                          reads=[d_const], writes=[bw], sbuf=bw)
                    for b in range(8):
                        P.op("pe", lambda e, b=b, kc=kc, wt=wt: e.matmul(P.ps(b)[0:2, :], lhsT=cT[:, kc, :], rhs=wt[:, b * 512:(b + 1) * 512],
                                                                         start=(kc == 0), stop=(kc == 15)),
                             reads=[b_cT, bw], writes=[P.pbuf[b]], signal=(kc == 15 or b == 7))
                for b in range(8):
                    c0 = cg * 4096 + b * 512
                    P.op("dve", lambda e, b=b, c0=c0: e.tensor_tensor(out=modsb[0:2, c0:c0 + 512], in0=P.ps(b)[0:2, :],
                                                                      in1=badab[0:2, c0:c0 + 512], op=ALU.add),
                         reads=[P.pbuf[b], b_bada], writes=[b_modsb])
            for slot, off in ((1, 0), (4, D)):
                P.op("dve", lambda e, slot=slot, off=off: e.scalar_tensor_tensor(
                    out=modsb[0:2, slot * D:(slot + 1) * D], in0=modsb[0:2, slot * D:(slot + 1) * D], scalar=1.0,
                    in1=nwb[0:2, off:off + D], op0=ALU.add, op1=ALU.mult), reads=[b_modsb, b_nwb], writes=[b_modsb])
            P.dma("sp", mod_d[l], modsb[0:2, :], reads=[b_modsb], writes=[B_mod], sbuf=b_modsb)
        P.barrier()
        P.new_phase()
        P.top = m0

        def rstd_of(ss, n, ncols, tmp, out, b_ss, b_tmp, b_out):
            P.op("act", lambda e: e.activation(out=tmp, in_=ss, func=AF.Ln, scale=1.0 / n, bias=EPS), reads=[b_ss], writes=[b_tmp])
            P.op("act", lambda e: e.activation(out=out, in_=tmp, func=AF.Exp, scale=-0.5), reads=[b_tmp], writes=[b_out])

        ps_rot = [0]

        def transposes_to(src, b_src, nblk, dst_fn, b_dst, evac_engs=("act", "dve")):
            i = 0
            k = 0
            while i < nblk:
                n = min(8, nblk - i)
                bank = ps_rot[0] % 2
                ps_rot[0] += 1
                pst = P.ps(bank, BF16)
                for j in range(n):
                    P.op("pe", lambda e, j=j, i=i, pst=pst: e.transpose(out=pst[:, j * 128:(j + 1) * 128], in_=src[:, (i + j) * 128:(i + j + 1) * 128],
                                                                       identity=ident),
                         reads=[b_src, b_ident], writes=[P.pbuf[bank]], signal=(j == n - 1))
                eng = evac_engs[k % len(evac_engs)]
                k += 1
                dst = dst_fn(i, n)
                srcv = pst[:, 0:n * 128].rearrange("p (a b) -> p a b", a=n)
                if eng == "act":
                    P.op("act", lambda e, dst=dst, srcv=srcv: e.copy(out=dst, in_=srcv), reads=[P.pbuf[bank]], writes=[b_dst])
                else:
                    P.op("dve", lambda e, dst=dst, srcv=srcv: e.tensor_copy(out=dst, in_=srcv), reads=[P.pbuf[bank]], writes=[b_dst])
                i += n

        mm_rot = [0]

        def mm_bank():
            b = 2 + (mm_rot[0] % 6)
            mm_rot[0] += 1
            return b

        for l in range(DEPTH):
            last = (l == DEPTH - 1)
            if l + 1 < DEPTH:
                convert_layer(l + 1)
            x_src = x_in if l == 0 else xs_d
            B_xsrc = (lambda t: d_const) if l == 0 else (lambda t: B_xs[t])

            mA = P.top
            hTs = [P.alloc(f"hT{i}", [16, 512], BF16) for i in range(2)]
            xb = [P.alloc(f"xb{i}", [D]) for i in range(2)]
            h32, b_h32 = P.alloc("h32", [D])
            hbf = [P.alloc(f"hbf{i}", [D], BF16) for i in range(4)]
            junk, b_junk = P.alloc("junk", [D], BF16)
            modA = [(P.alloc(f"a1bc{i}", [D]), P.alloc(f"sh1bc{i}", [D])) for i in range(2)]
            small, b_small = P.alloc("smallA", [8])
            ropet, b_rope = P.alloc("ropet", [4, 128])
            dtbb, b_dtbb = P.alloc("dtbb", [32])
            stg = [P.alloc(f"stg{i}", [512], BF16) for i in range(6)]
            rt = [P.alloc(f"rt{i}", [4, 64]) for i in range(4)]
            dtst = [P.alloc(f"dtst{i}", [32]) for i in range(2)]
            P.dma("sp", dtbb, dt_bias[l, :].partition_broadcast(128), reads=[d_const], writes=[b_dtbb], sbuf=b_dtbb)
            stg_n = 0

            def prepA_nonpe(gi):
                tiles_, ci_ = groups[gi]
                (a1bc, b_a1), (sh1bc, b_sh1) = modA[gi % 2]
                P.dma("sp", a1bc, modrow(l, ci_, 1).partition_broadcast(128), reads=[B_mod], writes=[b_a1], sbuf=b_a1)
                P.dma("sp", sh1bc, modrow(l, ci_, 0).partition_broadcast(128), reads=[B_mod], writes=[b_sh1], sbuf=b_sh1)
                for j, tt in enumerate(tiles_):
                    pump_convert(1)
                    xt, bx = xb[j % 2]
                    hb, bh = hbf[j]
                    P.dma("sp", xt, x_src[tt * T:(tt + 1) * T, :], reads=[B_xsrc(tt)], writes=[bx], sbuf=bx)
                    P.op("dve", lambda e: e.memset(small[:, 0:1], 0.0), writes=[b_small])
                    P.op("act", lambda e, xt=xt: e.activation(out=junk, in_=xt, func=AF.Square, accum_out=small[:, 0:1]),
                         reads=[bx], writes=[b_junk, b_small])
                    rstd_of(small[:, 0:1], D, 1, small[:, 1:2], small[:, 2:3], b_small, b_small, b_small)
                    P.op("dve", lambda e, xt=xt: e.scalar_tensor_tensor(out=h32, in0=xt, scalar=small[:, 2:3], in1=a1bc,
                                                                        op0=ALU.mult, op1=ALU.mult),
                         reads=[bx, b_small, b_a1], writes=[b_h32])
                    P.op("pool", lambda e, hb=hb: e.tensor_tensor(out=hb, in0=h32, in1=sh1bc, op=ALU.add),
                         reads=[b_h32, b_sh1], writes=[bh])

            def prepA_pe(gi):
                tiles_, ci_ = groups[gi]
                hTn, b_hTn = hTs[gi % 2]
                for j, tt in enumerate(tiles_):
                    hb, bh = hbf[j]
                    transposes_to(hb, bh, 16, lambda i, n, j=j: hTn[:, i:i + n, j * T:(j + 1) * T], b_hTn)

            prepA_nonpe(0)
            prepA_pe(0)
            for gi, (tiles, ci) in enumerate(groups):
                G = len(tiles) * T
                hT, b_hT = hTs[gi % 2]
                if ci == 1:
                    r0 = (tiles[0] - NP * CP) * T
                    P.dma("sp", ropet[:, 0:len(tiles), :], rope_cs[r0:r0 + G, :].rearrange("(j p) c -> p j c", p=128),
                          reads=[d_const], writes=[b_rope], sbuf=b_rope)
                for cb in range(14):
                    if cb == 1 and gi + 1 < len(groups):
                        prepA_nonpe(gi + 1)
                    if cb == 9 and gi + 1 < len(groups):
                        prepA_pe(gi + 1)
                    wt, bw = next_w()
                    nco = 512 if cb < 13 else 32
                    for j, tt in enumerate(tiles):
                        bank = mm_bank()
                        psv = P.ps(bank)[:, 0:nco]
                        for kc in range(16):
                            P.op("pe", lambda e, kc=kc, j=j, wt=wt, psv=psv, nco=nco: e.matmul(
                                psv, lhsT=hT[:, kc, j * T:(j + 1) * T], rhs=wt[:, kc, 0:nco], start=(kc == 0), stop=(kc == 15)),
                                reads=[b_hT, bw], writes=[P.pbuf[bank]], signal=(kc == 15))
                        rows = slice(tt * T, (tt + 1) * T)
                        if cb < 13:
                            sgt, bs = stg[stg_n % 6]
                            stg_n += 1
                        if cb < 4:
                            if ci == 1:
                                ps3 = psv.rearrange("p (h n) -> p h n", h=4)
                                x1 = ps3[:, :, 0:64]
                                x2 = ps3[:, :, 64:128]
                                cs = bc_mid(ropet[:, j, 0:64], 4)
                                sn = bc_mid(ropet[:, j, 64:128], 4)
                                o3 = sgt.rearrange("p (h n) -> p h n", h=4)
                                (t1, bt1), (t2, bt2), (t3, bt3), (t4, bt4) = rt
                                P.op("dve", lambda e, x1=x1, cs=cs, t1=t1: e.tensor_tensor(out=t1, in0=x1, in1=cs, op=ALU.mult),
                                     reads=[P.pbuf[bank], b_rope], writes=[bt1])
                                P.op("dve", lambda e, x2=x2, sn=sn, t2=t2: e.tensor_tensor(out=t2, in0=x2, in1=sn, op=ALU.mult),
                                     reads=[P.pbuf[bank], b_rope], writes=[bt2])
                                P.op("dve", lambda e, x1=x1, sn=sn, t3=t3: e.tensor_tensor(out=t3, in0=x1, in1=sn, op=ALU.mult),
                                     reads=[P.pbuf[bank], b_rope], writes=[bt3])
                                P.op("dve", lambda e, x2=x2, cs=cs, t4=t4: e.tensor_tensor(out=t4, in0=x2, in1=cs, op=ALU.mult),
                                     reads=[P.pbuf[bank], b_rope], writes=[bt4])
                                P.op("pool", lambda e, o3=o3, t1=t1, t2=t2: e.tensor_tensor(out=o3[:, :, 0:64], in0=t1, in1=t2, op=ALU.subtract),
                                     reads=[bt1, bt2], writes=[bs])
                                P.op("pool", lambda e, o3=o3, t3=t3, t4=t4: e.tensor_tensor(out=o3[:, :, 64:128], in0=t3, in1=t4, op=ALU.add),
                                     reads=[bt3, bt4], writes=[bs])
                            else:
                                P.op("act", lambda e, sgt=sgt, psv=psv: e.copy(out=sgt, in_=psv), reads=[P.pbuf[bank]], writes=[bs])
                            P.dma("pool", qk_d[rows, cb * 512:(cb + 1) * 512], sgt, reads=[bs], writes=[B_qk[tt]], sbuf=bs)
                        elif cb < 6:
                            P.op("act", lambda e, sgt=sgt, psv=psv: e.copy(out=sgt, in_=psv), reads=[P.pbuf[bank]], writes=[bs])
                            P.dma("act", v_d[rows, (cb - 4) * 512:(cb - 3) * 512], sgt, reads=[bs], writes=[B_v[tt]], sbuf=bs)
                        elif cb < 10:
                            P.op("act", lambda e, sgt=sgt, psv=psv: e.activation(out=sgt, in_=psv, func=AF.Silu),
                                 reads=[P.pbuf[bank]], writes=[bs])
                            if cb < 8:
                                P.dma("act", sg_d[rows, (cb - 6) * 512:(cb - 5) * 512], sgt, reads=[bs], writes=[B_sg[tt]], sbuf=bs)
                            else:
                                P.dma("act", sz_d[rows, (cb - 8) * 512:(cb - 7) * 512], sgt, reads=[bs], writes=[B_sz[tt]], sbuf=bs)
                        elif cb < 13:
                            P.op("act", lambda e, sgt=sgt, psv=psv: e.copy(out=sgt, in_=psv), reads=[P.pbuf[bank]], writes=[bs])
                            si = [k for k, s in enumerate(seqs) if s[0] <= tt < s[0] + s[1]][0]
                            r = xrow(si, (tt - seqs[si][0]) * T)
                            P.dma("act", xpre_d[r:r + T, (cb - 10) * 512:(cb - 9) * 512], sgt, reads=[bs], writes=[B_xpre], sbuf=bs)
                        else:
                            dtt, bdt = dtst[j % 2]
                            P.op("dve", lambda e, dtt=dtt, psv=psv: e.tensor_tensor(out=dtt, in0=psv, in1=dtbb, op=ALU.add),
                                 reads=[P.pbuf[bank], b_dtbb], writes=[bdt])
                            P.op("dve", lambda e, dtt=dtt: e.tensor_scalar_min(out=dtt, in0=dtt, scalar1=60.0), reads=[bdt], writes=[bdt])
                            P.op("act", lambda e, dtt=dtt: e.activation(out=dtt, in_=dtt, func=AF.Exp), reads=[bdt], writes=[bdt])
                            P.op("act", lambda e, dtt=dtt: e.activation(out=dtt, in_=dtt, func=AF.Ln, bias=1.0, scale=1.0), reads=[bdt], writes=[bdt])
                            P.dma("pool", dt_d[rows, :], dtt, reads=[bdt], writes=[B_dt[tt]], sbuf=bdt)
            P.barrier()
            P.new_phase()
            P.top = mA

            mB = P.top
            ldbc, b_ld = P.alloc("ldbc", [16])
            Mret, b_Mret = P.alloc("Mret", [8, 128])
            rtab, b_rtab = P.alloc("rtab", [5, 16])
            Abc, b_Abc = P.alloc("Abc", [32])
            dskb, b_dsk = P.alloc("dskb", [16])
            cwbc, b_cw = P.alloc("cwbc", [3, 1536])
            cbbc, b_cb = P.alloc("cbbc", [1536])
            snwb, b_snw = P.alloc("snwb", [1024])
            mt = [P.alloc(f"mt{i}", [128]) for i in range(4)]
            P.dma("sp", ldbc, rld[l, :].partition_broadcast(128), reads=[d_const], writes=[b_ld], sbuf=b_ld)
            P.dma("sp", Abc, a_log[l, :].partition_broadcast(128), reads=[d_const], writes=[b_Abc], sbuf=b_Abc)
            P.dma("sp", dskb, d_skip[l, :].partition_broadcast(128), reads=[d_const], writes=[b_dsk], sbuf=b_dsk)
            P.dma("sp", cwbc, conv_w[l, :].partition_broadcast(128).rearrange("p (k c) -> p k c", k=3), reads=[d_const], writes=[b_cw], sbuf=b_cw)
            P.dma("sp", cbbc, conv_b[l, :].partition_broadcast(128), reads=[d_const], writes=[b_cb], sbuf=b_cb)
            P.dma("sp", snwb, ssd_nw[l, :].partition_broadcast(128), reads=[d_const], writes=[b_snw], sbuf=b_snw)
            P.op("act", lambda e: e.activation(out=Abc, in_=Abc, func=AF.Exp), reads=[b_Abc], writes=[b_Abc])
            P.op("dve", lambda e: e.tensor_scalar_mul(out=Abc, in0=Abc, scalar1=-1.0), reads=[b_Abc], writes=[b_Abc])
            P.op("act", lambda e: e.activation(out=rtab[:, 0, 0:8], in_=ldbc[:, 0:8], func=AF.Exp, scale=pidx[:, 0:1]), reads=[b_ld, b_pidx], writes=[b_rtab])
            P.op("act", lambda e: e.activation(out=rtab[:, 0, 8:16], in_=ldbc[:, 8:16], func=AF.Exp, scale=pidx[:, 2:3]), reads=[b_ld, b_pidx], writes=[b_rtab])
            P.op("dve", lambda e: e.tensor_scalar_mul(out=rtab[:, 0, :], in0=rtab[:, 0, :], scalar1=DKS), reads=[b_rtab], writes=[b_rtab])
            P.op("act", lambda e: e.activation(out=rtab[:, 1, 0:8], in_=ldbc[:, 0:8], func=AF.Exp, scale=pidx[:, 1:2]), reads=[b_ld, b_pidx], writes=[b_rtab])
            P.op("act", lambda e: e.activation(out=rtab[:, 1, 8:16], in_=ldbc[:, 8:16], func=AF.Exp, scale=pidx[:, 3:4]), reads=[b_ld, b_pidx], writes=[b_rtab])
            P.op("act", lambda e: e.activation(out=rtab[:, 2, :], in_=ldbc, func=AF.Exp, scale=float(T)), reads=[b_ld], writes=[b_rtab])
            (dpos, b_dpos), (dneg, b_dneg), (mtmp, b_mtmp), (mtmp2, b_mtmp2) = mt
            P.op("dve", lambda e: e.tensor_scalar_max(out=dpos, in0=diffm, scalar1=0.0), reads=[b_diff], writes=[b_dpos])
            P.op("dve", lambda e: e.tensor_scalar(out=dneg, in0=diffm, scalar1=-1.0, scalar2=0.0, op0=ALU.mult, op1=ALU.max), reads=[b_diff], writes=[b_dneg])
            for h in range(8):
                P.op("act", lambda e, h=h: e.activation(out=mtmp, in_=dpos, func=AF.Exp, scale=ldbc[:, h:h + 1]), reads=[b_dpos, b_ld], writes=[b_mtmp])
                P.op("act", lambda e, h=h: e.activation(out=mtmp2, in_=dneg, func=AF.Exp, scale=ldbc[:, 8 + h:9 + h]), reads=[b_dneg, b_ld], writes=[b_mtmp2])
                P.op("dve", lambda e: e.tensor_tensor(out=mtmp, in0=mtmp, in1=M_le, op=ALU.mult), reads=[b_mtmp, b_cmask], writes=[b_mtmp])
                P.op("dve", lambda e: e.tensor_tensor(out=mtmp2, in0=mtmp2, in1=M_ge, op=ALU.mult), reads=[b_mtmp2, b_cmask], writes=[b_mtmp2])
                P.op("dve", lambda e, h=h: e.scalar_tensor_tensor(out=Mret[:, h, :], in0=mtmp, scalar=DKS, in1=mtmp2, op0=ALU.mult, op1=ALU.add),
                     reads=[b_mtmp, b_mtmp2], writes=[b_Mret])
                P.op("dve", lambda e, h=h: e.scalar_tensor_tensor(out=Mret[:, h, :], in0=mtmp2, scalar=DKS - 1.0, in1=Mret[:, h, :], op0=ALU.mult, op1=ALU.add),
                     reads=[b_mtmp2, b_Mret], writes=[b_Mret])

            Sf = [P.alloc(f"Sf{i}", [1024]) for i in range(2)]
            Sb = [P.alloc(f"Sb{i}", [1024]) for i in range(2)]
            Sbb = [P.alloc(f"Sbb{i}", [1024], BF16) for i in range(2)]
            NB = 1
            qkt = [P.alloc(f"qkt{i}", [2048], BF16) for i in range(NB)]
            vt = [P.alloc(f"vt{i}", [1024], BF16) for i in range(NB)]
            xsh = [[P.alloc(f"xsh{i}_{k}", [1536], BF16) for k in range(3)] for i in range(1)]
            xpo = [P.alloc(f"xpo{i}", [1536], BF16) for i in range(NB)]
            dtb = [P.alloc(f"dtb{i}", [32]) for i in range(2)]
            sgz = [P.alloc(f"sgz{i}", [2048], BF16) for i in range(NB)]
            stfb = [P.alloc(f"stfb{i}", [2048], BF16) for i in range(NB)]
            cstb = [P.alloc(f"cstb{i}", [2048]) for i in range(NB)]
            cdbb = [P.alloc(f"cdbb{i}", [16]) for i in range(NB)]
            cacc, b_cacc = P.alloc("cacc", [1536])
            ct1, b_ct1 = P.alloc("ct1", [1536])
            ct2, b_ct2 = P.alloc("ct2", [1536])
            yw, b_yw = cacc[:, 0:1024], b_cacc
            yt, b_yt = ct1[:, 0:1024], b_ct1
            ysq, b_ysq = ct2[:, 0:1024], b_ct2
            av, b_av = P.alloc("av", [32])
            dec, b_dec = P.alloc("dec", [64])
            wgt, b_wgt = P.alloc("wgt", [32])
            xw = [P.alloc(f"xw{i}", [1024], BF16) for i in range(2)]
            kw = [P.alloc(f"kw{i}", [1024], BF16) for i in range(2)]
            stst = [P.alloc(f"stst{i}", [2048], BF16) for i in range(1)]
            cst_st = [P.alloc(f"cst_st{i}", [2048]) for i in range(1)]
            cdst = [P.alloc(f"cdst{i}", [16]) for i in range(2)]
            qT, b_qT = P.alloc("qT", [8, 128], BF16)
            kT, b_kT = P.alloc("kT", [8, 128], BF16)
            bcT, b_bcT = P.alloc("bcT", [4, 128], BF16)
            AT, b_AT = P.alloc("AT", [8, 128], BF16)
            Xd = [P.alloc(f"Xd{i}", [16, 128], BF16) for i in range(2)]
            ahl, b_ahl = P.alloc("ahl", [2, 32], BF16)
            cmb, b_cmb = P.alloc("cmb", [2, 128], BF16)
            P.op("dve", lambda e: e.tensor_copy(out=cmb[:, 0, :], in_=M_le), reads=[b_cmask], writes=[b_cmb])
            P.op("dve", lambda e: e.tensor_copy(out=cmb[:, 1, :], in_=M_ge), reads=[b_cmask], writes=[b_cmb])
            Ed = [P.alloc(f"Ed{i}", [16, 128], BF16) for i in range(2)]
            Md = Ed
            Gm, b_Gm = P.alloc("Gm", [4, 128], BF16)
            ysm, b_ysm = P.alloc("ysm", [32])
            yst = [P.alloc(f"yst{i}", [2048], BF16) for i in range(1)]

            cnt1 = [0]
            cnt2 = [0]
            def run_pass1(seq):
                (t0, nck, is_s, sidx) = seq
                for ty in range(2):
                    sv, bsv = Sf[ty]
                    if is_s:
                        src = (st_ret if ty == 0 else st_ssd)[l, 0]
                        P.dma("sp", sv, src, reads=[d_const], writes=[bsv], sbuf=bsv)
                    else:
                        P.op("pool", lambda e, sv=sv: e.memset(sv, 0.0), writes=[bsv])
                for c in range(nck):
                    pump_convert(1)
                    tt = t0 + c
                    i = cnt1[0] % NB
                    cnt1[0] += 1
                    rows = slice(tt * T, (tt + 1) * T)
                    kt_, bkt = qkt[i]
                    vt_, bvt = vt[i]
                    dtt, bdtt = dtb[c % 2]
                    P.dma("sp", kt_[:, 1024:2048], qk_d[rows, 1024:2048], reads=[B_qk[tt]], writes=[bkt], sbuf=bkt)
                    P.dma("sp", vt_, v_d[rows, :], reads=[B_v[tt]], writes=[bvt], sbuf=bvt)
                    P.dma("sp", dtt, dt_d[rows, :], reads=[B_dt[tt]], writes=[bdtt], sbuf=bdtt)
                    r = xrow(sidx, c * T)
                    for k3 in range(3):
                        xv, bxv = xsh[0][k3]
                        P.dma("sp", xv, xpre_d[r - 1 + k3:r - 1 + k3 + T, :], reads=[B_xpre], writes=[bxv], sbuf=bxv)
                    P.op("dve", lambda e, xv=xsh[0][1][0]: e.tensor_tensor(out=cacc, in0=xv, in1=cwbc[:, 1, :], op=ALU.mult),
                         reads=[xsh[0][1][1], b_cw], writes=[b_cacc])
                    P.op("pool", lambda e, xv=xsh[0][0][0]: e.tensor_tensor(out=ct1, in0=xv, in1=cwbc[:, 0, :], op=ALU.mult),
                         reads=[xsh[0][0][1], b_cw], writes=[b_ct1])
                    P.op("pool", lambda e, xv=xsh[0][2][0]: e.tensor_tensor(out=ct2, in0=xv, in1=cwbc[:, 2, :], op=ALU.mult),
                         reads=[xsh[0][2][1], b_cw], writes=[b_ct2])
                    P.op("pool", lambda e: e.tensor_tensor(out=ct1, in0=ct1, in1=cbbc, op=ALU.add), reads=[b_ct1, b_cb], writes=[b_ct1])
                    P.op("dve", lambda e: e.tensor_tensor(out=cacc, in0=cacc, in1=ct2, op=ALU.add), reads=[b_cacc, b_ct2], writes=[b_cacc])
                    P.op("dve", lambda e: e.tensor_tensor(out=cacc, in0=cacc, in1=ct1, op=ALU.add), reads=[b_cacc, b_ct1], writes=[b_cacc])
                    xp_, bxp = xpo[i]
                    P.op("act", lambda e, xp_=xp_: e.activation(out=xp_, in_=cacc, func=AF.Silu), reads=[b_cacc], writes=[bxp])
                    P.dma("act", xpost_d[rows, :], xp_, reads=[bxp], writes=[B_xpost[tt]], sbuf=bxp)
                    xs3 = xp_[:, 0:1024].rearrange("p (h c) -> p h c", h=16)
                    Btok = xp_[:, 1024:1280].rearrange("p (g n) -> p g n", g=2)
                    P.op("dve", lambda e, dtt=dtt: e.tensor_tensor(out=av, in0=dtt, in1=Abc, op=ALU.mult), reads=[bdtt, b_Abc], writes=[b_av])
                    bk = mm_bank()
                    pc = P.ps(bk)
                    P.op("pe", lambda e, pc=pc: e.matmul(pc[:, 0:16], lhsT=M_gt, rhs=av[:, 0:16], start=True, stop=True), reads=[b_cmask, b_av], writes=[P.pbuf[bk]], signal=False)
                    P.op("pe", lambda e, pc=pc: e.matmul(pc[:, 16:32], lhsT=M_lt, rhs=av[:, 16:32], start=True, stop=True), reads=[b_cmask, b_av], writes=[P.pbuf[bk]], signal=False)
                    P.op("pe", lambda e, pc=pc: e.matmul(pc[:, 32:64], lhsT=M_one, rhs=av[:, 0:32], start=True, stop=True), reads=[b_cmask, b_av], writes=[P.pbuf[bk]])
                    P.op("act", lambda e, pc=pc: e.activation(out=dec, in_=pc[:, 0:64], func=AF.Exp), reads=[P.pbuf[bk]], writes=[b_dec])
                    P.op("dve", lambda e, dtt=dtt: e.tensor_tensor(out=wgt, in0=dtt, in1=dec[:, 0:32], op=ALU.mult), reads=[bdtt, b_dec], writes=[b_wgt])
                    k3v = kt_[:, 1024:2048].rearrange("p (h n) -> p h n", h=8)
                    v3v = vt_.rearrange("p (h c) -> p h c", h=8)
                    for d in range(2):
                        xw_, bxw = xw[d]
                        kw_, bkw = kw[d]
                        P.op("dve", lambda e, d=d, xw_=xw_: e.tensor_tensor(out=xw_.rearrange("p (h c) -> p h c", h=16), in0=xs3,
                                                                          in1=bc_last(wgt[:, d * 16:(d + 1) * 16], 64), op=ALU.mult),
                             reads=[bxp, b_wgt], writes=[bxw])
                        P.op("pool", lambda e, d=d, kw_=kw_: e.tensor_tensor(out=kw_.rearrange("p (h n) -> p h n", h=8), in0=k3v,
                                                                           in1=bc_last(rtab[:, 1, d * 8:(d + 1) * 8], 128), op=ALU.mult),
                             reads=[bkt, b_rtab], writes=[bkw])
                    def chunk_state_mms(d):
                        res = {}
                        for ty in range(2):
                            b0, b1 = mm_bank(), mm_bank()
                            res[ty] = (b0, b1)
                            if ty == 0:
                                kw3 = kw[d][0].rearrange("p (h n) -> p h n", h=8)
                                for h in range(8):
                                    bnk = (b0, b1)[h // 4]
                                    P.op("pe", lambda e, h=h, bnk=bnk, kw3=kw3: e.matmul(P.ps(bnk)[:, (h % 4) * 128:(h % 4 + 1) * 128], lhsT=kw3[:, h, :],
                                                                                       rhs=v3v[:, h, :], start=True, stop=True),
                                         reads=[kw[d][1], bvt], writes=[P.pbuf[bnk]], signal=(h % 4 == 3))
                            else:
                                for g in range(2):
                                    bnk = (b0, b1)[g]
                                    P.op("pe", lambda e, g=g, bnk=bnk, d=d: e.matmul(P.ps(bnk), lhsT=Btok[:, g, :], rhs=xw[d][0][:, g * 512:(g + 1) * 512],
                                                                                    start=True, stop=True),
                                         reads=[bxp, xw[d][1]], writes=[P.pbuf[bnk]])
                        return res
                    sst, bsst = stst[0]
                    for ty in range(2):
                        sv, bsv = Sf[ty]
                        P.op("act", lambda e, ty=ty, sv=sv, sst=sst: e.copy(out=sst[:, ty * 1024:(ty + 1) * 1024], in_=sv), reads=[bsv], writes=[bsst])
                    P.dma("act", stf_d[tt], sst, reads=[bsst], writes=[B_stf[tt]], sbuf=bsst)
                    psf = chunk_state_mms(0)
                    for ty in range(2):
                        sv, bsv = Sf[ty]
                        b0, b1 = psf[ty]
                        if ty == 0:
                            P.op("dve", lambda e, sv=sv: e.tensor_tensor(out=sv.rearrange("p (h c) -> p h c", h=8), in0=sv.rearrange("p (h c) -> p h c", h=8),
                                                                        in1=bc_last(rtab[:, 2, 0:8], 128), op=ALU.mult), reads=[bsv, b_rtab], writes=[bsv])
                        else:
                            P.op("dve", lambda e, sv=sv: e.tensor_tensor(out=sv.rearrange("p (h c) -> p h c", h=16), in0=sv.rearrange("p (h c) -> p h c", h=16),
                                                                        in1=bc_last(dec[:, 32:48], 64), op=ALU.mult), reads=[bsv, b_dec], writes=[bsv])
                        for hb_, bnk in enumerate((b0, b1)):
                            P.op("dve", lambda e, sv=sv, hb_=hb_, bnk=bnk: e.tensor_tensor(out=sv[:, hb_ * 512:(hb_ + 1) * 512], in0=sv[:, hb_ * 512:(hb_ + 1) * 512],
                                                                                       in1=P.ps(bnk), op=ALU.add), reads=[bsv, P.pbuf[bnk]], writes=[bsv])
                    psbk = chunk_state_mms(1)
                    cs_, bcs = cst_st[0]
                    for ty in range(2):
                        b0, b1 = psbk[ty]
                        for hb_, bnk in enumerate((b0, b1)):
                            P.op("act", lambda e, ty=ty, hb_=hb_, bnk=bnk, cs_=cs_: e.copy(out=cs_[:, ty * 1024 + hb_ * 512: ty * 1024 + (hb_ + 1) * 512], in_=P.ps(bnk)),
                                 reads=[P.pbuf[bnk]], writes=[bcs])
                    P.dma("act", cstb_d[tt], cs_, reads=[bcs], writes=[B_cstb[tt]], sbuf=bcs)
                    cd_, bcd = cdst[c % 2]
                    P.op("act", lambda e, cd_=cd_: e.copy(out=cd_, in_=dec[:, 48:64]), reads=[b_dec], writes=[bcd])
                    P.dma("act", cdb_d[tt], cd_, reads=[bcd], writes=[B_cdb[tt]], sbuf=bcd)
                if not is_s:
                    for ty in range(2):
                        sv, bsv = Sf[ty]
                        dst = (ns_ret if ty == 0 else ns_ssd)[sidx, l, 0]
                        P.dma("pool", dst, sv, reads=[bsv], writes=[B_out], sbuf=bsv)

                if is_s and EXCH:
                    for ty in range(2):
                        sv, bsv = Sf[ty]
                        P.dma("sp", exst_in[:, ty * 1024:(ty + 1) * 1024], sv, reads=[bsv], writes=[B_exin], sbuf=bsv)
                    P.collective("AllGather", PAIRS, exst_in, exst_out, reads=[B_exin], writes=[B_exout], inc=1)

            def run_pass2(seq):
                (t0, nck, is_s, sidx) = seq
                for ty in range(2):
                    sv, bsv = Sb[ty]
                    if is_s and EXCH:
                        ev, bev = cstb[0]
                        od, bod = cst_st[0]
                        P.dma("sp", ev[:, 0:1024], exst_out[0:128, ty * 1024:(ty + 1) * 1024], reads=[B_exout], writes=[bev], sbuf=bev)
                        P.dma("sp", od[:, 0:1024], exst_out[128:256, ty * 1024:(ty + 1) * 1024], reads=[B_exout], writes=[bod], sbuf=bod)
                        P.op("dve", lambda e, sv=sv, ev=ev: e.tensor_scalar(out=sv, in0=ev[:, 0:1024], scalar1=psel[:, 0:1], scalar2=None, op0=ALU.mult),
                             reads=[bev, b_psel], writes=[bsv])
                        P.op("dve", lambda e, sv=sv, od=od: e.scalar_tensor_tensor(out=sv, in0=od[:, 0:1024], scalar=psel[:, 1:2], in1=sv, op0=ALU.mult, op1=ALU.add),
                             reads=[bod, b_psel, bsv], writes=[bsv])
                    elif is_s:
                        src = (st_ret if ty == 0 else st_ssd)[l, 1]
                        P.dma("sp", sv, src, reads=[d_const], writes=[bsv], sbuf=bsv)
                    else:
                        P.op("pool", lambda e, sv=sv: e.memset(sv, 0.0), writes=[bsv])
                for ty in range(2):
                    P.op("act", lambda e, ty=ty: e.copy(out=Sbb[ty][0], in_=Sb[ty][0]), reads=[Sb[ty][1]], writes=[Sbb[ty][1]])
                for c in range(nck - 1, -1, -1):
                    pump_convert(2)
                    tt = t0 + c
                    i = cnt2[0] % NB
                    cnt2[0] += 1
                    rows = slice(tt * T, (tt + 1) * T)
                    qk_, bqk = qkt[i]
                    vt_, bvt = vt[i]
                    dtt, bdtt = dtb[c % 2]
                    xp_, bxp = xpo[i]
                    sgz_, bsgz = sgz[i]
                    stf_, bstf = stfb[i]
                    csb_, bcsb = cstb[i]
                    cdb_, bcdb = cdbb[i]
                    P.dma("sp", qk_, qk_d[rows, :], reads=[B_qk[tt]], writes=[bqk], sbuf=bqk)
                    P.dma("sp", vt_, v_d[rows, :], reads=[B_v[tt]], writes=[bvt], sbuf=bvt)
                    P.dma("sp", dtt, dt_d[rows, :], reads=[B_dt[tt]], writes=[bdtt], sbuf=bdtt)
                    P.dma("sp", xp_, xpost_d[rows, :], reads=[B_xpost[tt]], writes=[bxp], sbuf=bxp)
                    P.dma("sp", sgz_[:, 0:1024], sg_d[rows, :], reads=[B_sg[tt]], writes=[bsgz], sbuf=bsgz)
                    P.dma("sp", sgz_[:, 1024:2048], sz_d[rows, :], reads=[B_sz[tt]], writes=[bsgz], sbuf=bsgz)
                    P.dma("sp", stf_, stf_d[tt], reads=[B_stf[tt]], writes=[bstf], sbuf=bstf)
                    P.dma("sp", csb_, cstb_d[tt], reads=[B_cstb[tt]], writes=[bcsb], sbuf=bcsb)
                    P.dma("sp", cdb_, cdb_d[tt], reads=[B_cdb[tt]], writes=[bcdb], sbuf=bcdb)
                    v3v = vt_.rearrange("p (h c) -> p h c", h=8)
                    xs3 = xp_[:, 0:1024].rearrange("p (h c) -> p h c", h=16)
                    transposes_to(qk_[:, 0:1024], bqk, 8, lambda i0, n: qT[:, i0:i0 + n, :], b_qT)
                    transposes_to(qk_[:, 1024:2048], bqk, 8, lambda i0, n: kT[:, i0:i0 + n, :], b_kT)
                    transposes_to(xp_[:, 1024:1536], bxp, 4, lambda i0, n: bcT[:, i0:i0 + n, :], b_bcT)
                    pb = (mm_bank(), mm_bank())
                    for h in range(8):
                        bnk = pb[h // 4]
                        P.op("pe", lambda e, h=h, bnk=bnk: e.matmul(P.ps(bnk)[:, (h % 4) * 128:(h % 4 + 1) * 128], lhsT=kT[:, h, :], rhs=qT[:, h, :],
                                                                  start=True, stop=True), reads=[b_kT, b_qT], writes=[P.pbuf[bnk]], signal=(h % 4 == 3))
                    for hb_ in range(2):
                        P.op("dve", lambda e, hb_=hb_: e.tensor_tensor(out=AT[:, hb_ * 4:(hb_ + 1) * 4, :], in0=P.ps(pb[hb_]).rearrange("p (h n) -> p h n", h=4),
                                                                     in1=Mret[:, hb_ * 4:(hb_ + 1) * 4, :], op=ALU.mult),
                             reads=[P.pbuf[pb[hb_]], b_Mret], writes=[b_AT])
                    pin = (mm_bank(), mm_bank())
                    pof = (mm_bank(), mm_bank())
                    for h in range(8):
                        bnk = pin[h // 4]
                        P.op("pe", lambda e, h=h, bnk=bnk: e.matmul(P.ps(bnk)[:, (h % 4) * 128:(h % 4 + 1) * 128], lhsT=AT[:, h, :], rhs=v3v[:, h, :],
                                                                  start=True, stop=True), reads=[b_AT, bvt], writes=[P.pbuf[bnk]], signal=(h % 4 == 3))
                    for h in range(8):
                        bnk = pof[h // 4]
                        P.op("pe", lambda e, h=h, bnk=bnk, stf_=stf_: e.matmul(P.ps(bnk)[:, (h % 4) * 128:(h % 4 + 1) * 128], lhsT=qT[:, h, :],
                                                                             rhs=stf_[:, h * 128:(h + 1) * 128], start=True, stop=True),
                             reads=[b_qT, bstf], writes=[P.pbuf[bnk]], signal=(h % 4 == 3))
                    yw3 = yw.rearrange("p (h c) -> p h c", h=8)
                    yt3 = yt.rearrange("p (h c) -> p h c", h=8)
                    for hb_ in range(2):
                        sl = slice(hb_ * 4, (hb_ + 1) * 4)
                        P.op("dve", lambda e, hb_=hb_, sl=sl: e.tensor_tensor(out=yt3[:, sl, :], in0=P.ps(pof[hb_]).rearrange("p (h n) -> p h n", h=4),
                                                                            in1=bc_last(rtab[:, 0, sl], 128), op=ALU.mult),
                             reads=[P.pbuf[pof[hb_]], b_rtab], writes=[b_yt])
                        P.op("dve", lambda e, hb_=hb_, sl=sl: e.tensor_tensor(out=yw3[:, sl, :], in0=P.ps(pin[hb_]).rearrange("p (h n) -> p h n", h=4),
                                                                            in1=yt3[:, sl, :], op=ALU.add),
                             reads=[P.pbuf[pin[hb_]], b_yt], writes=[b_yw])
                    pob = (mm_bank(), mm_bank())
                    for h in range(8):
                        bnk = pob[h // 4]
                        P.op("pe", lambda e, h=h, bnk=bnk: e.matmul(P.ps(bnk)[:, (h % 4) * 128:(h % 4 + 1) * 128], lhsT=qT[:, h, :],
                                                                  rhs=Sbb[0][0][:, h * 128:(h + 1) * 128], start=True, stop=True),
                             reads=[b_qT, Sbb[0][1]], writes=[P.pbuf[bnk]], signal=(h % 4 == 3))
                    for hb_ in range(2):
                        sl = slice(hb_ * 4, (hb_ + 1) * 4)
                        P.op("dve", lambda e, hb_=hb_, sl=sl: e.tensor_tensor(out=yt3[:, sl, :], in0=P.ps(pob[hb_]).rearrange("p (h n) -> p h n", h=4),
                                                                            in1=bc_last(rtab[:, 0, 8 + hb_ * 4:8 + (hb_ + 1) * 4], 128), op=ALU.mult),
                             reads=[P.pbuf[pob[hb_]], b_rtab], writes=[b_yt])
                    P.op("pool", lambda e: e.tensor_tensor(out=yw, in0=yw, in1=yt, op=ALU.add), reads=[b_yw, b_yt], writes=[b_yw])
                    P.op("act", lambda e: e.activation(out=ysq, in_=yw, func=AF.Square), reads=[b_yw], writes=[b_ysq])
                    P.op("dve", lambda e: e.tensor_reduce(out=ysm[:, 0:8], in_=ysq.rearrange("p (h c) -> p h c", h=8), axis=AX.X, op=ALU.add),
                         reads=[b_ysq], writes=[b_ysm])
                    rstd_of(ysm[:, 0:8], 128, 8, ysm[:, 8:16], ysm[:, 16:24], b_ysm, b_ysm, b_ysm)
                    yo_, byo = yst[0]
                    P.op("dve", lambda e: e.tensor_tensor(out=yw3, in0=yw3, in1=bc_last(ysm[:, 16:24], 128), op=ALU.mult), reads=[b_yw, b_ysm], writes=[b_yw])
                    P.op("pool", lambda e, yo_=yo_, sgz_=sgz_: e.tensor_tensor(out=yo_[:, 0:1024], in0=yw, in1=sgz_[:, 0:1024], op=ALU.mult),
                         reads=[b_yw, bsgz], writes=[byo])
                    P.op("dve", lambda e, dtt=dtt: e.tensor_tensor(out=av, in0=dtt, in1=Abc, op=ALU.mult), reads=[bdtt, b_Abc], writes=[b_av])
                    bk = mm_bank()
                    pc = P.ps(bk)
                    P.op("pe", lambda e, pc=pc: e.matmul(pc[:, 0:16], lhsT=M_le, rhs=av[:, 0:16], start=True, stop=True), reads=[b_cmask, b_av], writes=[P.pbuf[bk]], signal=False)
                    P.op("pe", lambda e, pc=pc: e.matmul(pc[:, 16:32], lhsT=M_ge, rhs=av[:, 16:32], start=True, stop=True), reads=[b_cmask, b_av], writes=[P.pbuf[bk]])
                    P.op("act", lambda e, pc=pc: e.activation(out=dec[:, 0:32], in_=pc[:, 0:32], func=AF.Exp), reads=[P.pbuf[bk]], writes=[b_dec])
                    for d in range(2):
                        xw_, bxw = xw[d]
                        P.op("dve", lambda e, d=d, xw_=xw_, dtt=dtt: e.tensor_tensor(out=xw_.rearrange("p (h c) -> p h c", h=16), in0=xs3,
                                                                                   in1=bc_last(dtt[:, d * 16:(d + 1) * 16], 64), op=ALU.mult),
                             reads=[bxp, bdtt], writes=[bxw])
                    bg = mm_bank()
                    for g in range(2):
                        P.op("pe", lambda e, g=g: e.matmul(P.ps(bg)[:, g * 128:(g + 1) * 128], lhsT=bcT[:, g, :], rhs=bcT[:, 2 + g, :], start=True, stop=True),
                             reads=[b_bcT], writes=[P.pbuf[bg]], signal=(g == 1))
                    P.op("dve", lambda e: e.tensor_tensor(out=Gm[:, 0:2, :], in0=P.ps(bg)[:, 0:256].rearrange("p (g n) -> p g n", g=2),
                                                          in1=bc_mid(M_le, 2), op=ALU.mult), reads=[P.pbuf[bg], b_cmask], writes=[b_Gm])
                    P.op("dve", lambda e: e.tensor_tensor(out=Gm[:, 2:4, :], in0=P.ps(bg)[:, 0:256].rearrange("p (g n) -> p g n", g=2),
                                                          in1=bc_mid(M_ge, 2), op=ALU.mult), reads=[P.pbuf[bg], b_cmask], writes=[b_Gm])
                    P.op("dve", lambda e: e.tensor_copy(out=ahl[:, 0, :], in_=av), reads=[b_av], writes=[b_ahl])
                    P.op("dve", lambda e: e.tensor_tensor(out=ahl[:, 1, :], in0=av, in1=ahl[:, 0, :], op=ALU.subtract), reads=[b_av, b_ahl], writes=[b_ahl])
                    for d in range(2):
                        Ed_, bEd = Ed[d]
                        Md_, bMd = Md[d]
                        um = cmb[:, d, :]
                        msk = M_gt if d == 0 else M_lt
                        for hl in range(2):
                            Xq, bXq = Xd[hl]
                            P.op("pool" if hl == 0 else "dve", lambda e, d=d, hl=hl, Xq=Xq, msk=msk: e.tensor_tensor(
                                out=Xq, in0=bc_last(ahl[:, hl, d * 16:(d + 1) * 16], 128), in1=bc_mid(msk, 16), op=ALU.mult),
                                reads=[b_ahl, b_cmask], writes=[bXq])
                        for q4 in range(4):
                            bnk = mm_bank()
                            for jj in range(4):
                                j = q4 * 4 + jj
                                for hl in range(2):
                                    P.op("pe", lambda e, j=j, jj=jj, bnk=bnk, hl=hl, um=um: e.matmul(P.ps(bnk)[:, jj * 128:(jj + 1) * 128], lhsT=Xd[hl][0][:, j, :], rhs=um,
                                                                                                   start=(hl == 0), stop=(hl == 1)),
                                         reads=[Xd[hl][1], b_cmb], writes=[P.pbuf[bnk]], signal=(jj == 3 and hl == 1))
                            P.op("act", lambda e, q4=q4, bnk=bnk, Ed_=Ed_: e.activation(out=Ed_[:, q4 * 4:(q4 + 1) * 4, :],
                                                                                       in_=P.ps(bnk).rearrange("p (h n) -> p h n", h=4), func=AF.Exp),
                                 reads=[P.pbuf[bnk]], writes=[bEd])
                        for g in range(2):
                            P.op("dve", lambda e, g=g, d=d, Ed_=Ed_, Md_=Md_: e.tensor_tensor(out=Md_[:, g * 8:(g + 1) * 8, :], in0=Ed_[:, g * 8:(g + 1) * 8, :],
                                                                                           in1=bc_mid(Gm[:, d * 2 + g, :], 8), op=ALU.mult),
                                 reads=[bEd, b_Gm], writes=[bMd])
                    pin = (mm_bank(), mm_bank())
                    for j in range(16):
                        bnk = pin[j // 8]
                        o = P.ps(bnk)[:, (j % 8) * 64:(j % 8 + 1) * 64]
                        P.op("pe", lambda e, j=j, o=o: e.matmul(o, lhsT=Md[0][0][:, j, :], rhs=xw[0][0][:, j * 64:(j + 1) * 64], start=True, stop=False),
                             reads=[Md[0][1], xw[0][1]], writes=[P.pbuf[bnk]], signal=False)
                        P.op("pe", lambda e, j=j, o=o: e.matmul(o, lhsT=Md[1][0][:, j, :], rhs=xw[1][0][:, j * 64:(j + 1) * 64], start=False, stop=True),
                             reads=[Md[1][1], xw[1][1]], writes=[P.pbuf[bnk]], signal=(j % 8 == 7))
                    yw16 = yw.rearrange("p (h c) -> p h c", h=16)
                    yt16 = yt.rearrange("p (h c) -> p h c", h=16)
                    for d in range(2):
                        po = (mm_bank(), mm_bank())
                        srcS = stf_[:, 1024:2048] if d == 0 else Sbb[1][0]
                        bsrc = bstf if d == 0 else Sbb[1][1]
                        for g in range(2):
                            P.op("pe", lambda e, g=g, po=po, srcS=srcS: e.matmul(P.ps(po[g]), lhsT=bcT[:, 2 + g, :], rhs=srcS[:, g * 512:(g + 1) * 512], start=True, stop=True),
                                 reads=[b_bcT, bsrc], writes=[P.pbuf[po[g]]])
                        for g in range(2):
                            sl = slice(g * 8, (g + 1) * 8)
                            P.op("dve", lambda e, g=g, d=d, sl=sl, po=po: e.tensor_tensor(out=yt16[:, sl, :], in0=P.ps(po[g]).rearrange("p (h c) -> p h c", h=8),
                                                                                        in1=bc_last(dec[:, d * 16 + g * 8:d * 16 + (g + 1) * 8], 64), op=ALU.mult),
                                 reads=[P.pbuf[po[g]], b_dec], writes=[b_yt])
                            if d == 0:
                                P.op("dve", lambda e, g=g, sl=sl: e.tensor_tensor(out=yw16[:, sl, :], in0=P.ps(pin[g]).rearrange("p (h c) -> p h c", h=8),
                                                                                in1=yt16[:, sl, :], op=ALU.add),
                                     reads=[P.pbuf[pin[g]], b_yt], writes=[b_yw])
                        if d == 1:
                            P.op("pool", lambda e: e.tensor_tensor(out=yw, in0=yw, in1=yt, op=ALU.add), reads=[b_yw, b_yt], writes=[b_yw])
                    P.op("dve", lambda e: e.tensor_tensor(out=yt16, in0=xs3, in1=bc_last(dskb, 64), op=ALU.mult), reads=[bxp, b_dsk], writes=[b_yt])
                    P.op("pool", lambda e: e.tensor_tensor(out=yw, in0=yw, in1=yt, op=ALU.add), reads=[b_yw, b_yt], writes=[b_yw])
                    P.op("dve", lambda e, sgz_=sgz_: e.tensor_tensor(out=yw, in0=yw, in1=sgz_[:, 1024:2048], op=ALU.mult), reads=[b_yw, bsgz], writes=[b_yw])
                    P.op("dve", lambda e: e.memset(ysm[:, 24:25], 0.0), writes=[b_ysm])
                    P.op("act", lambda e: e.activation(out=ysq, in_=yw, func=AF.Square, accum_out=ysm[:, 24:25]), reads=[b_yw], writes=[b_ysq, b_ysm])
                    rstd_of(ysm[:, 24:25], 1024, 1, ysm[:, 25:26], ysm[:, 26:27], b_ysm, b_ysm, b_ysm)
                    P.op("dve", lambda e, yo_=yo_: e.scalar_tensor_tensor(out=yo_[:, 1024:2048], in0=yw, scalar=ysm[:, 26:27], in1=snwb, op0=ALU.mult, op1=ALU.mult),
                         reads=[b_yw, b_ysm, b_snw], writes=[byo])
                    P.dma("pool", y_d[rows, :], yo_, reads=[byo], writes=[B_y[tt]], sbuf=byo)
                    for ty in range(2):
                        sv, bsv = Sb[ty]
                        if ty == 0:
                            P.op("dve", lambda e, sv=sv: e.tensor_tensor(out=sv.rearrange("p (h c) -> p h c", h=8), in0=sv.rearrange("p (h c) -> p h c", h=8),
                                                                        in1=bc_last(rtab[:, 2, 8:16], 128), op=ALU.mult), reads=[bsv, b_rtab], writes=[bsv])
                        else:
                            P.op("dve", lambda e, sv=sv, cdb_=cdb_: e.tensor_tensor(out=sv.rearrange("p (h c) -> p h c", h=16), in0=sv.rearrange("p (h c) -> p h c", h=16),
                                                                                   in1=bc_last(cdb_, 64), op=ALU.mult), reads=[bsv, bcdb], writes=[bsv])
                        P.op("pool", lambda e, sv=sv, ty=ty, csb_=csb_: e.tensor_tensor(out=sv, in0=sv, in1=csb_[:, ty * 1024:(ty + 1) * 1024], op=ALU.add),
                             reads=[bsv, bcsb], writes=[bsv])
                        P.op("act", lambda e, ty=ty, sv=sv: e.copy(out=Sbb[ty][0], in_=sv), reads=[bsv], writes=[Sbb[ty][1]])
                if not is_s:
                    for ty in range(2):
                        sv, bsv = Sb[ty]
                        dst = (ns_ret if ty == 0 else ns_ssd)[sidx, l, 1]
                        P.dma("pool", dst, sv, reads=[bsv], writes=[B_out], sbuf=bsv)

            if EXCH:
                rl = xrow(NSEQ - 1, CS * T - 1)
                P.dma("sp", exrow_in, xpre_d[rl:rl + 1, :], reads=[B_xpre], writes=[B_exrin], sem_key=("x", "exrow"))
                P.collective("AllGather", PAIRS, exrow_in, exrow_out, reads=[B_exrin], writes=[B_exrout], inc=1)
                r0_, b0_ = xsh[0][0]
                r1_, b1_ = xsh[0][1]
                r2_, b2_ = xsh[0][2]
                P.dma("sp", r0_[0:1, :], exrow_out[0:1, :], reads=[B_exrout], writes=[b0_], sbuf=b0_)
                P.dma("sp", r1_[0:1, :], exrow_out[1:2, :], reads=[B_exrout], writes=[b1_], sbuf=b1_)
                P.op("dve", lambda e: e.tensor_scalar(out=r2_[0:1, :], in0=r0_[0:1, :], scalar1=psel[0:1, 0:1], scalar2=None, op0=ALU.mult),
                     reads=[b0_, b_psel], writes=[b2_])
                P.op("dve", lambda e: e.scalar_tensor_tensor(out=r2_[0:1, :], in0=r1_[0:1, :], scalar=psel[0:1, 1:2], in1=r2_[0:1, :], op0=ALU.mult, op1=ALU.add),
                     reads=[b1_, b_psel, b2_], writes=[b2_])
                P.dma("sp", xpre_d[rl + 1:rl + 2, :], r2_[0:1, :], reads=[b2_], writes=[B_xpre], sbuf=b2_)
                run_pass1(seqs[-1])
                for sq in seqs[:-1]:
                    run_pass1(sq)
                    run_pass2(sq)
                run_pass2(seqs[-1])
            else:
                for sq in seqs:
                    run_pass1(sq)
                    run_pass2(sq)
            P.barrier()
            P.new_phase()
            P.top = mB

            mC = P.top
            hT, b_hT = P.alloc("hTc", [16, 512], BF16)
            uT, b_uT = P.alloc("uT", [64, 512], BF16)
            xg, b_xg0 = P.alloc("xg", [4, D])
            b_xg = [Buf(f"xg{j}") for j in range(4)]
            modc = [P.alloc(f"modc{i}", [D], BF16) for i in range(4)]
            modl, b_modl = P.alloc("modl", [D])
            yb = [P.alloc(f"yb{i}", [D], BF16) for i in range(1)]
            h2b, b_h2b = P.alloc("h2b", [D], BF16)
            tmpc = [P.alloc(f"tmpc{i}", [512]) for i in range(2)]
            junk, b_junk = h2b, b_h2b
            small, b_small = P.alloc("smallC", [8])
            if last:
                fnb, b_fnb = P.alloc("fnb", [D], BF16)
                P.dma("sp", modl, fnw[0, :].partition_broadcast(128), reads=[d_const], writes=[b_modl], sbuf=b_modl)
                P.op("dve", lambda e: e.tensor_copy(out=fnb, in_=modl), reads=[b_modl], writes=[b_fnb])
            cur_ci = -1
            tn = 0
            for (tiles, ci) in groups:
                G = len(tiles) * T
                if ci != cur_ci:
                    cur_ci = ci
                    for k, slot in enumerate((2, 4, 3, 5)):
                        P.dma("sp", modl, modrow(l, ci, slot).partition_broadcast(128), reads=[B_mod], writes=[b_modl], sbuf=b_modl)
                        P.op("dve", lambda e, k=k: e.tensor_copy(out=modc[k][0], in_=modl), reads=[b_modl], writes=[modc[k][1]])
                (g1, bg1), (a2, ba2), (sh2, bsh2), (g2, bg2) = modc
                for j, tt in enumerate(tiles):
                    yv, byv = yb[0]
                    P.dma("sp", yv, y_d[tt * T:(tt + 1) * T, :], reads=[B_y[tt]], writes=[byv], sbuf=byv)
                    P.dma("sp", xg[:, j, :], x_src[tt * T:(tt + 1) * T, :], reads=[B_xsrc(tt)], writes=[b_xg[j]], sbuf=b_xg[j])
                    transposes_to(yv, byv, 16, lambda i, n, j=j: hT[:, i:i + n, j * T:(j + 1) * T], b_hT)
                for cb in range(4):
                    wt, bw = next_w()
                    for j, tt in enumerate(tiles):
                        bank = mm_bank()
                        for kc in range(16):
                            P.op("pe", lambda e, kc=kc, j=j, wt=wt, bank=bank: e.matmul(P.ps(bank), lhsT=hT[:, kc, j * T:(j + 1) * T], rhs=wt[:, kc, :],
                                                                                       start=(kc == 0), stop=(kc == 15)),
                                 reads=[b_hT, bw], writes=[P.pbuf[bank]], signal=(kc == 15))
                        tm, btm = tmpc[tn % 2]
                        tn += 1
                        P.op("dve", lambda e, tm=tm, bank=bank, cb=cb: e.tensor_tensor(out=tm, in0=P.ps(bank), in1=g1[:, cb * 512:(cb + 1) * 512], op=ALU.mult),
                             reads=[P.pbuf[bank], bg1], writes=[btm])
                        P.op("pool", lambda e, tm=tm, j=j, cb=cb: e.tensor_tensor(out=xg[:, j, cb * 512:(cb + 1) * 512], in0=xg[:, j, cb * 512:(cb + 1) * 512], in1=tm, op=ALU.add),
                             reads=[btm, b_xg[j]], writes=[b_xg[j]])
                for j, tt in enumerate(tiles):
                    xj = xg[:, j, :]
                    P.op("dve", lambda e: e.memset(small[:, 0:1], 0.0), writes=[b_small])
                    P.op("act", lambda e, xj=xj: e.activation(out=junk, in_=xj, func=AF.Square, accum_out=small[:, 0:1]), reads=[b_xg[j]], writes=[b_junk, b_small])
                    rstd_of(small[:, 0:1], D, 1, small[:, 1:2], small[:, 2:3], b_small, b_small, b_small)
                    P.op("dve", lambda e, xj=xj: e.scalar_tensor_tensor(out=modl, in0=xj, scalar=small[:, 2:3], in1=a2, op0=ALU.mult, op1=ALU.mult),
                         reads=[b_xg[j], b_small, ba2], writes=[b_modl])
                    P.op("pool", lambda e: e.tensor_tensor(out=h2b, in0=modl, in1=sh2, op=ALU.add), reads=[b_modl, bsh2], writes=[b_h2b])
                    transposes_to(h2b, b_h2b, 16, lambda i, n, j=j: hT[:, i:i + n, j * T:(j + 1) * T], b_hT)
                for fb in range(16):
                    if fb % 2 == 0:
                        pump_convert(2)
                    wt, bw = next_w()
                    for fc in range(4):
                        bank = mm_bank()
                        for kc in range(16):
                            P.op("pe", lambda e, kc=kc, fc=fc, wt=wt, bank=bank, G=G: e.matmul(P.ps(bank)[:, 0:G], lhsT=wt[:, kc, fc * 128:(fc + 1) * 128], rhs=hT[:, kc, 0:G],
                                                                                              start=(kc == 0), stop=(kc == 15)),
                                 reads=[b_hT, bw], writes=[P.pbuf[bank]], signal=(kc == 15))
                        P.op("act" if fc % 2 == 0 else "dve",
                             (lambda e, bank=bank, fb=fb, fc=fc, G=G: e.activation(out=uT[:, fb * 4 + fc, 0:G], in_=P.ps(bank)[:, 0:G], func=AF.Relu)) if fc % 2 == 0 else
                             (lambda e, bank=bank, fb=fb, fc=fc, G=G: e.tensor_scalar_max(out=uT[:, fb * 4 + fc, 0:G], in0=P.ps(bank)[:, 0:G], scalar1=0.0)),
                             reads=[P.pbuf[bank]], writes=[b_uT])
                        P.op("pool", lambda e, fb=fb, fc=fc, G=G: e.tensor_tensor(out=uT[:, fb * 4 + fc, 0:G], in0=uT[:, fb * 4 + fc, 0:G], in1=uT[:, fb * 4 + fc, 0:G], op=ALU.mult),
                             reads=[b_uT], writes=[b_uT])
                for cb in range(4):
                    banks = [2 + ((cb * 4 + j) % 6) for j in range(len(tiles))]
                    for kp in range(4):
                        wt, bw = next_w()
                        for j, tt in enumerate(tiles):
                            bank = banks[j]
                            for kc in range(16):
                                P.op("pe", lambda e, kc=kc, j=j, wt=wt, bank=bank, kp=kp: e.matmul(P.ps(bank), lhsT=uT[:, kp * 16 + kc, j * T:(j + 1) * T], rhs=wt[:, kc, :],
                                                                                                 start=(kp == 0 and kc == 0), stop=(kp == 3 and kc == 15)),
                                     reads=[b_uT, bw], writes=[P.pbuf[bank]], signal=(kc == 15))
                    for j, tt in enumerate(tiles):
                        bank = banks[j]
                        tm, btm = tmpc[tn % 2]
                        tn += 1
                        P.op("dve", lambda e, tm=tm, bank=bank, cb=cb: e.tensor_tensor(out=tm, in0=P.ps(bank), in1=g2[:, cb * 512:(cb + 1) * 512], op=ALU.mult),
                             reads=[P.pbuf[bank], bg2], writes=[btm])
                        P.op("pool", lambda e, tm=tm, j=j, cb=cb: e.tensor_tensor(out=xg[:, j, cb * 512:(cb + 1) * 512], in0=xg[:, j, cb * 512:(cb + 1) * 512], in1=tm, op=ALU.add),
                             reads=[btm, b_xg[j]], writes=[b_xg[j]])
                for j, tt in enumerate(tiles):
                    xj = xg[:, j, :]
                    if not last:
                        P.dma("pool", xs_d[tt * T:(tt + 1) * T, :], xj, reads=[b_xg[j]], writes=[B_xs[tt]], sbuf=b_xg[j])
                    else:
                        P.op("dve", lambda e: e.memset(small[:, 0:1], 0.0), writes=[b_small])
                        P.op("act", lambda e, xj=xj: e.activation(out=junk, in_=xj, func=AF.Square, accum_out=small[:, 0:1]), reads=[b_xg[j]], writes=[b_junk, b_small])
                        rstd_of(small[:, 0:1], D, 1, small[:, 1:2], small[:, 2:3], b_small, b_small, b_small)
                        P.op("dve", lambda e, xj=xj: e.scalar_tensor_tensor(out=xj, in0=xj, scalar=small[:, 2:3], in1=fnb, op0=ALU.mult, op1=ALU.mult),
                             reads=[b_xg[j], b_small, b_fnb], writes=[b_xg[j]])
                        P.dma("pool", y_out[tt * T:(tt + 1) * T, :], xj, reads=[b_xg[j]], writes=[B_out], sbuf=b_xg[j])
            pump_convert(10 ** 6)
            P.barrier()
            P.new_phase()
            P.top = mC

        P.barrier()
        print("instruction counts", P.n_inst, "sems", P.nsem)
        P.emit()
    return nc


def _consts():
    s = np.arange(128)
    tt, ss = s[:, None], s[None, :]
    cm = np.stack([(tt > ss), (tt <= ss), (tt < ss), (tt >= ss), np.ones((128, 128), bool)], 1).astype(np.float32)
    diff = (ss - tt).astype(np.float32)
    pidx = np.stack([s + 1, 128 - 1 - s, 128 - s, s], 1).astype(np.float32)
    return cm, diff, pidx


def _rope(length):
    GRID_W = 64
    pos = np.arange(length)
    row = (pos // GRID_W).astype(np.float32)
    col = (pos % GRID_W).astype(np.float32)
    half = 64
    inv = (1.0 / (np.float32(10000.0) ** (np.arange(0, half, 2, dtype=np.float32) / np.float32(half)))).astype(np.float32)
    ang = np.concatenate([row[:, None] * inv, col[:, None] * inv], -1).astype(np.float32)
    return np.concatenate([np.cos(ang), np.sin(ang)], -1).astype(np.float32)


_NC_CACHE = {}
DEBUG_SCRATCH = False
_LAST_RES = [None]


def kernel(x_prompt, x_sample, state_ret, state_ssd, c, c_ctx, w_ada, b_ada, norm1_w, w_in,
           ret_log_decay, conv_w, conv_b, dt_bias, a_log, d_skip, ssd_norm_w, w_out, norm2_w,
           w_ff1, w_ff2, final_norm_w, _n_cores=8):
    f = lambda a: np.ascontiguousarray(np.asarray(a, dtype=np.float32))
    x_prompt, x_sample, state_ret, state_ssd, c, c_ctx = map(f, (x_prompt, x_sample, state_ret, state_ssd, c, c_ctx))
    BP, PL, _ = x_prompt.shape
    BS, SLEN, _ = x_sample.shape
    DEPTH = w_ada.shape[0]
    n_cores = _n_cores
    EXCH = (n_cores == 2 * BS) and (BP % n_cores == 0) and (SLEN % 256 == 0)
    if EXCH:
        n_work = n_cores
        NP = BP // n_cores
        SL = SLEN // 2
    else:
        n_work = BS
        NP = BP // n_work
        SL = SLEN
    key = (DEPTH, NP, PL, SL, EXCH)
    if key not in _NC_CACHE:
        _NC_CACHE[key] = build_program(DEPTH, NP, PL, SL, EXCH)
    nc = _NC_CACHE[key]
    cm, diff, pidx = _consts()
    rope = _rope(SLEN)
    w_in = f(w_in)
    rld = f(ret_log_decay)
    cw = f(conv_w)
    dtb = f(dt_bias)
    alg = f(a_log)
    base = dict(
        cmask=cm, diffm=diff, pidx=pidx, zrow=np.zeros((1, 1536), ml_dtypes.bfloat16),
        conv_b=f(conv_b), d_skip=f(d_skip),
        ssd_norm_w=f(ssd_norm_w), w_out=f(w_out), w_ff1=f(w_ff1), w_ff2=f(w_ff2),
        final_norm_w=f(final_norm_w).reshape(1, D),
    )
    w_ada = f(w_ada)
    b_ada = f(b_ada)
    if EXCH:
        ada_half = [dict(w_ada_h=np.ascontiguousarray(w_ada[:, :, h * 3 * D:(h + 1) * 3 * D]),
                         b_ada_h=np.ascontiguousarray(b_ada[:, h * 3 * D:(h + 1) * 3 * D]),
                         nw_h=f(norm1_w) if h == 0 else f(norm2_w)) for h in range(2)]
    else:
        base.update(w_ada=w_ada, b_ada=b_ada, norm1_w=f(norm1_w), norm2_w=f(norm2_w))
    variants = {}
    for flip in ((False, True) if EXCH else (False,)):
        if not flip:
            v = dict(w_in=w_in, ret_log_decay=rld.reshape(DEPTH, 16), conv_w=cw.reshape(DEPTH, 3 * 1536),
                     dt_bias=dtb.reshape(DEPTH, 32), a_log=alg.reshape(DEPTH, 32))
        else:
            w2 = w_in.copy()
            w2[:, :, 6656:6672] = w_in[:, :, 6672:6688]
            w2[:, :, 6672:6688] = w_in[:, :, 6656:6672]
            v = dict(w_in=w2, ret_log_decay=np.ascontiguousarray(rld[:, ::-1]).reshape(DEPTH, 16),
                     conv_w=np.ascontiguousarray(cw[:, ::-1]).reshape(DEPTH, 3 * 1536),
                     dt_bias=np.ascontiguousarray(dtb[:, ::-1]).reshape(DEPTH, 32),
                     a_log=np.ascontiguousarray(alg[:, ::-1]).reshape(DEPTH, 32))
        variants[flip] = v
    tr = lambda s_: np.ascontiguousarray(s_.transpose(0, 1, 4, 2, 3)).reshape(DEPTH, 2, 128, 1024)
    in_maps = []
    meta = []
    for core in range(n_cores):
        if EXCH:
            b, half = core // 2, core % 2
            flip = (half == 1)
            plist = list(range(core * NP, (core + 1) * NP))
            xs_ = x_sample[b, half * SL:(half + 1) * SL]
            pos = np.arange(half * SL, (half + 1) * SL)
            sr, ss = tr(state_ret[b]), tr(state_ssd[b])
            if flip:
                xs_ = xs_[::-1]
                pos = pos[::-1]
                sr, ss = np.ascontiguousarray(sr[:, ::-1]), np.ascontiguousarray(ss[:, ::-1])
            xps = [x_prompt[p][::-1] if flip else x_prompt[p] for p in plist]
            psel = np.zeros((128, 2), np.float32)
            psel[:, 1 - half] = 1.0
        else:
            w = core % n_work
            b, half, flip = w, 0, False
            plist = list(range(w * NP, (w + 1) * NP))
            xs_ = x_sample[b]
            pos = np.arange(SLEN)
            sr, ss = tr(state_ret[b]), tr(state_ssd[b])
            xps = [x_prompt[p] for p in plist]
            psel = np.zeros((128, 2), np.float32)
        m = dict(base)
        m.update(variants[flip])
        if EXCH:
            m.update(ada_half[half])
        m.update(x_in=np.ascontiguousarray(np.concatenate(xps + [xs_], 0)), cond=np.ascontiguousarray(np.stack([c_ctx, c[b]], 0)),
                 st_ret=sr, st_ssd=ss, rope_cs=np.ascontiguousarray(rope[pos]), psel=psel)
        in_maps.append(m)
        meta.append((b, half, flip, plist))
    res = run_bass_kernel_spmd(nc, in_maps, core_ids=list(range(n_cores)))
    _LAST_RES[0] = res
    y_prompt = np.zeros((BP, PL, D), np.float32)
    y_sample = np.zeros((BS, SLEN, D), np.float32)
    nsr = np.zeros((BP, DEPTH, 2, 8, 128, 128), np.float32)
    nss = np.zeros((BP, DEPTH, 2, 16, 64, 128), np.float32)
    for core in range(n_work):
        b, half, flip, plist = meta[core]
        r = res.results[core]
        yo = r["y_out"]
        a = r["ns_ret"].reshape(NP, DEPTH, 2, 128, 8, 128).transpose(0, 1, 2, 4, 5, 3)
        bb = r["ns_ssd"].reshape(NP, DEPTH, 2, 128, 16, 64).transpose(0, 1, 2, 4, 5, 3)
        for i, p in enumerate(plist):
            yp = yo[i * PL:(i + 1) * PL]
            y_prompt[p] = yp[::-1] if flip else yp
            nsr[p] = a[i][:, ::-1] if flip else a[i]
            nss[p] = bb[i][:, ::-1] if flip else bb[i]
        ys = yo[NP * PL:]
        y_sample[b, half * SL:(half + 1) * SL] = ys[::-1] if flip else ys
    return (y_prompt, y_sample, nsr, nss)
```
